# Optimizing a Trainium2 kernel written in Bass

```python
import jax, jax.numpy as jnp
from jax import lax
import numpy as np

D_MODEL = 1024
BATCH = 4
SEQ = 8192
DEPTH = 1

GRID_W = 64
CTX_LEN = 256
N_FOURIER_GROUPS = 4
FOURIER_GROUP_DIM = 128
FOURIER_WIDTH = N_FOURIER_GROUPS * FOURIER_GROUP_DIM
NA_HEADS = 8
HEAD_DIM = 64
NA_WIDTH = NA_HEADS * HEAD_DIM
IN_WIDTH = FOURIER_WIDTH + 3 * NA_WIDTH
WIN_ROWS = 8
WIN_COLS = 16
N_BRANCHES = 2
D_FF = 4 * D_MODEL
ROPE_THETA = 10000.0
NORM_EPS = 1e-6
N_MOD = 6

kernel_name = "hybrid_fourier_natten_dit_block"


def rms_norm(x, g):
    xf = x.astype(jnp.float32)
    y = xf * lax.rsqrt(jnp.mean(xf * xf, axis=-1, keepdims=True) + NORM_EPS)
    return (y * g.astype(jnp.float32)).astype(x.dtype)


def modulate(h, shift, scale):
    return h * (1 + scale) + shift


def ada_mod(cond, w, b):
    m = jax.nn.silu(cond) @ w + b
    return jnp.split(m, N_MOD, axis=-1)


def split_heads(a):
    return a.reshape(a.shape[0], a.shape[1], NA_HEADS, HEAD_DIM)


def split_projection(p):
    f = p[..., :FOURIER_WIDTH]
    q = p[..., FOURIER_WIDTH:FOURIER_WIDTH + NA_WIDTH]
    k = p[..., FOURIER_WIDTH + NA_WIDTH:FOURIER_WIDTH + 2 * NA_WIDTH]
    v = p[..., FOURIER_WIDTH + 2 * NA_WIDTH:]
    return f, split_heads(q), split_heads(k), split_heads(v)


def rope_1d(x, cos, sin):
    m = x.shape[-1] // 2
    x1, x2 = x[..., :m], x[..., m:]
    cos = cos[None, :, None, :]
    sin = sin[None, :, None, :]
    return jnp.concatenate([x1 * cos - x2 * sin, x1 * sin + x2 * cos], axis=-1)


def axial_rope(x):
    n_tok = x.shape[1]
    t = jnp.arange(n_tok)
    row = (t // GRID_W).astype(jnp.float32)
    col = (t % GRID_W).astype(jnp.float32)
    axis_dim = HEAD_DIM // 2
    inv_freq = ROPE_THETA ** (-jnp.arange(0, axis_dim, 2, dtype=jnp.float32) / axis_dim)
    ang_r = row[:, None] * inv_freq
    ang_c = col[:, None] * inv_freq
    xf = x.astype(jnp.float32)
    xr = rope_1d(xf[..., :axis_dim], jnp.cos(ang_r), jnp.sin(ang_r))
    xc = rope_1d(xf[..., axis_dim:], jnp.cos(ang_c), jnp.sin(ang_c))
    return jnp.concatenate([xr, xc], axis=-1).astype(x.dtype)


def fourier_mix(u):
    b, n, _ = u.shape
    ug = u.astype(jnp.float32).reshape(b, n, N_FOURIER_GROUPS, FOURIER_GROUP_DIM)
    y = jnp.fft.fft2(ug, axes=(1, 3), norm="ortho").real
    return y.reshape(b, n, FOURIER_WIDTH).astype(u.dtype)


def neighbourhood_attention(q, k, v, k_ctx, v_ctx, rpb):
    b, s, h, dh = q.shape
    rows = s // GRID_W
    kr = min(WIN_ROWS, rows)
    kc = WIN_COLS
    n_loc = kr * kc
    scale = dh ** -0.5
    qg = q.reshape(b, rows, GRID_W, h, dh)
    kg = k.reshape(b, rows, GRID_W, h, dh)
    vg = v.reshape(b, rows, GRID_W, h, dh)
    r = jnp.arange(rows)
    cc = jnp.arange(GRID_W)
    row_start = jnp.clip(r - kr // 2, 0, rows - kr)
    col_start = jnp.clip(cc - kc // 2, 0, GRID_W - kc)
    col_idx = col_start[:, None] + jnp.arange(kc)
    dc_idx = col_idx - cc[:, None] + (WIN_COLS - 1)
    dr_idx = row_start[:, None] + jnp.arange(kr) - r[:, None] + (WIN_ROWS - 1)

    def one_row(args):
        q_row, rs, dr = args
        k_rows = lax.dynamic_slice_in_dim(kg, rs, kr, axis=1)
        v_rows = lax.dynamic_slice_in_dim(vg, rs, kr, axis=1)
        k_win = k_rows[:, :, col_idx]
        v_win = v_rows[:, :, col_idx]
        bias = rpb[:, dr][:, :, dc_idx].transpose(0, 2, 1, 3)
        s_loc = (jnp.einsum('bqhd,brqchd->bhqrc', q_row, k_win).astype(jnp.float32) * scale
                 + bias[None].astype(jnp.float32))
        s_ctx = jnp.einsum('bqhd,bkhd->bhqk', q_row, k_ctx).astype(jnp.float32) * scale
        sc = jnp.concatenate([s_loc.reshape(b, h, GRID_W, n_loc), s_ctx], axis=-1)
        p = jax.nn.softmax(sc, axis=-1).astype(v.dtype)
        p_loc = p[..., :n_loc].reshape(b, h, GRID_W, kr, kc)
        p_ctx = p[..., n_loc:]
        return (jnp.einsum('bhqrc,brqchd->bqhd', p_loc, v_win)
                + jnp.einsum('bhqk,bkhd->bqhd', p_ctx, v_ctx))

    out = lax.map(one_row, (qg.transpose(1, 0, 2, 3, 4), row_start, dr_idx))
    return out.transpose(1, 0, 2, 3, 4).reshape(b, s, h * dh)


def context_attention(q, k, v):
    b, n, h, dh = q.shape
    s = jnp.einsum('bqhd,bkhd->bhqk', q, k).astype(jnp.float32) * dh ** -0.5
    p = jax.nn.softmax(s, axis=-1).astype(v.dtype)
    return jnp.einsum('bhqk,bkhd->bqhd', p, v).reshape(b, n, h * dh)


def gated_merge(h, y_four, y_attn, w_gate, w_fo, w_ao, w_o):
    gates = jax.nn.sigmoid(h @ w_gate)
    g_f, g_a = jnp.split(gates, N_BRANCHES, axis=-1)
    merged = g_f * (y_four @ w_fo) + g_a * (y_attn @ w_ao)
    return merged @ w_o


def sqrelu_mlp(h, w1, w2):
    a = jax.nn.relu(h @ w1)
    return (a * a) @ w2


def setup_inputs(seed: int = 0) -> dict:
    key = jax.random.key(seed)
    ks = jax.random.split(key, 20)
    nrm = lambda k, shape, s: jax.random.normal(k, shape, jnp.float32) * s
    return {
        "x": nrm(ks[0], (BATCH, SEQ, D_MODEL), 1.0),
        "c": nrm(ks[1], (BATCH, D_MODEL), 1.0),
        "ctx": nrm(ks[2], (BATCH, CTX_LEN, D_MODEL), 1.0),
        "c_ctx": nrm(ks[3], (D_MODEL,), 1.0),
        "w_ada": nrm(ks[4], (DEPTH, D_MODEL, N_MOD * D_MODEL), D_MODEL ** -0.5),
        "b_ada": nrm(ks[5], (DEPTH, N_MOD * D_MODEL), 0.02),
        "norm1_g": 1.0 + nrm(ks[6], (DEPTH, D_MODEL), 0.02),
        "norm2_g": 1.0 + nrm(ks[7], (DEPTH, D_MODEL), 0.02),
        "w_in": nrm(ks[8], (DEPTH, D_MODEL, IN_WIDTH), D_MODEL ** -0.5),
        "q_norm_g": 1.0 + nrm(ks[9], (DEPTH, HEAD_DIM), 0.02),
        "k_norm_g": 1.0 + nrm(ks[10], (DEPTH, HEAD_DIM), 0.02),
        "rpb": nrm(ks[11], (DEPTH, NA_HEADS, 2 * WIN_ROWS - 1, 2 * WIN_COLS - 1), 0.1),
        "w_branch_gate": nrm(ks[12], (DEPTH, D_MODEL, N_BRANCHES * D_MODEL), D_MODEL ** -0.5),
        "w_fourier_out": nrm(ks[13], (DEPTH, FOURIER_WIDTH, D_MODEL), FOURIER_WIDTH ** -0.5),
        "w_attn_out": nrm(ks[14], (DEPTH, NA_WIDTH, D_MODEL), NA_WIDTH ** -0.5),
        "w_out": nrm(ks[15], (DEPTH, D_MODEL, D_MODEL), D_MODEL ** -0.5),
        "w_mlp1": nrm(ks[16], (DEPTH, D_MODEL, D_FF), D_MODEL ** -0.5),
        "w_mlp2": nrm(ks[17], (DEPTH, D_FF, D_MODEL), D_FF ** -0.5),
    }


def reference(x, c, ctx, c_ctx, w_ada, b_ada, norm1_g, norm2_g, w_in, q_norm_g, k_norm_g,
              rpb, w_branch_gate, w_fourier_out, w_attn_out, w_out, w_mlp1, w_mlp2):
    h_ctx = ctx
    for l in range(DEPTH):
        last = l == DEPTH - 1
        sh1, sc1, g1, sh2, sc2, g2 = [m[:, None, :] for m in ada_mod(c, w_ada[l], b_ada[l])]
        csh1, csc1, cg1, csh2, csc2, cg2 = ada_mod(c_ctx, w_ada[l], b_ada[l])

        hc = modulate(rms_norm(h_ctx, norm1_g[l]), csh1, csc1)
        fc, qc, kc, vc = split_projection(hc @ w_in[l])
        kc = rms_norm(kc, k_norm_g[l])

        hx = modulate(rms_norm(x, norm1_g[l]), sh1, sc1)
        fx, qx, kx, vx = split_projection(hx @ w_in[l])
        qx = axial_rope(rms_norm(qx, q_norm_g[l]))
        kx = axial_rope(rms_norm(kx, k_norm_g[l]))
        y_four = fourier_mix(fx)
        y_attn = neighbourhood_attention(qx, kx, vx, kc, vc, rpb[l])
        mixed = gated_merge(hx, y_four, y_attn, w_branch_gate[l], w_fourier_out[l],
                            w_attn_out[l], w_out[l])
        x = x + g1 * mixed
        hx2 = modulate(rms_norm(x, norm2_g[l]), sh2, sc2)
        x = x + g2 * sqrelu_mlp(hx2, w_mlp1[l], w_mlp2[l])

        if not last:
            qc = rms_norm(qc, q_norm_g[l])
            yc_four = fourier_mix(fc)
            yc_attn = context_attention(qc, kc, vc)
            mixed_c = gated_merge(hc, yc_four, yc_attn, w_branch_gate[l], w_fourier_out[l],
                                  w_attn_out[l], w_out[l])
            h_ctx = h_ctx + cg1 * mixed_c
            hc2 = modulate(rms_norm(h_ctx, norm2_g[l]), csh2, csc2)
            h_ctx = h_ctx + cg2 * sqrelu_mlp(hc2, w_mlp1[l], w_mlp2[l])
    return x
```

```python
import numpy as np
import ml_dtypes
from contextlib import ExitStack
import concourse.bass as bass
import concourse.mybir as mybir
from concourse.bass_utils import run_bass_kernel_spmd

F32 = mybir.dt.float32
BF16 = mybir.dt.bfloat16
AF = mybir.ActivationFunctionType
ALU = mybir.AluOpType
AX = mybir.AxisListType
NEG = -30000.0
EPS = 1e-6


class Buf:
    __slots__ = ("lw", "rd")

    def __init__(self):
        self.lw = None
        self.rd = {}


class Op:
    __slots__ = ("eng", "fn", "deps", "dma", "sem", "val", "signal")


CENG = ["gpsimd", "scalar", "vector", "tensor"]
ENGS = ["sync"] + CENG


class Sched:
    def __init__(self, nc, stack, ndma=40):
        self.nc = nc
        self.csem = {e: stack.enter_context(nc.semaphore("c_" + e)) for e in CENG}
        self.ccount = {e: 0 for e in CENG}
        self.dsem = {q: [stack.enter_context(nc.semaphore("d_%s%d" % (q, i))) for i in range(ndma)]
                     for q in ("sync", "gpsimd")}
        self.dcount = {q: [0] * ndma for q in self.dsem}
        self.dnext = {q: 0 for q in self.dsem}
        self.dlast = {q: [None] * ndma for q in self.dsem}
        self.ndma = ndma
        self.ops = []
        self.waited = {e: {} for e in ENGS}
        self.semobj = {}

    def add(self, eng, fn, reads=(), writes=(), dma=False):
        op = Op()
        op.eng, op.fn, op.dma, op.deps, op.signal = eng, fn, dma, {}, dma
        op.sem = op.val = None
        for b in reads:
            if b.lw is not None:
                op.deps[b.lw] = True
        for b in writes:
            if b.lw is not None and b.lw not in op.deps:
                op.deps[b.lw] = False
            for r in b.rd.values():
                if r is not op and r not in op.deps:
                    op.deps[r] = False
        key = id(op) if dma else eng
        for b in reads:
            b.rd[key] = op
        for b in writes:
            b.lw = op
            b.rd = {}
        self.ops.append(op)
        return op

    def dma(self, q, out, in_, reads=(), writes=()):
        return self.add(q, lambda e: e.dma_start(out=out, in_=in_), reads, writes, dma=True)

    def pe(self, fn, reads=(), writes=()):
        return self.add("tensor", fn, reads, writes)

    def act(self, fn, reads=(), writes=()):
        return self.add("scalar", fn, reads, writes)

    def dve(self, fn, reads=(), writes=()):
        return self.add("vector", fn, reads, writes)

    def pool(self, fn, reads=(), writes=()):
        return self.add("gpsimd", fn, reads, writes)

    def flush(self):
        ops = self.ops
        self.ops = []
        if not ops:
            return
        need = {}
        last = {}
        cur = set(ops)
        for op in ops:
            nl = []
            for d, raw in op.deps.items():
                if d not in cur:
                    continue
                if d.dma or d.eng != op.eng or (raw and not op.dma) or op.dma:
                    if not d.dma:
                        d.signal = True
                    nl.append(d)
            need[op] = nl
            if not op.dma:
                last[op.eng] = op
        for op in last.values():
            op.signal = True
        for op in ops:
            if op.dma:
                q = op.eng
                s = self.dnext[q]
                self.dnext[q] = (s + 1) % self.ndma
                prev = self.dlast[q][s]
                if prev is not None:
                    need[op].append(prev)
                self.dcount[q][s] += 16
                op.sem, op.val = self.dsem[q][s], self.dcount[q][s]
                self.dlast[q][s] = op
            elif op.signal:
                self.ccount[op.eng] += 1
                op.sem, op.val = self.csem[op.eng], self.ccount[op.eng]
        streams = {e: [] for e in ENGS}
        for op in ops:
            streams[op.eng].append(op)
        finals = []
        for e in CENG:
            if self.ccount[e] > 0:
                finals.append((self.csem[e], self.ccount[e]))
        for q in self.dsem:
            for i in range(self.ndma):
                if self.dcount[q][i] > 0:
                    finals.append((self.dsem[q][i], self.dcount[q][i]))

        def run(ename):
            def body(e):
                wd = self.waited[ename]
                for op in streams[ename]:
                    for d in need[op]:
                        k = id(d.sem)
                        if wd.get(k, 0) < d.val:
                            e.wait_ge(d.sem, d.val)
                            wd[k] = d.val
                    ins = op.fn(e)
                    if op.signal:
                        ins.then_inc(op.sem, 16 if op.dma else 1)
                for sem, val in finals:
                    k = id(sem)
                    if wd.get(k, 0) < val:
                        e.wait_ge(sem, val)
                        wd[k] = val
            return body

        with self.nc.Block() as block:
            block.sync(run("sync"))
            block.gpsimd(run("gpsimd"))
            block.scalar(run("scalar"))
            block.vector(run("vector"))
            block.tensor(run("tensor"))


def mkap(t, offset, dims):
    base = t[:]
    p = base.ap[0]
    return bass.AP(tensor=base.tensor, offset=offset, ap=[[p[0], p[1]]] + [list(d) for d in dims])


def build(debug=False, phases="MABCD"):
    nc = bass.Bass("TRN2", target_bir_lowering=False)

    def din(name, shape, dt=F32):
        return nc.dram_tensor(name, list(shape), dt, kind="ExternalInput").ap()

    xf = din("xf", [8192, 1024])
    xl = din("xl", [4608, 1024])
    ctx = din("ctx", [256, 1024])
    cc = din("cc", [128, 8, 2])
    w_ada = din("w_ada", [1024, 6144])
    bT = din("bT", [128, 48])
    bg = din("bg", [128, 2, 1024])
    ng = din("ng", [128, 16])
    gqk = din("gqk", [128, 2, 512])
    w_in = din("w_in", [1024, 2048])
    w_gate = din("w_gate", [1024, 2048])
    w_fo = din("w_fo", [512, 1024])
    w_ao = din("w_ao", [512, 1024])
    w_out = din("w_out", [1024, 1024])
    w_mlp1 = din("w_mlp1", [1024, 4096])
    w_mlp2 = din("w_mlp2", [4096, 1024])
    rope = din("rope", [38, 128, 128])
    MTd = din("MT", [128, 8, 576], BF16)
    FTd = din("FT", [128, 8, 896], BF16)
    qmd = din("qmask", [1, 2048], BF16)
    identd = din("ident", [128, 128], BF16)
    cs128d = din("cs128", [128, 256], BF16)
    ccscd = din("ccsc", [128, 512], BF16)
    t3d = din("t3", [128, 2, 128, 32], BF16)
    out = nc.dram_tensor("out", [4096, 1024], F32, kind="ExternalOutput").ap()
    skind = "ExternalOutput" if debug else "Internal"
    yt_scr = nc.dram_tensor("yt_scr", [4, 128, 4096], BF16, kind=skind).ap()
    ya_scr = nc.dram_tensor("ya_scr", [4, 128, 4096], BF16, kind=skind).ap()
    hx_scr = nc.dram_tensor("hx_scr", [8, 128, 4096], BF16, kind=skind).ap()
    x1_scr = nc.dram_tensor("x1_scr", [4096, 1024], F32, kind=skind).ap()
    if debug:
        dbg = nc.dram_tensor("dbg", [128, 2048], F32, kind="ExternalOutput").ap()

    with ExitStack() as gst:
        GA = gst.enter_context
        S = Sched(nc, gst)

        def sb(st, name, shape, dt):
            return st.enter_context(nc.sbuf_tensor(name, list(shape), dt)), Buf()

        def ps(st, name, shape, dt=F32):
            return st.enter_context(nc.psum_tensor(name, list(shape), dt)), Buf()

        mod, b_mod = sb(gst, "mod", [128, 48, 2], F32)
        sce, b_sce = sb(gst, "sce", [128, 2, 8, 2], F32)
        gb, b_gb = sb(gst, "gb", [128, 2, 1024], F32)
        idt, b_idt = sb(gst, "idt", [128, 128], BF16)
        ones, b_ones = sb(gst, "ones", [128, 128], BF16)
        epsb, b_eps = sb(gst, "epsb", [128, 1], F32)

        S.pool(lambda e: e.memset(epsb[:], EPS), writes=[b_eps])
        S.pool(lambda e: e.memset(ones[:], 1.0), writes=[b_ones])
        S.dma("sync", idt[:], identd, writes=[b_idt])

        def pipeline(n, stages, hooks=None):
            d = len(stages)
            for step in range(n + d - 1):
                if hooks and step in hooks:
                    hooks[step]()
                for si in range(d - 1, -1, -1):
                    t = step - si
                    if 0 <= t < n:
                        stages[si](t)

        def nt_stats(xt, b_xt, W):
            S.act(lambda e: e.activation(out=W["junk"][:], in_=xt[:], func=AF.Square, accum_out=W["st"][:, 0:1]),
                  reads=[b_xt], writes=[W["b_junk"], W["b_st"]])
            S.act(lambda e: e.activation(out=W["st"][:, 1:2], in_=W["st"][:, 0:1], func=AF.Sqrt, scale=1.0 / 1024,
                                         bias=epsb[:, 0:1]), reads=[W["b_st"], b_eps], writes=[W["b_st"]])

        def nt_recip(W):
            S.dve(lambda e: e.reciprocal(out=W["st"][:, 2:3], in_=W["st"][:, 1:2]), reads=[W["b_st"]],
                  writes=[W["b_st"]])

        def nt_scale(xt, b_xt, W):
            S.act(lambda e: e.activation(out=W["xn"][:], in_=xt[:], func=AF.Copy, scale=W["st"][:, 2:3]),
                  reads=[b_xt, W["b_st"]], writes=[W["b_xn"]])

        def nt_tr(W, pT, b_pT):
            for j in range(8):
                S.pe(lambda e, j=j: e.transpose(pT[:, j * 128:(j + 1) * 128], W["xn"][:, j * 128:(j + 1) * 128],
                                                idt[:]), reads=[W["b_xn"], b_idt], writes=[b_pT])

        def nt_mod(pT, b_pT, dst3, b_dst, sc_off, sh_off):
            for j in range(8):
                S.act(lambda e, j=j: e.activation(out=dst3(j), in_=pT[:, j * 128:(j + 1) * 128], func=AF.Identity,
                                                  scale=mkap(sce, sc_off + 2 * j, [[1, 1]]),
                                                  bias=mkap(mod, sh_off + 2 * j, [[1, 1]])),
                      reads=[b_pT, b_sce, b_mod], writes=[b_dst])

        def norm_work(st, tag, n):
            junk, b_junk = sb(st, "junk" + tag, [128, 1024], BF16)
            res = []
            for i in range(n):
                st_, b_st = sb(st, "st%s%d" % (tag, i), [128, 4], F32)
                xn, b_xn = sb(st, "xn%s%d" % (tag, i), [128, 1024], BF16)
                res.append(dict(junk=junk, b_junk=b_junk, st=st_, b_st=b_st, xn=xn, b_xn=b_xn))
            return res

        if "A" in phases:
            with ExitStack() as ph:
                cc_sb, b_cc = sb(ph, "cc_sb", [128, 8, 2], F32)
                scb, b_scb = sb(ph, "scb", [128, 8, 2], BF16)
                screp, b_screp = sb(ph, "screp", [128, 8, 128], BF16)
                bT_sb, b_bT = sb(ph, "bT_sb", [128, 48], F32)
                bg_sb, b_bg = sb(ph, "bg_sb", [128, 2, 1024], F32)
                ng_sb, b_ng = sb(ph, "ng_sb", [128, 16], F32)
                wsl = [sb(ph, "wsl%d" % i, [128, 8, 512], BF16) for i in range(2)]
                w_inf, b_winf = sb(ph, "w_inf", [128, 8, 512], BF16)
                u_all = ph.enter_context(nc.sbuf_tensor("u_all", [128, 64, 512], BF16))
                b_u = [Buf() for _ in range(64)]
                Bg, b_Bg = sb(ph, "Bg", [128, 64, 256], BF16)
                YTg = [sb(ph, "YTg%d" % i, [128, 4096], BF16) for i in range(1)]
                Zt = [sb(ph, "Zt%d" % i, [128, 512], BF16) for i in range(3)]
                t3_sb, b_t3 = sb(ph, "t3_sb", [128, 2, 128, 32], BF16)
                cs_sb, b_cs = sb(ph, "cs_sb", [128, 256], BF16)
                ccsc_sb, b_ccsc = sb(ph, "ccsc_sb", [128, 512], BF16)
                xt = [sb(ph, "xtA%d" % i, [128, 1024], F32) for i in range(3)]
                hxT = [sb(ph, "hxTA%d" % i, [128, 8, 128], BF16) for i in range(2)]
                W = norm_work(ph, "A", 3)
                pT = [ps(ph, "pTA%d" % i, [128, 1024], BF16) for i in range(2)]
                pu = [ps(ph, "puA%d" % i, [128, 512]) for i in range(2)]
                pa = [ps(ph, "paA%d" % i, [128, 512]) for i in range(2)]
                pb = [ps(ph, "pbA%d" % i, [128, 512]) for i in range(2)]
                wv = w_in.rearrange("(k p) n -> p k n", p=128)
                for kc in range(8):
                    S.dma("gpsimd", w_inf[:, kc, :], wv[:, kc, 0:512], writes=[b_winf])
                S.dma("sync", t3_sb[:], t3d, writes=[b_t3])
                S.dma("sync", cs_sb[:], cs128d, writes=[b_cs])
                S.dma("sync", ccsc_sb[:], ccscd, writes=[b_ccsc])
                xsrc = xf.rearrange("(n1 n2) d -> n2 n1 d", n2=64)
                pmod, b_pmod = pb[0]
                pg = [pa[0], pa[1]]
                S.dma("sync", cc_sb[:], cc, writes=[b_cc])
                S.dma("sync", bT_sb[:], bT, writes=[b_bT])
                S.dma("sync", bg_sb[:], bg, writes=[b_bg])
                S.dma("sync", ng_sb[:], ng, writes=[b_ng])
                S.act(lambda e: e.activation(out=scb[:], in_=cc_sb[:], func=AF.Silu), reads=[b_cc], writes=[b_scb])
                S.act(lambda e: e.activation(out=screp[:], in_=cc_sb[:, :, 0:1].broadcast_to([128, 8, 128]),
                                             func=AF.Silu), reads=[b_cc], writes=[b_screp])
                wav = w_ada.rearrange("(k p) n -> p k n", p=128)

                def m_dma(hs):
                    w, b_w = wsl[hs % 2]
                    S.dma("gpsimd", w[:], wav[:, :, hs * 512:(hs + 1) * 512], writes=[b_w])

                def m_proc(hs):
                    w, b_w = wsl[hs % 2]
                    v, hh = hs // 2, hs % 2
                    for j4 in range(4):
                        col = (v * 8 + hh * 4 + j4) * 2
                        for kc in range(8):
                            S.pe(lambda e, j4=j4, kc=kc, col=col: e.matmul(
                                pmod[:, col:col + 2], lhsT=w[:, kc, j4 * 128:(j4 + 1) * 128], rhs=scb[:, kc, :],
                                start=(kc == 0), stop=(kc == 7)), reads=[b_w, b_scb], writes=[b_pmod])
                    if v in (2, 5):
                        gi = 0 if v == 2 else 1
                        pgt, b_pg = pg[hh]
                        for kc in range(8):
                            S.pe(lambda e, kc=kc: e.matmul(pgt[:, :], lhsT=screp[:, kc, :], rhs=w[:, kc, :],
                                                           start=(kc == 0), stop=(kc == 7)),
                                 reads=[b_w, b_screp], writes=[b_pg])
                        S.dve(lambda e: e.tensor_tensor(out=gb[:, gi, hh * 512:(hh + 1) * 512], in0=pgt[:, :],
                                                        in1=bg_sb[:, gi, hh * 512:(hh + 1) * 512], op=ALU.add),
                              reads=[b_pg, b_bg], writes=[b_gb])
                    if hs + 2 < 12:
                        m_dma(hs + 2)
                    if hh == 1:
                        S.dve(lambda e: e.tensor_tensor(
                            out=mod[:, v * 8:(v + 1) * 8, :],
                            in0=pmod[:, v * 16:(v + 1) * 16].rearrange("p (a b) -> p a b", b=2),
                            in1=bT_sb[:, v * 8:(v + 1) * 8].unsqueeze(2).broadcast_to([128, 8, 2]), op=ALU.add),
                            reads=[b_pmod, b_bT], writes=[b_mod])
                        if v in (1, 4):
                            wi = 0 if v == 1 else 1
                            S.dve(lambda e: e.scalar_tensor_tensor(
                                out=sce[:, wi], in0=mod[:, v * 8:(v + 1) * 8, :], scalar=1.0,
                                in1=ng_sb[:, wi * 8:(wi + 1) * 8].unsqueeze(2).broadcast_to([128, 8, 2]),
                                op0=ALU.add, op1=ALU.mult), reads=[b_mod, b_ng], writes=[b_sce])

                m_dma(0)
                m_dma(1)
                for hs in range(4):
                    m_proc(hs)
                sh_rep, b_shrep = sb(ph, "sh_rep", [128, 8, 128], BF16)
                shw, b_shw = sb(ph, "shw", [128, 512], F32)
                S.act(lambda e: e.copy(out=sh_rep[:], in_=mkap(mod, 0, [[2, 8], [0, 128]])), reads=[b_mod],
                      writes=[b_shrep])
                for kc in range(8):
                    S.pe(lambda e, kc=kc: e.matmul(pu[0][0][:, :], lhsT=sh_rep[:, kc, :], rhs=w_inf[:, kc, :],
                                                   start=(kc == 0), stop=(kc == 7)),
                         reads=[b_shrep, b_winf], writes=[pu[0][1]])
                S.dve(lambda e: e.tensor_copy(out=shw[:, :], in_=pu[0][0][:, :]), reads=[pu[0][1]], writes=[b_shw])

                def a0(t):
                    S.dma("sync", xt[t % 3][0][:], xsrc[t], writes=[xt[t % 3][1]])

                def a1(t):
                    nt_stats(xt[t % 3][0], xt[t % 3][1], W[t % 3])

                def a2(t):
                    nt_recip(W[t % 3])

                def a3(t):
                    nt_scale(xt[t % 3][0], xt[t % 3][1], W[t % 3])

                def a4(t):
                    nt_tr(W[t % 3], pT[t % 2][0], pT[t % 2][1])

                def a5(t):
                    h_t, b_h = hxT[t % 2]
                    p_T, b_pT = pT[t % 2]
                    S.dve(lambda e: e.tensor_tensor(out=h_t[:], in0=p_T[:, :].rearrange("p (a b) -> p a b", b=128),
                                                    in1=mkap(sce, 0, [[2, 8], [0, 128]]), op=ALU.mult),
                          reads=[b_pT, b_sce], writes=[b_h])

                def a6(t):
                    h_t, b_h = hxT[t % 2]
                    p_u, b_pu = pu[t % 2]
                    for kc in range(8):
                        S.pe(lambda e, kc=kc: e.matmul(p_u[:, :], lhsT=h_t[:, kc, :], rhs=w_inf[:, kc, :],
                                                       start=(kc == 0), stop=(kc == 7)),
                             reads=[b_h, b_winf], writes=[b_pu])

                def a7(t):
                    p_u, b_pu = pu[t % 2]
                    S.dve(lambda e: e.tensor_tensor(out=u_all[:, t, :], in0=p_u[:, :], in1=shw[:, :], op=ALU.add),
                          reads=[b_pu, b_shw], writes=[b_u[t]])

                pipeline(64, [a0, a1, a2, a3, a4, a5, a6, a7], hooks={8 * (hs - 3): (lambda hs=hs: m_proc(hs))
                                                                    for hs in range(4, 12)})

                for g in range(4):
                    def s1mm(pr, g=g):
                        p_a, b_pa = pa[pr % 2]
                        for q in range(2):
                            n2 = 2 * pr + q
                            S.pe(lambda e, n2=n2, q=q: e.matmul(
                                p_a[:, q * 256:(q + 1) * 256], lhsT=u_all[:, n2, g * 128:(g + 1) * 128],
                                rhs=cs_sb[:, :], start=True, stop=True), reads=[b_u[n2], b_cs], writes=[b_pa])

                    def s1ev(pr):
                        p_a, b_pa = pa[pr % 2]
                        dst = Bg[:, 2 * pr:2 * pr + 2, :]
                        src = p_a[:, :].rearrange("p (a b) -> p a b", b=256)
                        if pr % 2 == 0:
                            S.act(lambda e: e.copy(out=dst, in_=src), reads=[b_pa], writes=[b_Bg])
                        else:
                            S.dve(lambda e: e.tensor_copy(out=dst, in_=src), reads=[b_pa], writes=[b_Bg])

                    pipeline(32, [s1mm, s1ev])
                    y_t, b_y = YTg[0]

                    def s2mm(j):
                        p_z, b_pz = pb[j % 2]
                        for lo in range(2):
                            k1 = 2 * j + lo
                            lr = mkap(Bg, k1, [[256, 64]])
                            ls = mkap(Bg, 128 + k1, [[256, 64]])
                            S.pe(lambda e, lr=lr, lo=lo: e.matmul(p_z[0:64, lo * 256:(lo + 1) * 256], lhsT=lr,
                                                                 rhs=ccsc_sb[:, 0:256], start=True, stop=False),
                                 reads=[b_Bg, b_ccsc], writes=[b_pz])
                            S.pe(lambda e, ls=ls, lo=lo: e.matmul(p_z[0:64, lo * 256:(lo + 1) * 256], lhsT=ls,
                                                                 rhs=ccsc_sb[:, 256:512], start=False, stop=True),
                                 reads=[b_Bg, b_ccsc], writes=[b_pz])

                    def s2ev(j):
                        p_z, b_pz = pb[j % 2]
                        z_t, b_z = Zt[j % 3]
                        if j % 2 == 0:
                            S.act(lambda e: e.copy(out=z_t[0:64, :], in_=p_z[0:64, :]), reads=[b_pz], writes=[b_z])
                        else:
                            S.dve(lambda e: e.tensor_copy(out=z_t[0:64, :], in_=p_z[0:64, :]), reads=[b_pz],
                                  writes=[b_z])

                    def s3mm(j, y_t=y_t, b_y=b_y):
                        z_t, b_z = Zt[j % 3]
                        for lo in range(2):
                            k1 = 2 * j + lo
                            col = (k1 % 16) * 32
                            p_y, b_py = pu[(k1 // 16) % 2]
                            S.pe(lambda e, k1=k1, col=col, p_y=p_y, lo=lo: e.matmul(
                                p_y[:, col:col + 32], lhsT=z_t[0:64, lo * 256:lo * 256 + 128],
                                rhs=t3_sb[0:64, 0, k1, :], start=True, stop=False), reads=[b_z, b_t3], writes=[b_py])
                            S.pe(lambda e, k1=k1, col=col, p_y=p_y, lo=lo: e.matmul(
                                p_y[:, col:col + 32], lhsT=z_t[0:64, lo * 256 + 128:lo * 256 + 256],
                                rhs=t3_sb[0:64, 1, k1, :], start=False, stop=True), reads=[b_z, b_t3], writes=[b_py])
                            if k1 % 16 == 15:
                                dst = mkap(y_t, 16 * (k1 // 16), [[1, 16], [128, 32]])
                                S.dve(lambda e, dst=dst, p_y=p_y: e.tensor_copy(
                                    out=dst, in_=p_y[:, :].rearrange("p (a b) -> p a b", b=32)),
                                    reads=[b_py], writes=[b_y])

                    pipeline(64, [s2mm, s2ev, s3mm])
                    S.dma("sync", yt_scr[g], y_t[:], reads=[b_y])
                S.flush()

        if "B" in phases:
            with ExitStack() as ph:
                KT = ph.enter_context(nc.sbuf_tensor("KT", [128, 4, 4864], BF16))
                b_KT = [Buf() for _ in range(38)]
                V = ph.enter_context(nc.sbuf_tensor("V", [128, 38, 512], BF16))
                b_V = [Buf() for _ in range(38)]
                qT = ph.enter_context(nc.sbuf_tensor("qT", [128, 4, 4096], BF16))
                b_qT = [Buf() for _ in range(32)]
                with ExitStack() as p1:
                    w_qkv, b_wqkv = sb(p1, "w_qkv", [128, 8, 1536], BF16)
                    gqk_sb, b_gqk = sb(p1, "gqk_sb", [128, 2, 512], F32)
                    xt = [sb(p1, "xtB%d" % i, [128, 1024], F32) for i in range(3)]
                    rp = [sb(p1, "rpB%d" % i, [128, 128], F32) for i in range(4)]
                    hxT = [sb(p1, "hxTB%d" % i, [128, 8, 128], BF16) for i in range(3)]
                    W = norm_work(p1, "B", 3)
                    pT = [ps(p1, "pTB%d" % i, [128, 1024], BF16) for i in range(1)]
                    pq = [[ps(p1, "pqB%d_%d" % (i, c), [128, 512]) for c in range(3)] for i in range(2)]
                    ptr = [ps(p1, "ptrB%d" % i, [128, 1024], BF16) for i in range(1)]
                    NB = 2
                    tq = []
                    for i in range(NB):
                        d = {}
                        for nm, dt in ():
                            d[nm] = sb(p1, "%s%d" % (nm, i), [128, 512], dt)
                        d["qr"] = sb(p1, "qr%d" % i, [128, 1024], BF16)
                        d["ss"] = sb(p1, "ss%d" % i, [128, 48], F32)
                        tq.append(d)
                    fr = [dict(qf=sb(p1, "qf%d" % i, [128, 512], F32), kf=sb(p1, "kf%d" % i, [128, 512], F32))
                          for i in range(3)]
                    t2r = dict(t2q=sb(p1, "t2q", [128, 512], F32), t2k=sb(p1, "t2k", [128, 512], F32))
                    t1r = [dict(t1q=sb(p1, "t1q%d" % i, [128, 512], F32), t1k=sb(p1, "t1k%d" % i, [128, 512], F32))
                           for i in range(3)]
                    wv = w_in.rearrange("(k p) n -> p k n", p=128)
                    for kc in range(8):
                        S.dma("gpsimd", w_qkv[:, kc, :], wv[:, kc, 512:2048], writes=[b_wqkv])
                    S.dma("sync", gqk_sb[:], gqk, writes=[b_gqk])
                    hxv = hx_scr.rearrange("k p n -> p k n")

                    def own(t):
                        return 2 <= t - 2 < 34

                    def b0(t):
                        src = ctx[t * 128:(t + 1) * 128, :] if t < 2 else xl[(t - 2) * 128:(t - 1) * 128, :]
                        S.dma("sync", xt[t % 3][0][:], src, writes=[xt[t % 3][1]])

                    def b1(t):
                        nt_stats(xt[t % 3][0], xt[t % 3][1], W[t % 3])

                    def b2(t):
                        nt_recip(W[t % 3])

                    def b3(t):
                        nt_scale(xt[t % 3][0], xt[t % 3][1], W[t % 3])

                    def b4(t):
                        nt_tr(W[t % 3], pT[0][0], pT[0][1])

                    def b5(t):
                        h_t, b_h = hxT[t % 3]
                        cond = 1 if t < 2 else 0
                        nt_mod(pT[0][0], pT[0][1], lambda j: h_t[:, j, :], b_h, cond, cond)

                    def b6(t):
                        h_t, b_h = hxT[t % 3]
                        if own(t):
                            o = (t - 4) * 128
                            S.dma("sync", hxv[:, :, o:o + 128], h_t[:], reads=[b_h])
                        for kc in range(8):
                            for c in range(3):
                                if c == 0 and not own(t):
                                    continue
                                p_q, b_pq = pq[t % 2][c]
                                S.pe(lambda e, kc=kc, c=c, p_q=p_q: e.matmul(
                                    p_q[:, :], lhsT=h_t[:, kc, :], rhs=w_qkv[:, kc, c * 512:(c + 1) * 512],
                                    start=(kc == 0), stop=(kc == 7)), reads=[b_h, b_wqkv], writes=[b_pq])

                    def b7(t):
                        d = tq[t % NB]
                        S.act(lambda e: e.copy(out=V[:, t, :], in_=pq[t % 2][2][0][:, :]),
                              reads=[pq[t % 2][2][1]], writes=[b_V[t]])
                        for w_, nm, sq in ((1, "kf", "t1k"), (0, "qf", "t1q")):
                            if w_ == 0 and not own(t):
                                continue
                            p_s, b_ps = pq[t % 2][w_]
                            f_t, b_f = fr[t % 3][nm]
                            s_t, b_s = t1r[t % 3][sq]
                            S.act(lambda e, f_t=f_t, p_s=p_s: e.copy(out=f_t[:, :], in_=p_s[:, :]),
                                  reads=[b_ps], writes=[b_f])
                            S.act(lambda e, s_t=s_t, p_s=p_s: e.activation(out=s_t[:, :], in_=p_s[:, :],
                                                                          func=AF.Square),
                                  reads=[b_ps], writes=[b_s])

                    def b8(t):
                        S.dma("sync", rp[t % 4][0][:], rope[t], writes=[rp[t % 4][1]])
                        d = tq[t % NB]
                        ss, b_ss = d["ss"]
                        for w_, sq in ((1, "t1k"), (0, "t1q")):
                            if w_ == 0 and not own(t):
                                continue
                            s_t, b_s = t1r[t % 3][sq]
                            S.dve(lambda e, w_=w_, s_t=s_t: e.tensor_reduce(
                                out=ss[:, w_ * 8:(w_ + 1) * 8], in_=s_t[:, :].rearrange("p (h d) -> p h d", d=64),
                                axis=AX.X, op=ALU.add), reads=[b_s], writes=[b_ss])

                    def b9(t):
                        d = tq[t % NB]
                        ss, b_ss = d["ss"]
                        lo_ = 0 if own(t) else 8
                        S.act(lambda e: e.activation(out=ss[:, 16 + lo_:32], in_=ss[:, lo_:16], func=AF.Sqrt,
                                                     scale=1.0 / 64, bias=epsb[:, 0:1]),
                              reads=[b_ss, b_eps], writes=[b_ss])

                    def b10(t):
                        d = tq[t % NB]
                        ss, b_ss = d["ss"]
                        rp_t, b_rp = rp[t % 4]
                        qr, b_qr = d["qr"]
                        lo_ = 0 if own(t) else 8
                        S.dve(lambda e: e.reciprocal(out=ss[:, 32 + lo_:48], in_=ss[:, 16 + lo_:32]), reads=[b_ss],
                              writes=[b_ss])
                        for w_, fn, nn, t1n, t2n in ((1, "kf", "kf", "t1k", "t2k"), (0, "qf", "qf", "t1q", "t2q")):
                            if w_ == 0 and not own(t):
                                continue
                            f_t, b_f = fr[t % 3][fn]
                            n_t, b_n = fr[t % 3][nn]
                            t1, b_t1 = t1r[t % 3][t1n]
                            t2, b_t2 = t2r[t2n]
                            S.dve(lambda e, w_=w_, f_t=f_t, n_t=n_t: e.scalar_tensor_tensor(
                                out=n_t[:, :].rearrange("p (h d) -> p h d", d=64),
                                in0=f_t[:, :].rearrange("p (h d) -> p h d", d=64),
                                scalar=(0.125 if w_ == 0 else 1.0),
                                in1=ss[:, 32 + w_ * 8:32 + (w_ + 1) * 8].unsqueeze(2).broadcast_to([128, 8, 64]),
                                op0=ALU.mult, op1=ALU.mult), reads=[b_f, b_ss], writes=[b_n])
                            S.dve(lambda e, w_=w_, n_t=n_t: e.tensor_tensor(out=n_t[:, :], in0=n_t[:, :],
                                                                         in1=gqk_sb[:, w_, :], op=ALU.mult),
                                  reads=[b_n, b_gqk], writes=[b_n])
                            S.dve(lambda e, n_t=n_t, t1=t1: e.tensor_tensor(
                                out=t1[:, :].rearrange("p (h d) -> p h d", d=64),
                                in0=n_t[:, :].rearrange("p (h d) -> p h d", d=64),
                                in1=rp_t[:, 0:64].unsqueeze(1).broadcast_to([128, 8, 64]), op=ALU.mult),
                                reads=[b_n, b_rp], writes=[b_t1])
                            for a in range(2):
                                o_ap = mkap(t2, a * 16, [[64, 8], [32, 2], [1, 16]])
                                i_ap = mkap(n_t, (1 - a) * 16, [[64, 8], [32, 2], [1, 16]])
                                s_ap = mkap(rp_t, 64 + a * 16, [[0, 8], [32, 2], [1, 16]])
                                S.dve(lambda e, o_ap=o_ap, i_ap=i_ap, s_ap=s_ap: e.tensor_tensor(
                                    out=o_ap, in0=i_ap, in1=s_ap, op=ALU.mult), reads=[b_n, b_rp], writes=[b_t2])
                            S.dve(lambda e, w_=w_, t1=t1, t2=t2: e.tensor_tensor(
                                out=qr[:, w_ * 512:(w_ + 1) * 512], in0=t1[:, :], in1=t2[:, :], op=ALU.add),
                                reads=[b_t1, b_t2], writes=[b_qr])

                    def b11(t):
                        d = tq[t % NB]
                        qr, b_qr = d["qr"]
                        p_t, b_pt = ptr[0]
                        for w_ in (1, 0):
                            if w_ == 0 and not own(t):
                                continue
                            for c in range(4):
                                S.pe(lambda e, c=c, w_=w_: e.transpose(
                                    p_t[:, w_ * 512 + c * 128:w_ * 512 + (c + 1) * 128],
                                    qr[:, w_ * 512 + c * 128:w_ * 512 + (c + 1) * 128], idt[:]),
                                    reads=[b_qr, b_idt], writes=[b_pt])

                    def b12(t):
                        p_t, b_pt = ptr[0]
                        S.act(lambda e: e.copy(out=KT[:, :, t * 128:(t + 1) * 128],
                                               in_=p_t[:, 512:1024].rearrange("p (a b) -> p a b", b=128)),
                              reads=[b_pt], writes=[b_KT[t]])
                        if own(t):
                            o = (t - 4) * 128
                            S.act(lambda e: e.copy(out=qT[:, :, o:o + 128],
                                                   in_=p_t[:, 0:512].rearrange("p (a b) -> p a b", b=128)),
                                  reads=[b_pt], writes=[b_qT[t - 4]])

                    pipeline(38, [b0, b1, b2, b3, b4, b5, b6, b7, b8, b9, b10, b11, b12])
                    S.flush()
                with ExitStack() as p2:
                    MT_sb, b_MT = sb(p2, "MT_sb", [128, 8, 576], BF16)
                    FT_sb, b_FT = sb(p2, "FT_sb", [128, 8, 896], BF16)
                    qm_sb, b_qm = sb(p2, "qm_sb", [1, 2048], BF16)
                    NP = 6
                    PT = [sb(p2, "PT%d" % i, [128, 512], BF16) for i in range(NP)]
                    rden = [sb(p2, "rden%d" % i, [128, 512], F32) for i in range(2)]
                    yat = [sb(p2, "yat%d" % i, [128, 4, 512], BF16) for i in range(2)]
                    pS = [ps(p2, "pS%d" % i, [128, 512]) for i in range(NP)]
                    pnum = [ps(p2, "pnum%d" % i, [128, 512]) for i in range(2)]
                    VA = p2.enter_context(nc.sbuf_tensor("VA", [128, 14, 8, 128], BF16))
                    b_VA = [Buf() for _ in range(14)]
                    S.pool(lambda e: e.memset(VA[:], 1.0), writes=b_VA)

                    def va_slot(vt):
                        return 12 + vt if vt < 2 else (vt - 2) % 12

                    def va_fill(vt):
                        sl = va_slot(vt)
                        S.act(lambda e: e.copy(out=VA[:, sl, :, 0:64],
                                               in_=V[:, vt, :].rearrange("p (h d) -> p h d", d=64)),
                              reads=[b_V[vt]], writes=[b_VA[sl]])
                    S.dma("sync", MT_sb[:], MTd, writes=[b_MT])
                    S.dma("sync", FT_sb[:], FTd, writes=[b_FT])
                    S.dma("sync", qm_sb[:], qmd, writes=[b_qm])
                    S.act(lambda e: e.activation(out=MT_sb[:], in_=MT_sb[:], func=AF.Exp), reads=[b_MT], writes=[b_MT])
                    S.act(lambda e: e.activation(out=FT_sb[:], in_=FT_sb[:], func=AF.Exp), reads=[b_FT], writes=[b_FT])
                    yav = ya_scr.rearrange("c p n -> p c n")
                    items = []
                    for G in range(8):
                        r0 = 4 + 8 * G
                        for h in range(8):
                            chunks = []
                            for c in range(2):
                                chunks.append((c * 128, c, 0, 512, None, None))
                            for m in range(8):
                                lrk = r0 - 4 + 2 * m
                                ia, ib = max(0, 2 * m - 7), min(7, 2 * m + 1)
                                a, b = ia * 64, (ib + 1) * 64
                                bias = MT_sb[:, h, (7 - 2 * m + ia) * 64:(7 - 2 * m + ib + 1) * 64]
                                qm = None
                                if G == 0:
                                    qm = qm_sb[0:1, a:b]
                                elif G == 7:
                                    qm = qm_sb[0:1, 1024 + a:1024 + b]
                                chunks.append((256 + lrk * 64, 2 + lrk // 2, a, b, bias, qm))
                            if G == 0:
                                for ee in range(4):
                                    lrk = r0 + 2 * ee
                                    chunks.append((256 + lrk * 64, 2 + lrk // 2, 0, 256,
                                                   FT_sb[:, h, (6 - 2 * ee) * 64:(10 - 2 * ee) * 64],
                                                   qm_sb[0:1, 512:768]))
                            if G == 7:
                                for ee in range(4):
                                    lrk = r0 + 2 * ee
                                    chunks.append((256 + lrk * 64, 2 + lrk // 2, 256, 512,
                                                   FT_sb[:, h, (10 - 2 * ee) * 64:(14 - 2 * ee) * 64],
                                                   qm_sb[0:1, 1536 + 256:1536 + 512]))
                            for ci, ch in enumerate(chunks):
                                items.append((G, h, ci, len(chunks)) + ch)

                    def c0(n):
                        G, h, ci, nch, kcol, vt, a, b, bias, qm = items[n]
                        if h == 0 and ci == 0:
                            if G == 0:
                                va_fill(0)
                                va_fill(1)
                            for lt in (range(0, 8) if G == 0 else range(4 * G + 4, 4 * G + 8)):
                                va_fill(2 + lt)
                        cp, po = h // 2, (h % 2) * 64
                        p_s, b_ps = pS[n % NP]
                        qb = b_qT[(G * 512 + a) // 128:(G * 512 + b + 127) // 128]
                        S.pe(lambda e: e.matmul(
                            p_s[:, a:b], lhsT=KT[po:po + 64, cp, kcol:kcol + 128],
                            rhs=qT[po:po + 64, cp, G * 512 + a:G * 512 + b], start=True, stop=(qm is None)),
                            reads=[b_KT[kcol // 128]] + qb, writes=[b_ps])
                        if qm is not None:
                            S.pe(lambda e: e.matmul(p_s[:, a:b], lhsT=ones[0:1, :], rhs=qm, start=False, stop=True),
                                 reads=[b_ones, b_qm], writes=[b_ps])

                    def c1(n):
                        G, h, ci, nch, kcol, vt, a, b, bias, qm = items[n]
                        p_s, b_ps = pS[n % NP]
                        p_t, b_pt = PT[n % NP]
                        S.act(lambda e: e.activation(out=p_t[:, a:b], in_=p_s[:, a:b], func=AF.Exp),
                              reads=[b_ps], writes=[b_pt])

                    def c2(n):
                        G, h, ci, nch, kcol, vt, a, b, bias, qm = items[n]
                        p_t, b_pt = PT[n % NP]
                        if bias is not None:
                            S.dve(lambda e: e.tensor_tensor(out=p_t[:, a:b], in0=p_t[:, a:b], in1=bias, op=ALU.mult),
                                  reads=[b_pt, b_MT, b_FT], writes=[b_pt])

                    def c3(n):
                        G, h, ci, nch, kcol, vt, a, b, bias, qm = items[n]
                        p_t, b_pt = PT[n % NP]
                        num, b_num = pnum[h % 2]
                        sl = va_slot(vt)
                        S.pe(lambda e: e.matmul(num[:, a:b], lhsT=VA[:, sl, h, :], rhs=p_t[:, a:b],
                                                start=(ci == 0), stop=(ci == nch - 1)),
                             reads=[b_pt, b_VA[sl]], writes=[b_num])

                    def c4(n):
                        G, h, ci, nch, kcol, vt, a, b, bias, qm = items[n]
                        if ci != nch - 1:
                            return
                        cp, po = h // 2, (h % 2) * 64
                        num, b_num = pnum[h % 2]
                        r_t, b_r = rden[h % 2]
                        y_t, b_y = yat[G % 2]
                        S.dve(lambda e: e.reciprocal(out=r_t[0:64, :], in_=num[64:128, :]),
                              reads=[b_num], writes=[b_r])
                        S.dve(lambda e: e.tensor_tensor(out=y_t[po:po + 64, cp, :], in0=num[0:64, :],
                                                        in1=r_t[0:64, :], op=ALU.mult),
                              reads=[b_num, b_r], writes=[b_y])
                        if h == 7:
                            S.dma("sync", yav[:, :, G * 512:(G + 1) * 512], y_t[:], reads=[b_y])

                    pipeline(len(items), [c0, c1, c2, c3, c4])
                    S.flush()

        if "C" in phases:
            with ExitStack() as ph:
                wg, b_wg = sb(ph, "wg", [128, 8, 2048], BF16)
                wfo, b_wfo = sb(ph, "wfo", [128, 4, 1024], BF16)
                wao, b_wao = sb(ph, "wao", [128, 4, 1024], BF16)
                wo, b_wo = sb(ph, "wo", [128, 8, 1024], BF16)
                hxg = [sb(ph, "hxg%d" % i, [128, 8, 512], BF16) for i in range(2)]
                ytg = [sb(ph, "ytg%d" % i, [128, 4, 512], BF16) for i in range(2)]
                yag = [sb(ph, "yag%d" % i, [128, 4, 512], BF16) for i in range(2)]
                mg = [sb(ph, "mg%d" % i, [128, 8, 512], BF16) for i in range(2)]
                xo = [sb(ph, "xo%d" % i, [128, 1024], F32) for i in range(8)]
                sgf = [sb(ph, "sgf%d" % i, [128, 512], F32) for i in range(2)]
                sga = [sb(ph, "sga%d" % i, [128, 512], F32) for i in range(2)]
                tf = [sb(ph, "tf%d" % i, [128, 512], F32) for i in range(2)]
                ta = [sb(ph, "ta%d" % i, [128, 512], F32) for i in range(2)]
                tm = [sb(ph, "tm%d" % i, [128, 512], F32) for i in range(2)]
                pA = [ps(ph, "pA%d" % i, [128, 512]) for i in range(2)]
                pB = [ps(ph, "pB%d" % i, [128, 512]) for i in range(2)]
                pC, b_pC = ps(ph, "pC", [128, 512])
                pD, b_pD = ps(ph, "pD", [128, 512])
                pE = [ps(ph, "pE%d" % i, [128, 512]) for i in range(2)]
                b_wgc = [Buf() for _ in range(8)]
                b_wfoc = [Buf() for _ in range(4)]
                b_waoc = [Buf() for _ in range(4)]
                wgv = w_gate.rearrange("(k p) n -> p k n", p=128)
                wfov = w_fo.rearrange("(k p) n -> p k n", p=128)
                waov = w_ao.rearrange("(k p) n -> p k n", p=128)
                for i4 in range(4):
                    for blk in (i4, 4 + i4):
                        S.dma("gpsimd", wg[:, :, blk * 256:(blk + 1) * 256], wgv[:, :, blk * 256:(blk + 1) * 256],
                              writes=[b_wgc[blk]])
                    S.dma("gpsimd", wfo[:, :, i4 * 256:(i4 + 1) * 256], wfov[:, :, i4 * 256:(i4 + 1) * 256],
                          writes=[b_wfoc[i4]])
                    S.dma("gpsimd", wao[:, :, i4 * 256:(i4 + 1) * 256], waov[:, :, i4 * 256:(i4 + 1) * 256],
                          writes=[b_waoc[i4]])
                for kc in range(0, 8, 4):
                    S.dma("gpsimd", wo[:, kc:kc + 4, :], w_out.rearrange("(k p) n -> p k n", p=128)[:, kc:kc + 4, :],
                          writes=[b_wo])
                hxv = hx_scr.rearrange("k p n -> p k n")
                ytv = yt_scr.rearrange("c p n -> p c n")
                yav = ya_scr.rearrange("c p n -> p c n")

                def loads(G):
                    i = G % 2
                    sl = slice(G * 512, (G + 1) * 512)
                    S.dma("sync", hxg[i][0][:], hxv[:, :, sl], writes=[hxg[i][1]])
                    S.dma("sync", ytg[i][0][:], ytv[:, :, sl], writes=[ytg[i][1]])
                    S.dma("sync", yag[i][0][:], yav[:, :, sl], writes=[yag[i][1]])
                    for tt in range(4):
                        x_o, b_xo = xo[(G * 4 + tt) % 8]
                        tok = G * 512 + tt * 128
                        S.dma("sync", x_o[:], xl[256 + tok:256 + tok + 128, :], writes=[b_xo])

                loads(0)
                for G in range(8):
                    i = G % 2
                    if G + 1 < 8:
                        loads(G + 1)
                    hx_t, b_hx = hxg[i]
                    yt_t, b_yt = ytg[i]
                    ya_t, b_ya = yag[i]
                    m_t, b_m = mg[i]
                    for j in range(8):
                        k = j % 2
                        p_a, b_pa = pA[k]
                        p_b, b_pb = pB[k]
                        for kc in range(8):
                            S.pe(lambda e, kc=kc, j=j, p_a=p_a, hx_t=hx_t: e.matmul(
                                p_a[:, :], lhsT=wg[:, kc, j * 128:(j + 1) * 128], rhs=hx_t[:, kc, :],
                                start=(kc == 0), stop=(kc == 7)), reads=[b_wgc[j // 2], b_hx], writes=[b_pa])
                        for kc in range(8):
                            S.pe(lambda e, kc=kc, j=j, p_b=p_b, hx_t=hx_t: e.matmul(
                                p_b[:, :], lhsT=wg[:, kc, 1024 + j * 128:1024 + (j + 1) * 128], rhs=hx_t[:, kc, :],
                                start=(kc == 0), stop=(kc == 7)), reads=[b_wgc[4 + j // 2], b_hx], writes=[b_pb])
                        for kc in range(4):
                            S.pe(lambda e, kc=kc, j=j, yt_t=yt_t: e.matmul(
                                pC[:, :], lhsT=wfo[:, kc, j * 128:(j + 1) * 128], rhs=yt_t[:, kc, :],
                                start=(kc == 0), stop=(kc == 3)), reads=[b_wfoc[j // 2], b_yt], writes=[b_pC])
                        for kc in range(4):
                            S.pe(lambda e, kc=kc, j=j, ya_t=ya_t: e.matmul(
                                pD[:, :], lhsT=wao[:, kc, j * 128:(j + 1) * 128], rhs=ya_t[:, kc, :],
                                start=(kc == 0), stop=(kc == 3)), reads=[b_waoc[j // 2], b_ya], writes=[b_pD])
                        s_f, b_sf = sgf[k]
                        s_a, b_sa = sga[k]
                        t_f, b_tf = tf[k]
                        t_a, b_ta = ta[k]
                        S.act(lambda e, s_f=s_f, p_a=p_a: e.activation(out=s_f[:, :], in_=p_a[:, :], func=AF.Sigmoid),
                              reads=[b_pa], writes=[b_sf])
                        S.act(lambda e, s_a=s_a, p_b=p_b: e.activation(out=s_a[:, :], in_=p_b[:, :], func=AF.Sigmoid),
                              reads=[b_pb], writes=[b_sa])
                        S.dve(lambda e, t_f=t_f, s_f=s_f: e.tensor_tensor(out=t_f[:, :], in0=pC[:, :], in1=s_f[:, :],
                                                                         op=ALU.mult),
                              reads=[b_pC, b_sf], writes=[b_tf])
                        S.dve(lambda e, t_a=t_a, s_a=s_a: e.tensor_tensor(out=t_a[:, :], in0=pD[:, :], in1=s_a[:, :],
                                                                         op=ALU.mult),
                              reads=[b_pD, b_sa], writes=[b_ta])
                        S.dve(lambda e, j=j, m_t=m_t, t_f=t_f, t_a=t_a: e.tensor_tensor(
                            out=m_t[:, j, :], in0=t_f[:, :], in1=t_a[:, :], op=ALU.add),
                            reads=[b_tf, b_ta], writes=[b_m])
                    for tt in range(4):
                        x_o, b_xo = xo[(G * 4 + tt) % 8]
                        tok = G * 512 + tt * 128
                        for hh in range(2):
                            p_e, b_pe = pE[hh]
                            t_m, b_tm = tm[hh]
                            for kc in range(8):
                                S.pe(lambda e, kc=kc, hh=hh, tt=tt, p_e=p_e, m_t=m_t: e.matmul(
                                    p_e[:, :], lhsT=m_t[:, kc, tt * 128:(tt + 1) * 128],
                                    rhs=wo[:, kc, hh * 512:(hh + 1) * 512], start=(kc == 0), stop=(kc == 7)),
                                    reads=[b_m, b_wo], writes=[b_pe])
                            S.dve(lambda e, hh=hh, p_e=p_e, t_m=t_m: e.tensor_tensor(
                                out=t_m[:, :], in0=p_e[:, :], in1=gb[:, 0, hh * 512:(hh + 1) * 512], op=ALU.mult),
                                reads=[b_pe, b_gb], writes=[b_tm])
                            S.dve(lambda e, hh=hh, t_m=t_m, x_o=x_o: e.tensor_tensor(
                                out=x_o[:, hh * 512:(hh + 1) * 512], in0=t_m[:, :],
                                in1=x_o[:, hh * 512:(hh + 1) * 512], op=ALU.add), reads=[b_tm, b_xo], writes=[b_xo])
                        S.dma("sync", x1_scr[tok:tok + 128, :], x_o[:], reads=[b_xo])
                S.flush()

        if "D" in phases:
            with ExitStack() as ph:
                w1, b_w1 = sb(ph, "w1", [128, 8, 4096], BF16)
                w2, b_w2 = sb(ph, "w2", [128, 32, 1024], BF16)
                xo = [sb(ph, "xoD%d" % i, [128, 1024], F32) for i in range(4)]
                hx2 = [sb(ph, "hx2%d" % i, [128, 8, 256], BF16) for i in range(2)]
                aT, b_aT = sb(ph, "aT", [128, 32, 256], BF16)
                rt = [sb(ph, "rt%d" % i, [128, 256], F32) for i in range(2)]
                tm = [sb(ph, "tmD%d" % i, [128, 512], F32) for i in range(2)]
                W = norm_work(ph, "D", 4)
                pT = [ps(ph, "pTD%d" % i, [128, 1024], BF16) for i in range(2)]
                pH = [ps(ph, "pH%d" % i, [128, 512]) for i in range(2)]
                pO = [ps(ph, "pO%d" % i, [128, 512]) for i in range(4)]
                b_w1c = [Buf() for _ in range(8)]
                b_w2c = [Buf() for _ in range(8)]
                w1v = w_mlp1.rearrange("(k p) n -> p k n", p=128)
                w2v = w_mlp2.rearrange("(k p) n -> p k n", p=128)
                for cb in range(8):
                    S.dma("gpsimd", w1[:, :, cb * 512:(cb + 1) * 512], w1v[:, :, cb * 512:(cb + 1) * 512],
                          writes=[b_w1c[cb]])
                for cb in range(8):
                    S.dma("gpsimd", w2[:, cb * 4:cb * 4 + 4, :], w2v[:, cb * 4:cb * 4 + 4, :], writes=[b_w2c[cb]])

                def xb(Gm, tt):
                    return xo[(Gm * 2 + tt) % 4]

                def wk(Gm, tt):
                    return W[(Gm * 2 + tt) % 4]

                def n_front(Gm):
                    for tt in range(2):
                        x_o, b_xo = xb(Gm, tt)
                        tok = Gm * 256 + tt * 128
                        S.dma("sync", x_o[:], x1_scr[tok:tok + 128, :], writes=[b_xo])
                    for tt in range(2):
                        nt_stats(xb(Gm, tt)[0], xb(Gm, tt)[1], wk(Gm, tt))
                    for tt in range(2):
                        nt_recip(wk(Gm, tt))
                    for tt in range(2):
                        nt_scale(xb(Gm, tt)[0], xb(Gm, tt)[1], wk(Gm, tt))

                def n_back_tr(Gm, tt, j):
                    Wt = wk(Gm, tt)
                    p_T, b_pT = pT[tt]
                    S.pe(lambda e: e.transpose(p_T[:, j * 128:(j + 1) * 128], Wt["xn"][:, j * 128:(j + 1) * 128], idt[:]),
                         reads=[Wt["b_xn"], b_idt], writes=[b_pT])

                def n_back_mod(Gm, tt):
                    h_t, b_h = hx2[Gm % 2]
                    nt_mod(pT[tt][0], pT[tt][1], lambda j, tt=tt: h_t[:, j, tt * 128:(tt + 1) * 128], b_h, 16, 48)

                def n_back(Gm):
                    for tt in range(2):
                        for j in range(8):
                            n_back_tr(Gm, tt, j)
                        n_back_mod(Gm, tt)

                n_front(0)
                n_back(0)
                for Gm in range(16):
                    h_t, b_h = hx2[Gm % 2]
                    if Gm + 1 < 16:
                        n_front(Gm + 1)
                    for c in range(32):
                        p_h, b_ph = pH[c % 2]
                        r_t, b_rt = rt[c % 2]
                        for kc in range(8):
                            S.pe(lambda e, kc=kc, c=c, p_h=p_h, h_t=h_t: e.matmul(
                                p_h[:, 0:256], lhsT=w1[:, kc, c * 128:(c + 1) * 128], rhs=h_t[:, kc, :],
                                start=(kc == 0), stop=(kc == 7)), reads=[b_w1c[c // 4], b_h], writes=[b_ph])
                        S.act(lambda e, p_h=p_h, r_t=r_t: e.activation(out=r_t[:, :], in_=p_h[:, 0:256], func=AF.Relu),
                              reads=[b_ph], writes=[b_rt])
                        S.dve(lambda e, c=c, r_t=r_t: e.tensor_tensor(out=aT[:, c, :], in0=r_t[:, :], in1=r_t[:, :],
                                                                    op=ALU.mult), reads=[b_rt], writes=[b_aT])
                        if Gm + 1 < 16 and 8 <= c < 24:
                            n_back_tr(Gm + 1, (c - 8) // 8, (c - 8) % 8)
                            if c % 8 == 7:
                                n_back_mod(Gm + 1, (c - 8) // 8)
                    for tt in range(2):
                        x_o, b_xo = xb(Gm, tt)
                        tok = Gm * 256 + tt * 128
                        for hh in range(2):
                            p_o, b_po = pO[tt * 2 + hh]
                            t_m, b_tm = tm[hh]
                            for c in range(32):
                                S.pe(lambda e, c=c, hh=hh, tt=tt, p_o=p_o: e.matmul(
                                    p_o[:, :], lhsT=aT[:, c, tt * 128:(tt + 1) * 128],
                                    rhs=w2[:, c, hh * 512:(hh + 1) * 512], start=(c == 0), stop=(c == 31)),
                                    reads=[b_aT, b_w2c[c // 4]], writes=[b_po])
                            S.dve(lambda e, hh=hh, p_o=p_o, t_m=t_m: e.tensor_tensor(
                                out=t_m[:, :], in0=p_o[:, :], in1=gb[:, 1, hh * 512:(hh + 1) * 512], op=ALU.mult),
                                reads=[b_po, b_gb], writes=[b_tm])
                            S.dve(lambda e, hh=hh, t_m=t_m, x_o=x_o: e.tensor_tensor(
                                out=x_o[:, hh * 512:(hh + 1) * 512], in0=t_m[:, :],
                                in1=x_o[:, hh * 512:(hh + 1) * 512], op=ALU.add), reads=[b_tm, b_xo], writes=[b_xo])
                        S.dma("sync", out[tok:tok + 128, :], x_o[:], reads=[b_xo])
                S.flush()
        S.flush()
    return nc


def _bf(a):
    return np.asarray(a, np.float32).astype(ml_dtypes.bfloat16)


def _consts():
    n = np.arange(128)
    ang = 2 * np.pi * ((n[:, None] * n[None, :]) % 128) / 128.0
    C, Sn = np.cos(ang), np.sin(ang)
    cs128 = np.concatenate([C, Sn], 1)
    ccsc = np.concatenate([C, Sn, -Sn, C], 1)
    ident = np.eye(128)
    return _bf(cs128), _bf(ccsc), _bf(ident)


def _t3(half):
    n2 = np.arange(128) % 64
    k1 = np.arange(128)
    k2 = np.arange(32) + 32 * half
    k = k1[None, :, None] + 128 * k2[None, None, :]
    ang = 2 * np.pi * ((n2[:, None, None] * k) % 8192) / 8192.0
    t = np.stack([np.cos(ang), -np.sin(ang)], 1) / 1024.0
    return _bf(t)


def _rope(half):
    inv = 10000.0 ** (-np.arange(0, 32, 2, dtype=np.float64) / 32.0)
    tab = np.zeros((38, 128, 128), np.float32)
    tab[0:2, :, 0:64] = 1.0
    p = np.arange(128)
    col = (p % 64).astype(np.float64)
    for lt in range(36):
        lr = 2 * lt + p // 64
        row = np.clip(64 * half - 4 + lr, 0, 127).astype(np.float64)
        ar = row[:, None] * inv[None, :]
        ac = col[:, None] * inv[None, :]
        cos = np.concatenate([np.cos(ar), np.cos(ar), np.cos(ac), np.cos(ac)], 1)
        sin = np.concatenate([-np.sin(ar), np.sin(ar), -np.sin(ac), np.sin(ac)], 1)
        tab[2 + lt, :, 0:64] = cos
        tab[2 + lt, :, 64:128] = sin
    return tab


def _bias_tables(rpb):
    kc = np.arange(64)
    c = np.arange(64)
    cs = np.clip(c - 8, 0, 48)
    colv = (kc[:, None] >= cs[None, :]) & (kc[:, None] < cs[None, :] + 16)
    dc = np.clip(kc[:, None] - c[None, :] + 15, 0, 30)
    MT = np.full((2, 64, 8, 9, 64), NEG, np.float32)
    FT = np.full((2, 64, 8, 14, 64), NEG, np.float32)
    for par in range(2):
        for s in range(9):
            dr = (3 - s) + par
            if -4 <= dr <= 3:
                vals = rpb[:, dr + 7][:, dc]
                MT[par, :, :, s, :] = np.where(colv[:, None, :], vals.transpose(1, 0, 2), NEG)
        for s in range(14):
            dr = (6 - s) + par
            if -7 <= dr <= 7:
                vals = rpb[:, dr + 7][:, dc]
                FT[par, :, :, s, :] = np.where(colv[:, None, :], vals.transpose(1, 0, 2), NEG)
    return _bf(MT.reshape(128, 8, 576)), _bf(FT.reshape(128, 8, 896))


def _qmask(half):
    q = np.zeros((4, 512), np.float32)
    if half == 0:
        q[0, 0:256] = NEG
        q[3, :] = NEG
    else:
        q[1, :] = NEG
        q[2, 256:512] = NEG
    return _bf(q.reshape(1, 2048))


def _pj(v, nj):
    return np.ascontiguousarray(np.asarray(v, np.float32).reshape(nj, 128).T)


def make_in_maps(x, c, ctx, c_ctx, w_ada, b_ada, norm1_g, norm2_g, w_in, q_norm_g, k_norm_g, rpb,
                 w_branch_gate, w_fourier_out, w_attn_out, w_out, w_mlp1, w_mlp2):
    f = lambda a: np.ascontiguousarray(np.asarray(a, np.float32))
    x, c, ctx, c_ctx = f(x), f(c), f(ctx), f(c_ctx)
    cs128, ccsc, ident = _consts()
    MT, FT = _bias_tables(f(rpb)[0])
    b = f(b_ada)[0]
    bT = _pj(b, 48)
    bgt = np.ascontiguousarray(np.broadcast_to(np.stack([b[2048:3072], b[5120:6144]], 0)[None], (128, 2, 1024)))
    ngt = np.concatenate([_pj(f(norm1_g)[0], 8), _pj(f(norm2_g)[0], 8)], 1)
    gq = np.tile(f(q_norm_g)[0], 8)
    gk = np.tile(f(k_norm_g)[0], 8)
    gqk = np.ascontiguousarray(np.broadcast_to(np.stack([gq, gk], 0)[None], (128, 2, 512)))
    shared = dict(w_ada=f(w_ada)[0], bT=bT, bg=bgt, ng=ngt, gqk=gqk, w_in=f(w_in)[0], w_gate=f(w_branch_gate)[0],
                  w_fo=f(w_fourier_out)[0], w_ao=f(w_attn_out)[0], w_out=f(w_out)[0], w_mlp1=f(w_mlp1)[0],
                  w_mlp2=f(w_mlp2)[0], MT=MT, FT=FT, ident=ident, cs128=cs128, ccsc=ccsc)
    per_half = [dict(rope=_rope(h), qmask=_qmask(h), t3=_t3(h)) for h in range(2)]
    maps = []
    for core in range(8):
        bi, half = core // 2, core % 2
        rows = np.clip(64 * half - 4 + np.arange(72), 0, 127)
        xl = np.ascontiguousarray(x[bi].reshape(128, 64, 1024)[rows].reshape(4608, 1024))
        ccl = np.ascontiguousarray(np.stack([_pj(c[bi], 8), _pj(c_ctx, 8)], 2))
        m = dict(shared)
        m.update(per_half[half])
        m.update(xf=x[bi], xl=xl, ctx=ctx[bi], cc=ccl)
        maps.append(m)
    return maps


_NC = {}


def kernel(**inputs):
    if "nc" not in _NC:
        _NC["nc"] = build()
    maps = make_in_maps(**inputs)
    res = run_bass_kernel_spmd(_NC["nc"], maps, core_ids=list(range(8)))
    outp = np.empty((4, 8192, 1024), np.float32)
    for core in range(8):
        bi, half = core // 2, core % 2
        outp[bi, half * 4096:(half + 1) * 4096] = res.results[core]["out"]
    return outp
```

```python
import numpy as np
import ml_dtypes
from contextlib import ExitStack
import concourse.bass as bass
import concourse.mybir as mybir
from concourse.bass_utils import run_bass_kernel_spmd

F32 = mybir.dt.float32
BF16 = mybir.dt.bfloat16
AF = mybir.ActivationFunctionType
ALU = mybir.AluOpType
AX = mybir.AxisListType
NEG = -30000.0
EPS = 1e-6


class Buf:
    __slots__ = ("lw", "rd")

    def __init__(self):
        self.lw = None
        self.rd = {}


class Op:
    __slots__ = ("eng", "fn", "deps", "dma", "sem", "val", "signal")


CENG = ["gpsimd", "scalar", "vector", "tensor"]
ENGS = ["sync"] + CENG


class Sched:
    def __init__(self, nc, stack, ndma=40):
        self.nc = nc
        self.csem = {e: stack.enter_context(nc.semaphore("c_" + e)) for e in CENG}
        self.ccount = {e: 0 for e in CENG}
        self.dsem = {q: [stack.enter_context(nc.semaphore("d_%s%d" % (q, i))) for i in range(ndma)]
                     for q in ("sync", "gpsimd")}
        self.dcount = {q: [0] * ndma for q in self.dsem}
        self.dnext = {q: 0 for q in self.dsem}
        self.dlast = {q: [None] * ndma for q in self.dsem}
        self.ndma = ndma
        self.ops = []
        self.waited = {e: {} for e in ENGS}
        self.semobj = {}

    def add(self, eng, fn, reads=(), writes=(), dma=False):
        op = Op()
        op.eng, op.fn, op.dma, op.deps, op.signal = eng, fn, dma, {}, dma
        op.sem = op.val = None
        for b in reads:
            if b.lw is not None:
                op.deps[b.lw] = True
        for b in writes:
            if b.lw is not None and b.lw not in op.deps:
                op.deps[b.lw] = False
            for r in b.rd.values():
                if r is not op and r not in op.deps:
                    op.deps[r] = False
        key = id(op) if dma else eng
        for b in reads:
            b.rd[key] = op
        for b in writes:
            b.lw = op
            b.rd = {}
        self.ops.append(op)
        return op

    def dma(self, q, out, in_, reads=(), writes=()):
        return self.add(q, lambda e: e.dma_start(out=out, in_=in_), reads, writes, dma=True)

    def pe(self, fn, reads=(), writes=()):
        return self.add("tensor", fn, reads, writes)

    def act(self, fn, reads=(), writes=()):
        return self.add("scalar", fn, reads, writes)

    def dve(self, fn, reads=(), writes=()):
        return self.add("vector", fn, reads, writes)

    def pool(self, fn, reads=(), writes=()):
        return self.add("gpsimd", fn, reads, writes)

    def flush(self):
        ops = self.ops
        self.ops = []
        if not ops:
            return
        need = {}
        last = {}
        cur = set(ops)
        for op in ops:
            nl = []
            for d, raw in op.deps.items():
                if d not in cur:
                    continue
                if d.dma or d.eng != op.eng or (raw and not op.dma) or op.dma:
                    if not d.dma:
                        d.signal = True
                    nl.append(d)
            need[op] = nl
            if not op.dma:
                last[op.eng] = op
        for op in last.values():
            op.signal = True
        for op in ops:
            if op.dma:
                q = op.eng
                s = self.dnext[q]
                self.dnext[q] = (s + 1) % self.ndma
                prev = self.dlast[q][s]
                if prev is not None:
                    need[op].append(prev)
                self.dcount[q][s] += 16
                op.sem, op.val = self.dsem[q][s], self.dcount[q][s]
                self.dlast[q][s] = op
            elif op.signal:
                self.ccount[op.eng] += 1
                op.sem, op.val = self.csem[op.eng], self.ccount[op.eng]
        streams = {e: [] for e in ENGS}
        for op in ops:
            streams[op.eng].append(op)
        finals = []
        for e in CENG:
            if self.ccount[e] > 0:
                finals.append((self.csem[e], self.ccount[e]))
        for q in self.dsem:
            for i in range(self.ndma):
                if self.dcount[q][i] > 0:
                    finals.append((self.dsem[q][i], self.dcount[q][i]))

        def run(ename):
            def body(e):
                wd = self.waited[ename]
                for op in streams[ename]:
                    for d in need[op]:
                        k = id(d.sem)
                        if wd.get(k, 0) < d.val:
                            e.wait_ge(d.sem, d.val)
                            wd[k] = d.val
                    ins = op.fn(e)
                    if op.signal:
                        ins.then_inc(op.sem, 16 if op.dma else 1)
                for sem, val in finals:
                    k = id(sem)
                    if wd.get(k, 0) < val:
                        e.wait_ge(sem, val)
                        wd[k] = val
            return body

        with self.nc.Block() as block:
            block.sync(run("sync"))
            block.gpsimd(run("gpsimd"))
            block.scalar(run("scalar"))
            block.vector(run("vector"))
            block.tensor(run("tensor"))


def mkap(t, offset, dims):
    base = t[:]
    p = base.ap[0]
    return bass.AP(tensor=base.tensor, offset=offset, ap=[[p[0], p[1]]] + [list(d) for d in dims])


def build(debug=False, phases="MABCD"):
    nc = bass.Bass("TRN2", target_bir_lowering=False)

    def din(name, shape, dt=F32):
        return nc.dram_tensor(name, list(shape), dt, kind="ExternalInput").ap()

    xf = din("xf", [8192, 1024])
    xl = din("xl", [4608, 1024])
    ctx = din("ctx", [256, 1024])
    cc = din("cc", [128, 8, 2])
    w_ada = din("w_ada", [1024, 6144])
    bT = din("bT", [128, 48])
    bg = din("bg", [128, 2, 1024])
    ng = din("ng", [128, 16])
    gqk = din("gqk", [128, 2, 512])
    w_in = din("w_in", [1024, 2048])
    w_gate = din("w_gate", [1024, 2048])
    w_fo = din("w_fo", [512, 1024])
    w_ao = din("w_ao", [512, 1024])
    w_out = din("w_out", [1024, 1024])
    w_mlp1 = din("w_mlp1", [1024, 4096])
    w_mlp2 = din("w_mlp2", [4096, 1024])
    rope = din("rope", [38, 128, 128])
    MTd = din("MT", [128, 8, 576], BF16)
    FTd = din("FT", [128, 8, 896], BF16)
    qmd = din("qmask", [1, 2048], BF16)
    identd = din("ident", [128, 128], BF16)
    cs128d = din("cs128", [128, 256], BF16)
    ccscd = din("ccsc", [128, 512], BF16)
    t3d = din("t3", [128, 2, 128, 32], BF16)
    out = nc.dram_tensor("out", [4096, 1024], F32, kind="ExternalOutput").ap()
    skind = "ExternalOutput" if debug else "Internal"
    yt_scr = nc.dram_tensor("yt_scr", [4, 128, 4096], BF16, kind=skind).ap()
    ya_scr = nc.dram_tensor("ya_scr", [4, 128, 4096], BF16, kind=skind).ap()
    hx_scr = nc.dram_tensor("hx_scr", [8, 128, 4096], BF16, kind=skind).ap()
    x1_scr = nc.dram_tensor("x1_scr", [4096, 1024], F32, kind=skind).ap()
    if debug:
        dbg = nc.dram_tensor("dbg", [128, 2048], F32, kind="ExternalOutput").ap()

    with ExitStack() as gst:
        GA = gst.enter_context
        S = Sched(nc, gst)

        def sb(st, name, shape, dt):
            return st.enter_context(nc.sbuf_tensor(name, list(shape), dt)), Buf()

        def ps(st, name, shape, dt=F32):
            return st.enter_context(nc.psum_tensor(name, list(shape), dt)), Buf()

        mod, b_mod = sb(gst, "mod", [128, 48, 2], F32)
        sce, b_sce = sb(gst, "sce", [128, 2, 8, 2], F32)
        gb, b_gb = sb(gst, "gb", [128, 2, 1024], F32)
        idt, b_idt = sb(gst, "idt", [128, 128], BF16)
        ones, b_ones = sb(gst, "ones", [128, 128], BF16)
        epsb, b_eps = sb(gst, "epsb", [128, 1], F32)

        S.pool(lambda e: e.memset(epsb[:], EPS), writes=[b_eps])
        S.pool(lambda e: e.memset(ones[:], 1.0), writes=[b_ones])
        S.dma("sync", idt[:], identd, writes=[b_idt])

        def pipeline(n, stages, hooks=None):
            d = len(stages)
            for step in range(n + d - 1):
                if hooks and step in hooks:
                    hooks[step]()
                for si in range(d - 1, -1, -1):
                    t = step - si
                    if 0 <= t < n:
                        stages[si](t)

        def nt_stats(xt, b_xt, W):
            S.act(lambda e: e.activation(out=W["junk"][:], in_=xt[:], func=AF.Square, accum_out=W["st"][:, 0:1]),
                  reads=[b_xt], writes=[W["b_junk"], W["b_st"]])
            S.act(lambda e: e.activation(out=W["st"][:, 1:2], in_=W["st"][:, 0:1], func=AF.Sqrt, scale=1.0 / 1024,
                                         bias=epsb[:, 0:1]), reads=[W["b_st"], b_eps], writes=[W["b_st"]])

        def nt_recip(W):
            S.dve(lambda e: e.reciprocal(out=W["st"][:, 2:3], in_=W["st"][:, 1:2]), reads=[W["b_st"]],
                  writes=[W["b_st"]])

        def nt_scale(xt, b_xt, W):
            S.act(lambda e: e.activation(out=W["xn"][:], in_=xt[:], func=AF.Copy, scale=W["st"][:, 2:3]),
                  reads=[b_xt, W["b_st"]], writes=[W["b_xn"]])

        def nt_tr(W, pT, b_pT):
            for j in range(8):
                S.pe(lambda e, j=j: e.transpose(pT[:, j * 128:(j + 1) * 128], W["xn"][:, j * 128:(j + 1) * 128],
                                                idt[:]), reads=[W["b_xn"], b_idt], writes=[b_pT])

        def nt_mod(pT, b_pT, dst3, b_dst, sc_off, sh_off):
            for j in range(8):
                S.act(lambda e, j=j: e.activation(out=dst3(j), in_=pT[:, j * 128:(j + 1) * 128], func=AF.Identity,
                                                  scale=mkap(sce, sc_off + 2 * j, [[1, 1]]),
                                                  bias=mkap(mod, sh_off + 2 * j, [[1, 1]])),
                      reads=[b_pT, b_sce, b_mod], writes=[b_dst])

        def norm_work(st, tag, n):
            junk, b_junk = sb(st, "junk" + tag, [128, 1024], BF16)
            res = []
            for i in range(n):
                st_, b_st = sb(st, "st%s%d" % (tag, i), [128, 4], F32)
                xn, b_xn = sb(st, "xn%s%d" % (tag, i), [128, 1024], BF16)
                res.append(dict(junk=junk, b_junk=b_junk, st=st_, b_st=b_st, xn=xn, b_xn=b_xn))
            return res

        if "A" in phases:
            with ExitStack() as ph:
                cc_sb, b_cc = sb(ph, "cc_sb", [128, 8, 2], F32)
                scb, b_scb = sb(ph, "scb", [128, 8, 2], BF16)
                screp, b_screp = sb(ph, "screp", [128, 8, 128], BF16)
                bT_sb, b_bT = sb(ph, "bT_sb", [128, 48], F32)
                bg_sb, b_bg = sb(ph, "bg_sb", [128, 2, 1024], F32)
                ng_sb, b_ng = sb(ph, "ng_sb", [128, 16], F32)
                wsl = [sb(ph, "wsl%d" % i, [128, 8, 512], BF16) for i in range(2)]
                w_inf, b_winf = sb(ph, "w_inf", [128, 8, 512], BF16)
                u_all = ph.enter_context(nc.sbuf_tensor("u_all", [128, 64, 512], BF16))
                b_u = [Buf() for _ in range(64)]
                Bg, b_Bg = sb(ph, "Bg", [128, 64, 256], BF16)
                YTg = [sb(ph, "YTg%d" % i, [128, 4096], BF16) for i in range(1)]
                Zt = [sb(ph, "Zt%d" % i, [128, 512], BF16) for i in range(3)]
                t3_sb, b_t3 = sb(ph, "t3_sb", [128, 2, 128, 32], BF16)
                cs_sb, b_cs = sb(ph, "cs_sb", [128, 256], BF16)
                ccsc_sb, b_ccsc = sb(ph, "ccsc_sb", [128, 512], BF16)
                xt = [sb(ph, "xtA%d" % i, [128, 1024], F32) for i in range(3)]
                hxT = [sb(ph, "hxTA%d" % i, [128, 8, 128], BF16) for i in range(2)]
                W = norm_work(ph, "A", 3)
                pT = [ps(ph, "pTA%d" % i, [128, 1024], BF16) for i in range(2)]
                pu = [ps(ph, "puA%d" % i, [128, 512]) for i in range(2)]
                pa = [ps(ph, "paA%d" % i, [128, 512]) for i in range(2)]
                pb = [ps(ph, "pbA%d" % i, [128, 512]) for i in range(2)]
                wv = w_in.rearrange("(k p) n -> p k n", p=128)
                for kc in range(8):
                    S.dma("gpsimd", w_inf[:, kc, :], wv[:, kc, 0:512], writes=[b_winf])
                S.dma("sync", t3_sb[:], t3d, writes=[b_t3])
                S.dma("sync", cs_sb[:], cs128d, writes=[b_cs])
                S.dma("sync", ccsc_sb[:], ccscd, writes=[b_ccsc])
                xsrc = xf.rearrange("(n1 n2) d -> n2 n1 d", n2=64)
                pmod, b_pmod = pb[0]
                pg = [pa[0], pa[1]]
                S.dma("sync", cc_sb[:], cc, writes=[b_cc])
                S.dma("sync", bT_sb[:], bT, writes=[b_bT])
                S.dma("sync", bg_sb[:], bg, writes=[b_bg])
                S.dma("sync", ng_sb[:], ng, writes=[b_ng])
                S.act(lambda e: e.activation(out=scb[:], in_=cc_sb[:], func=AF.Silu), reads=[b_cc], writes=[b_scb])
                S.act(lambda e: e.activation(out=screp[:], in_=cc_sb[:, :, 0:1].broadcast_to([128, 8, 128]),
                                             func=AF.Silu), reads=[b_cc], writes=[b_screp])
                wav = w_ada.rearrange("(k p) n -> p k n", p=128)

                def m_dma(hs):
                    w, b_w = wsl[hs % 2]
                    S.dma("gpsimd", w[:], wav[:, :, hs * 512:(hs + 1) * 512], writes=[b_w])

                def m_proc(hs):
                    w, b_w = wsl[hs % 2]
                    v, hh = hs // 2, hs % 2
                    for j4 in range(4):
                        col = (v * 8 + hh * 4 + j4) * 2
                        for kc in range(8):
                            S.pe(lambda e, j4=j4, kc=kc, col=col: e.matmul(
                                pmod[:, col:col + 2], lhsT=w[:, kc, j4 * 128:(j4 + 1) * 128], rhs=scb[:, kc, :],
                                start=(kc == 0), stop=(kc == 7)), reads=[b_w, b_scb], writes=[b_pmod])
                    if v in (2, 5):
                        gi = 0 if v == 2 else 1
                        pgt, b_pg = pg[hh]
                        for kc in range(8):
                            S.pe(lambda e, kc=kc: e.matmul(pgt[:, :], lhsT=screp[:, kc, :], rhs=w[:, kc, :],
                                                           start=(kc == 0), stop=(kc == 7)),
                                 reads=[b_w, b_screp], writes=[b_pg])
                        S.dve(lambda e: e.tensor_tensor(out=gb[:, gi, hh * 512:(hh + 1) * 512], in0=pgt[:, :],
                                                        in1=bg_sb[:, gi, hh * 512:(hh + 1) * 512], op=ALU.add),
                              reads=[b_pg, b_bg], writes=[b_gb])
                    if hs + 2 < 12:
                        m_dma(hs + 2)
                    if hh == 1:
                        S.dve(lambda e: e.tensor_tensor(
                            out=mod[:, v * 8:(v + 1) * 8, :],
                            in0=pmod[:, v * 16:(v + 1) * 16].rearrange("p (a b) -> p a b", b=2),
                            in1=bT_sb[:, v * 8:(v + 1) * 8].unsqueeze(2).broadcast_to([128, 8, 2]), op=ALU.add),
                            reads=[b_pmod, b_bT], writes=[b_mod])
                        if v in (1, 4):
                            wi = 0 if v == 1 else 1
                            S.dve(lambda e: e.scalar_tensor_tensor(
                                out=sce[:, wi], in0=mod[:, v * 8:(v + 1) * 8, :], scalar=1.0,
                                in1=ng_sb[:, wi * 8:(wi + 1) * 8].unsqueeze(2).broadcast_to([128, 8, 2]),
                                op0=ALU.add, op1=ALU.mult), reads=[b_mod, b_ng], writes=[b_sce])

                m_dma(0)
                m_dma(1)
                for hs in range(4):
                    m_proc(hs)
                sh_rep, b_shrep = sb(ph, "sh_rep", [128, 8, 128], BF16)
                shw, b_shw = sb(ph, "shw", [128, 512], F32)
                S.act(lambda e: e.copy(out=sh_rep[:], in_=mkap(mod, 0, [[2, 8], [0, 128]])), reads=[b_mod],
                      writes=[b_shrep])
                for kc in range(8):
                    S.pe(lambda e, kc=kc: e.matmul(pu[0][0][:, :], lhsT=sh_rep[:, kc, :], rhs=w_inf[:, kc, :],
                                                   start=(kc == 0), stop=(kc == 7)),
                         reads=[b_shrep, b_winf], writes=[pu[0][1]])
                S.dve(lambda e: e.tensor_copy(out=shw[:, :], in_=pu[0][0][:, :]), reads=[pu[0][1]], writes=[b_shw])

                def a0(t):
                    S.dma("sync", xt[t % 3][0][:], xsrc[t], writes=[xt[t % 3][1]])

                def a1(t):
                    nt_stats(xt[t % 3][0], xt[t % 3][1], W[t % 3])

                def a2(t):
                    nt_recip(W[t % 3])

                def a3(t):
                    nt_scale(xt[t % 3][0], xt[t % 3][1], W[t % 3])

                def a4(t):
                    nt_tr(W[t % 3], pT[t % 2][0], pT[t % 2][1])

                def a5(t):
                    h_t, b_h = hxT[t % 2]
                    p_T, b_pT = pT[t % 2]
                    S.dve(lambda e: e.tensor_tensor(out=h_t[:], in0=p_T[:, :].rearrange("p (a b) -> p a b", b=128),
                                                    in1=mkap(sce, 0, [[2, 8], [0, 128]]), op=ALU.mult),
                          reads=[b_pT, b_sce], writes=[b_h])

                def a6(t):
                    h_t, b_h = hxT[t % 2]
                    p_u, b_pu = pu[t % 2]
                    for kc in range(8):
                        S.pe(lambda e, kc=kc: e.matmul(p_u[:, :], lhsT=h_t[:, kc, :], rhs=w_inf[:, kc, :],
                                                       start=(kc == 0), stop=(kc == 7)),
                             reads=[b_h, b_winf], writes=[b_pu])

                def a7(t):
                    p_u, b_pu = pu[t % 2]
                    S.dve(lambda e: e.tensor_tensor(out=u_all[:, t, :], in0=p_u[:, :], in1=shw[:, :], op=ALU.add),
                          reads=[b_pu, b_shw], writes=[b_u[t]])

                pipeline(64, [a0, a1, a2, a3, a4, a5, a6, a7], hooks={8 * (hs - 3): (lambda hs=hs: m_proc(hs))
                                                                    for hs in range(4, 12)})

                for g in range(4):
                    def s1mm(pr, g=g):
                        p_a, b_pa = pa[pr % 2]
                        for q in range(2):
                            n2 = 2 * pr + q
                            S.pe(lambda e, n2=n2, q=q: e.matmul(
                                p_a[:, q * 256:(q + 1) * 256], lhsT=u_all[:, n2, g * 128:(g + 1) * 128],
                                rhs=cs_sb[:, :], start=True, stop=True), reads=[b_u[n2], b_cs], writes=[b_pa])

                    def s1ev(pr):
                        p_a, b_pa = pa[pr % 2]
                        dst = Bg[:, 2 * pr:2 * pr + 2, :]
                        src = p_a[:, :].rearrange("p (a b) -> p a b", b=256)
                        if pr % 2 == 0:
                            S.act(lambda e: e.copy(out=dst, in_=src), reads=[b_pa], writes=[b_Bg])
                        else:
                            S.dve(lambda e: e.tensor_copy(out=dst, in_=src), reads=[b_pa], writes=[b_Bg])

                    pipeline(32, [s1mm, s1ev])
                    y_t, b_y = YTg[0]

                    def s2mm(j):
                        p_z, b_pz = pb[j % 2]
                        for lo in range(2):
                            k1 = 2 * j + lo
                            lr = mkap(Bg, k1, [[256, 64]])
                            ls = mkap(Bg, 128 + k1, [[256, 64]])
                            S.pe(lambda e, lr=lr, lo=lo: e.matmul(p_z[0:64, lo * 256:(lo + 1) * 256], lhsT=lr,
                                                                 rhs=ccsc_sb[:, 0:256], start=True, stop=False),
                                 reads=[b_Bg, b_ccsc], writes=[b_pz])
                            S.pe(lambda e, ls=ls, lo=lo: e.matmul(p_z[0:64, lo * 256:(lo + 1) * 256], lhsT=ls,
                                                                 rhs=ccsc_sb[:, 256:512], start=False, stop=True),
                                 reads=[b_Bg, b_ccsc], writes=[b_pz])

                    def s2ev(j):
                        p_z, b_pz = pb[j % 2]
                        z_t, b_z = Zt[j % 3]
                        if j % 2 == 0:
                            S.act(lambda e: e.copy(out=z_t[0:64, :], in_=p_z[0:64, :]), reads=[b_pz], writes=[b_z])
                        else:
                            S.dve(lambda e: e.tensor_copy(out=z_t[0:64, :], in_=p_z[0:64, :]), reads=[b_pz],
                                  writes=[b_z])

                    def s3mm(j, y_t=y_t, b_y=b_y):
                        z_t, b_z = Zt[j % 3]
                        for lo in range(2):
                            k1 = 2 * j + lo
                            col = (k1 % 16) * 32
                            p_y, b_py = pu[(k1 // 16) % 2]
                            S.pe(lambda e, k1=k1, col=col, p_y=p_y, lo=lo: e.matmul(
                                p_y[:, col:col + 32], lhsT=z_t[0:64, lo * 256:lo * 256 + 128],
                                rhs=t3_sb[0:64, 0, k1, :], start=True, stop=False), reads=[b_z, b_t3], writes=[b_py])
                            S.pe(lambda e, k1=k1, col=col, p_y=p_y, lo=lo: e.matmul(
                                p_y[:, col:col + 32], lhsT=z_t[0:64, lo * 256 + 128:lo * 256 + 256],
                                rhs=t3_sb[0:64, 1, k1, :], start=False, stop=True), reads=[b_z, b_t3], writes=[b_py])
                            if k1 % 16 == 15:
                                dst = mkap(y_t, 16 * (k1 // 16), [[1, 16], [128, 32]])
                                S.dve(lambda e, dst=dst, p_y=p_y: e.tensor_copy(
                                    out=dst, in_=p_y[:, :].rearrange("p (a b) -> p a b", b=32)),
                                    reads=[b_py], writes=[b_y])

                    pipeline(64, [s2mm, s2ev, s3mm])
                    S.dma("sync", yt_scr[g], y_t[:], reads=[b_y])
                S.flush()

        if "B" in phases:
            with ExitStack() as ph:
                KT = ph.enter_context(nc.sbuf_tensor("KT", [128, 4, 4864], BF16))
                b_KT = [Buf() for _ in range(38)]
                V = ph.enter_context(nc.sbuf_tensor("V", [128, 38, 512], BF16))
                b_V = [Buf() for _ in range(38)]
                qT = ph.enter_context(nc.sbuf_tensor("qT", [128, 4, 4096], BF16))
                b_qT = [Buf() for _ in range(32)]
                with ExitStack() as p1:
                    w_qkv, b_wqkv = sb(p1, "w_qkv", [128, 8, 1536], BF16)
                    gqk_sb, b_gqk = sb(p1, "gqk_sb", [128, 2, 512], F32)
                    xt = [sb(p1, "xtB%d" % i, [128, 1024], F32) for i in range(3)]
                    rp = [sb(p1, "rpB%d" % i, [128, 128], F32) for i in range(4)]
                    hxT = [sb(p1, "hxTB%d" % i, [128, 8, 128], BF16) for i in range(3)]
                    W = norm_work(p1, "B", 3)
                    pT = [ps(p1, "pTB%d" % i, [128, 1024], BF16) for i in range(1)]
                    pq = [[ps(p1, "pqB%d_%d" % (i, c), [128, 512]) for c in range(3)] for i in range(2)]
                    ptr = [ps(p1, "ptrB%d" % i, [128, 1024], BF16) for i in range(1)]
                    NB = 2
                    tq = []
                    for i in range(NB):
                        d = {}
                        for nm, dt in ():
                            d[nm] = sb(p1, "%s%d" % (nm, i), [128, 512], dt)
                        d["qr"] = sb(p1, "qr%d" % i, [128, 1024], BF16)
                        d["ss"] = sb(p1, "ss%d" % i, [128, 48], F32)
                        tq.append(d)
                    fr = [dict(qf=sb(p1, "qf%d" % i, [128, 512], F32), kf=sb(p1, "kf%d" % i, [128, 512], F32))
                          for i in range(3)]
                    t2r = dict(t2q=sb(p1, "t2q", [128, 512], F32), t2k=sb(p1, "t2k", [128, 512], F32))
                    t1r = [dict(t1q=sb(p1, "t1q%d" % i, [128, 512], F32), t1k=sb(p1, "t1k%d" % i, [128, 512], F32))
                           for i in range(3)]
                    wv = w_in.rearrange("(k p) n -> p k n", p=128)
                    for kc in range(8):
                        S.dma("gpsimd", w_qkv[:, kc, :], wv[:, kc, 512:2048], writes=[b_wqkv])
                    S.dma("sync", gqk_sb[:], gqk, writes=[b_gqk])
                    hxv = hx_scr.rearrange("k p n -> p k n")

                    def own(t):
                        return 2 <= t - 2 < 34

                    def b0(t):
                        src = ctx[t * 128:(t + 1) * 128, :] if t < 2 else xl[(t - 2) * 128:(t - 1) * 128, :]
                        S.dma("sync", xt[t % 3][0][:], src, writes=[xt[t % 3][1]])

                    def b1(t):
                        nt_stats(xt[t % 3][0], xt[t % 3][1], W[t % 3])

                    def b2(t):
                        nt_recip(W[t % 3])

                    def b3(t):
                        nt_scale(xt[t % 3][0], xt[t % 3][1], W[t % 3])

                    def b4(t):
                        nt_tr(W[t % 3], pT[0][0], pT[0][1])

                    def b5(t):
                        h_t, b_h = hxT[t % 3]
                        cond = 1 if t < 2 else 0
                        nt_mod(pT[0][0], pT[0][1], lambda j: h_t[:, j, :], b_h, cond, cond)

                    def b6(t):
                        h_t, b_h = hxT[t % 3]
                        if own(t):
                            o = (t - 4) * 128
                            S.dma("sync", hxv[:, :, o:o + 128], h_t[:], reads=[b_h])
                        for kc in range(8):
                            for c in range(3):
                                if c == 0 and not own(t):
                                    continue
                                p_q, b_pq = pq[t % 2][c]
                                S.pe(lambda e, kc=kc, c=c, p_q=p_q: e.matmul(
                                    p_q[:, :], lhsT=h_t[:, kc, :], rhs=w_qkv[:, kc, c * 512:(c + 1) * 512],
                                    start=(kc == 0), stop=(kc == 7)), reads=[b_h, b_wqkv], writes=[b_pq])

                    def b7(t):
                        d = tq[t % NB]
                        S.act(lambda e: e.copy(out=V[:, t, :], in_=pq[t % 2][2][0][:, :]),
                              reads=[pq[t % 2][2][1]], writes=[b_V[t]])
                        for w_, nm, sq in ((1, "kf", "t1k"), (0, "qf", "t1q")):
                            if w_ == 0 and not own(t):
                                continue
                            p_s, b_ps = pq[t % 2][w_]
                            f_t, b_f = fr[t % 3][nm]
                            s_t, b_s = t1r[t % 3][sq]
                            S.act(lambda e, f_t=f_t, p_s=p_s: e.copy(out=f_t[:, :], in_=p_s[:, :]),
                                  reads=[b_ps], writes=[b_f])
                            S.act(lambda e, s_t=s_t, p_s=p_s: e.activation(out=s_t[:, :], in_=p_s[:, :],
                                                                          func=AF.Square),
                                  reads=[b_ps], writes=[b_s])

                    def b8(t):
                        S.dma("sync", rp[t % 4][0][:], rope[t], writes=[rp[t % 4][1]])
                        d = tq[t % NB]
                        ss, b_ss = d["ss"]
                        for w_, sq in ((1, "t1k"), (0, "t1q")):
                            if w_ == 0 and not own(t):
                                continue
                            s_t, b_s = t1r[t % 3][sq]
                            S.dve(lambda e, w_=w_, s_t=s_t: e.tensor_reduce(
                                out=ss[:, w_ * 8:(w_ + 1) * 8], in_=s_t[:, :].rearrange("p (h d) -> p h d", d=64),
                                axis=AX.X, op=ALU.add), reads=[b_s], writes=[b_ss])

                    def b9(t):
                        d = tq[t % NB]
                        ss, b_ss = d["ss"]
                        lo_ = 0 if own(t) else 8
                        S.act(lambda e: e.activation(out=ss[:, 16 + lo_:32], in_=ss[:, lo_:16], func=AF.Sqrt,
                                                     scale=1.0 / 64, bias=epsb[:, 0:1]),
                              reads=[b_ss, b_eps], writes=[b_ss])

                    def b10(t):
                        d = tq[t % NB]
                        ss, b_ss = d["ss"]
                        rp_t, b_rp = rp[t % 4]
                        qr, b_qr = d["qr"]
                        lo_ = 0 if own(t) else 8
                        S.dve(lambda e: e.reciprocal(out=ss[:, 32 + lo_:48], in_=ss[:, 16 + lo_:32]), reads=[b_ss],
                              writes=[b_ss])
                        for w_, fn, nn, t1n, t2n in ((1, "kf", "kf", "t1k", "t2k"), (0, "qf", "qf", "t1q", "t2q")):
                            if w_ == 0 and not own(t):
                                continue
                            f_t, b_f = fr[t % 3][fn]
                            n_t, b_n = fr[t % 3][nn]
                            t1, b_t1 = t1r[t % 3][t1n]
                            t2, b_t2 = t2r[t2n]
                            S.dve(lambda e, w_=w_, f_t=f_t, n_t=n_t: e.scalar_tensor_tensor(
                                out=n_t[:, :].rearrange("p (h d) -> p h d", d=64),
                                in0=f_t[:, :].rearrange("p (h d) -> p h d", d=64),
                                scalar=(0.125 if w_ == 0 else 1.0),
                                in1=ss[:, 32 + w_ * 8:32 + (w_ + 1) * 8].unsqueeze(2).broadcast_to([128, 8, 64]),
                                op0=ALU.mult, op1=ALU.mult), reads=[b_f, b_ss], writes=[b_n])
                            S.dve(lambda e, w_=w_, n_t=n_t: e.tensor_tensor(out=n_t[:, :], in0=n_t[:, :],
                                                                         in1=gqk_sb[:, w_, :], op=ALU.mult),
                                  reads=[b_n, b_gqk], writes=[b_n])
                            S.dve(lambda e, n_t=n_t, t1=t1: e.tensor_tensor(
                                out=t1[:, :].rearrange("p (h d) -> p h d", d=64),
                                in0=n_t[:, :].rearrange("p (h d) -> p h d", d=64),
                                in1=rp_t[:, 0:64].unsqueeze(1).broadcast_to([128, 8, 64]), op=ALU.mult),
                                reads=[b_n, b_rp], writes=[b_t1])
                            for a in range(2):
                                o_ap = mkap(t2, a * 16, [[64, 8], [32, 2], [1, 16]])
                                i_ap = mkap(n_t, (1 - a) * 16, [[64, 8], [32, 2], [1, 16]])
                                s_ap = mkap(rp_t, 64 + a * 16, [[0, 8], [32, 2], [1, 16]])
                                S.dve(lambda e, o_ap=o_ap, i_ap=i_ap, s_ap=s_ap: e.tensor_tensor(
                                    out=o_ap, in0=i_ap, in1=s_ap, op=ALU.mult), reads=[b_n, b_rp], writes=[b_t2])
                            S.dve(lambda e, w_=w_, t1=t1, t2=t2: e.tensor_tensor(
                                out=qr[:, w_ * 512:(w_ + 1) * 512], in0=t1[:, :], in1=t2[:, :], op=ALU.add),
                                reads=[b_t1, b_t2], writes=[b_qr])

                    def b11(t):
                        d = tq[t % NB]
                        qr, b_qr = d["qr"]
                        p_t, b_pt = ptr[0]
                        for w_ in (1, 0):
                            if w_ == 0 and not own(t):
                                continue
                            for c in range(4):
                                S.pe(lambda e, c=c, w_=w_: e.transpose(
                                    p_t[:, w_ * 512 + c * 128:w_ * 512 + (c + 1) * 128],
                                    qr[:, w_ * 512 + c * 128:w_ * 512 + (c + 1) * 128], idt[:]),
                                    reads=[b_qr, b_idt], writes=[b_pt])

                    def b12(t):
                        p_t, b_pt = ptr[0]
                        S.act(lambda e: e.copy(out=KT[:, :, t * 128:(t + 1) * 128],
                                               in_=p_t[:, 512:1024].rearrange("p (a b) -> p a b", b=128)),
                              reads=[b_pt], writes=[b_KT[t]])
                        if own(t):
                            o = (t - 4) * 128
                            S.act(lambda e: e.copy(out=qT[:, :, o:o + 128],
                                                   in_=p_t[:, 0:512].rearrange("p (a b) -> p a b", b=128)),
                                  reads=[b_pt], writes=[b_qT[t - 4]])

                    pipeline(38, [b0, b1, b2, b3, b4, b5, b6, b7, b8, b9, b10, b11, b12])
                    S.flush()
                with ExitStack() as p2:
                    MT_sb, b_MT = sb(p2, "MT_sb", [128, 8, 576], BF16)
                    FT_sb, b_FT = sb(p2, "FT_sb", [128, 8, 896], BF16)
                    qm_sb, b_qm = sb(p2, "qm_sb", [1, 2048], BF16)
                    NP = 4
                    PT = [sb(p2, "PT%d" % i, [128, 512], BF16) for i in range(NP)]
                    rden = [sb(p2, "rden%d" % i, [128, 512], F32) for i in range(4)]
                    yat = [sb(p2, "yat%d" % i, [128, 4, 512], BF16) for i in range(2)]
                    pS = [ps(p2, "pS%d" % i, [128, 512]) for i in range(NP)]
                    pnum = [ps(p2, "pnum%d" % i, [128, 512]) for i in range(4)]
                    VA = p2.enter_context(nc.sbuf_tensor("VA", [128, 14, 8, 128], BF16))
                    b_VA = [Buf() for _ in range(14)]
                    S.pool(lambda e: e.memset(VA[:], 1.0), writes=b_VA)

                    def va_slot(vt):
                        return 12 + vt if vt < 2 else (vt - 2) % 12

                    def va_fill(vt):
                        sl = va_slot(vt)
                        S.act(lambda e: e.copy(out=VA[:, sl, :, 0:64],
                                               in_=V[:, vt, :].rearrange("p (h d) -> p h d", d=64)),
                              reads=[b_V[vt]], writes=[b_VA[sl]])
                    S.dma("sync", MT_sb[:], MTd, writes=[b_MT])
                    S.dma("sync", FT_sb[:], FTd, writes=[b_FT])
                    S.dma("sync", qm_sb[:], qmd, writes=[b_qm])
                    S.act(lambda e: e.activation(out=MT_sb[:], in_=MT_sb[:], func=AF.Exp), reads=[b_MT], writes=[b_MT])
                    S.act(lambda e: e.activation(out=FT_sb[:], in_=FT_sb[:], func=AF.Exp), reads=[b_FT], writes=[b_FT])
                    yav = ya_scr.rearrange("c p n -> p c n")
                    items = []

                    def head_chunks(G, h):
                        r0 = 4 + 8 * G
                        chunks = []
                        for c in range(2):
                            chunks.append((c * 128, c, 0, 512, None, None))
                        for m in range(8):
                            lrk = r0 - 4 + 2 * m
                            ia, ib = max(0, 2 * m - 7), min(7, 2 * m + 1)
                            a, b = ia * 64, (ib + 1) * 64
                            bias = MT_sb[:, h, (7 - 2 * m + ia) * 64:(7 - 2 * m + ib + 1) * 64]
                            qm = None
                            if G == 0:
                                qm = qm_sb[0:1, a:b]
                            elif G == 7:
                                qm = qm_sb[0:1, 1024 + a:1024 + b]
                            chunks.append((256 + lrk * 64, 2 + lrk // 2, a, b, bias, qm))
                        if G == 0:
                            for ee in range(4):
                                lrk = r0 + 2 * ee
                                chunks.append((256 + lrk * 64, 2 + lrk // 2, 0, 256,
                                               FT_sb[:, h, (6 - 2 * ee) * 64:(10 - 2 * ee) * 64],
                                               qm_sb[0:1, 512:768]))
                        if G == 7:
                            for ee in range(4):
                                lrk = r0 + 2 * ee
                                chunks.append((256 + lrk * 64, 2 + lrk // 2, 256, 512,
                                               FT_sb[:, h, (10 - 2 * ee) * 64:(14 - 2 * ee) * 64],
                                               qm_sb[0:1, 1536 + 256:1536 + 512]))
                        return chunks

                    for G in range(8):
                        for p in range(4):
                            chA = head_chunks(G, 2 * p)
                            chB = head_chunks(G, 2 * p + 1)
                            for ci in range(len(chA)):
                                items.append((G, 2 * p, ci, len(chA)) + chA[ci])
                                items.append((G, 2 * p + 1, ci, len(chB)) + chB[ci])

                    def c0(n):
                        G, h, ci, nch, kcol, vt, a, b, bias, qm = items[n]
                        if h == 0 and ci == 0:
                            if G == 0:
                                va_fill(0)
                                va_fill(1)
                            for lt in (range(0, 8) if G == 0 else range(4 * G + 4, 4 * G + 8)):
                                va_fill(2 + lt)
                        cp, po = h // 2, (h % 2) * 64
                        p_s, b_ps = pS[n % NP]
                        qb = b_qT[(G * 512 + a) // 128:(G * 512 + b + 127) // 128]
                        S.pe(lambda e: e.matmul(
                            p_s[:, a:b], lhsT=KT[po:po + 64, cp, kcol:kcol + 128],
                            rhs=qT[po:po + 64, cp, G * 512 + a:G * 512 + b], start=True, stop=(qm is None)),
                            reads=[b_KT[kcol // 128]] + qb, writes=[b_ps])
                        if qm is not None:
                            S.pe(lambda e: e.matmul(p_s[:, a:b], lhsT=ones[0:1, :], rhs=qm, start=False, stop=True),
                                 reads=[b_ones, b_qm], writes=[b_ps])

                    def c1(n):
                        G, h, ci, nch, kcol, vt, a, b, bias, qm = items[n]
                        p_s, b_ps = pS[n % NP]
                        p_t, b_pt = PT[n % NP]
                        S.act(lambda e: e.activation(out=p_t[:, a:b], in_=p_s[:, a:b], func=AF.Exp),
                              reads=[b_ps], writes=[b_pt])

                    def c2(n):
                        G, h, ci, nch, kcol, vt, a, b, bias, qm = items[n]
                        p_t, b_pt = PT[n % NP]
                        if bias is not None:
                            S.dve(lambda e: e.tensor_tensor(out=p_t[:, a:b], in0=p_t[:, a:b], in1=bias, op=ALU.mult),
                                  reads=[b_pt, b_MT, b_FT], writes=[b_pt])

                    def c3(n):
                        G, h, ci, nch, kcol, vt, a, b, bias, qm = items[n]
                        p_t, b_pt = PT[n % NP]
                        num, b_num = pnum[h % 4]
                        sl = va_slot(vt)
                        S.pe(lambda e: e.matmul(num[:, a:b], lhsT=VA[:, sl, h, :], rhs=p_t[:, a:b],
                                                start=(ci == 0), stop=(ci == nch - 1)),
                             reads=[b_pt, b_VA[sl]], writes=[b_num])

                    def c4(n):
                        G, h, ci, nch, kcol, vt, a, b, bias, qm = items[n]
                        if ci != nch - 1:
                            return
                        cp, po = h // 2, (h % 2) * 64
                        num, b_num = pnum[h % 4]
                        r_t, b_r = rden[h % 4]
                        y_t, b_y = yat[G % 2]
                        S.act(lambda e: e.activation(out=r_t[0:64, :], in_=num[64:128, :], func=AF.Ln),
                              reads=[b_num], writes=[b_r])
                        S.act(lambda e: e.activation(out=r_t[0:64, :], in_=r_t[0:64, :], func=AF.Exp, scale=-1.0),
                              reads=[b_r], writes=[b_r])
                        S.dve(lambda e: e.tensor_tensor(out=y_t[po:po + 64, cp, :], in0=num[0:64, :],
                                                        in1=r_t[0:64, :], op=ALU.mult),
                              reads=[b_num, b_r], writes=[b_y])
                        if h == 7:
                            S.dma("sync", yav[:, :, G * 512:(G + 1) * 512], y_t[:], reads=[b_y])

                    def batched(fn):
                        return lambda bi: [fn(2 * bi + k) for k in range(2)]

                    pipeline(len(items) // 2, [batched(c0), batched(c1), batched(c2), batched(c3), batched(c4)])
                    S.flush()

        if "C" in phases:
            with ExitStack() as ph:
                wg, b_wg = sb(ph, "wg", [128, 8, 2048], BF16)
                wfo, b_wfo = sb(ph, "wfo", [128, 4, 1024], BF16)
                wao, b_wao = sb(ph, "wao", [128, 4, 1024], BF16)
                wo, b_wo = sb(ph, "wo", [128, 8, 1024], BF16)
                hxg = [sb(ph, "hxg%d" % i, [128, 8, 512], BF16) for i in range(2)]
                ytg = [sb(ph, "ytg%d" % i, [128, 4, 512], BF16) for i in range(2)]
                yag = [sb(ph, "yag%d" % i, [128, 4, 512], BF16) for i in range(2)]
                mg = [sb(ph, "mg%d" % i, [128, 8, 512], BF16) for i in range(2)]
                xo = [sb(ph, "xo%d" % i, [128, 1024], F32) for i in range(8)]
                sgf = [sb(ph, "sgf%d" % i, [128, 512], F32) for i in range(2)]
                sga = [sb(ph, "sga%d" % i, [128, 512], F32) for i in range(2)]
                tf = [sb(ph, "tf%d" % i, [128, 512], F32) for i in range(2)]
                ta = [sb(ph, "ta%d" % i, [128, 512], F32) for i in range(2)]
                tm = [sb(ph, "tm%d" % i, [128, 512], F32) for i in range(2)]
                pA = [ps(ph, "pA%d" % i, [128, 512]) for i in range(2)]
                pB = [ps(ph, "pB%d" % i, [128, 512]) for i in range(2)]
                pC, b_pC = ps(ph, "pC", [128, 512])
                pD, b_pD = ps(ph, "pD", [128, 512])
                pE = [ps(ph, "pE%d" % i, [128, 512]) for i in range(2)]
                b_wgc = [Buf() for _ in range(8)]
                b_wfoc = [Buf() for _ in range(4)]
                b_waoc = [Buf() for _ in range(4)]
                wgv = w_gate.rearrange("(k p) n -> p k n", p=128)
                wfov = w_fo.rearrange("(k p) n -> p k n", p=128)
                waov = w_ao.rearrange("(k p) n -> p k n", p=128)
                for i4 in range(4):
                    for blk in (i4, 4 + i4):
                        S.dma("gpsimd", wg[:, :, blk * 256:(blk + 1) * 256], wgv[:, :, blk * 256:(blk + 1) * 256],
                              writes=[b_wgc[blk]])
                    S.dma("gpsimd", wfo[:, :, i4 * 256:(i4 + 1) * 256], wfov[:, :, i4 * 256:(i4 + 1) * 256],
                          writes=[b_wfoc[i4]])
                    S.dma("gpsimd", wao[:, :, i4 * 256:(i4 + 1) * 256], waov[:, :, i4 * 256:(i4 + 1) * 256],
                          writes=[b_waoc[i4]])
                for kc in range(0, 8, 4):
                    S.dma("gpsimd", wo[:, kc:kc + 4, :], w_out.rearrange("(k p) n -> p k n", p=128)[:, kc:kc + 4, :],
                          writes=[b_wo])
                hxv = hx_scr.rearrange("k p n -> p k n")
                ytv = yt_scr.rearrange("c p n -> p c n")
                yav = ya_scr.rearrange("c p n -> p c n")

                def loads(G):
                    i = G % 2
                    sl = slice(G * 512, (G + 1) * 512)
                    S.dma("sync", hxg[i][0][:], hxv[:, :, sl], writes=[hxg[i][1]])
                    S.dma("sync", ytg[i][0][:], ytv[:, :, sl], writes=[ytg[i][1]])
                    S.dma("sync", yag[i][0][:], yav[:, :, sl], writes=[yag[i][1]])
                    for tt in range(4):
                        x_o, b_xo = xo[(G * 4 + tt) % 8]
                        tok = G * 512 + tt * 128
                        S.dma("sync", x_o[:], xl[256 + tok:256 + tok + 128, :], writes=[b_xo])

                loads(0)
                for G in range(8):
                    i = G % 2
                    if G + 1 < 8:
                        loads(G + 1)
                    hx_t, b_hx = hxg[i]
                    yt_t, b_yt = ytg[i]
                    ya_t, b_ya = yag[i]
                    m_t, b_m = mg[i]
                    for j in range(8):
                        k = j % 2
                        p_a, b_pa = pA[k]
                        p_b, b_pb = pB[k]
                        for kc in range(8):
                            S.pe(lambda e, kc=kc, j=j, p_a=p_a, hx_t=hx_t: e.matmul(
                                p_a[:, :], lhsT=wg[:, kc, j * 128:(j + 1) * 128], rhs=hx_t[:, kc, :],
                                start=(kc == 0), stop=(kc == 7)), reads=[b_wgc[j // 2], b_hx], writes=[b_pa])
                        for kc in range(8):
                            S.pe(lambda e, kc=kc, j=j, p_b=p_b, hx_t=hx_t: e.matmul(
                                p_b[:, :], lhsT=wg[:, kc, 1024 + j * 128:1024 + (j + 1) * 128], rhs=hx_t[:, kc, :],
                                start=(kc == 0), stop=(kc == 7)), reads=[b_wgc[4 + j // 2], b_hx], writes=[b_pb])
                        for kc in range(4):
                            S.pe(lambda e, kc=kc, j=j, yt_t=yt_t: e.matmul(
                                pC[:, :], lhsT=wfo[:, kc, j * 128:(j + 1) * 128], rhs=yt_t[:, kc, :],
                                start=(kc == 0), stop=(kc == 3)), reads=[b_wfoc[j // 2], b_yt], writes=[b_pC])
                        for kc in range(4):
                            S.pe(lambda e, kc=kc, j=j, ya_t=ya_t: e.matmul(
                                pD[:, :], lhsT=wao[:, kc, j * 128:(j + 1) * 128], rhs=ya_t[:, kc, :],
                                start=(kc == 0), stop=(kc == 3)), reads=[b_waoc[j // 2], b_ya], writes=[b_pD])
                        s_f, b_sf = sgf[k]
                        s_a, b_sa = sga[k]
                        t_f, b_tf = tf[k]
                        t_a, b_ta = ta[k]
                        S.act(lambda e, s_f=s_f, p_a=p_a: e.activation(out=s_f[:, :], in_=p_a[:, :], func=AF.Sigmoid),
                              reads=[b_pa], writes=[b_sf])
                        S.act(lambda e, s_a=s_a, p_b=p_b: e.activation(out=s_a[:, :], in_=p_b[:, :], func=AF.Sigmoid),
                              reads=[b_pb], writes=[b_sa])
                        S.dve(lambda e, t_f=t_f, s_f=s_f: e.tensor_tensor(out=t_f[:, :], in0=pC[:, :], in1=s_f[:, :],
                                                                         op=ALU.mult),
                              reads=[b_pC, b_sf], writes=[b_tf])
                        S.dve(lambda e, t_a=t_a, s_a=s_a: e.tensor_tensor(out=t_a[:, :], in0=pD[:, :], in1=s_a[:, :],
                                                                         op=ALU.mult),
                              reads=[b_pD, b_sa], writes=[b_ta])
                        S.dve(lambda e, j=j, m_t=m_t, t_f=t_f, t_a=t_a: e.tensor_tensor(
                            out=m_t[:, j, :], in0=t_f[:, :], in1=t_a[:, :], op=ALU.add),
                            reads=[b_tf, b_ta], writes=[b_m])
                    for tt in range(4):
                        x_o, b_xo = xo[(G * 4 + tt) % 8]
                        tok = G * 512 + tt * 128
                        for hh in range(2):
                            p_e, b_pe = pE[hh]
                            t_m, b_tm = tm[hh]
                            for kc in range(8):
                                S.pe(lambda e, kc=kc, hh=hh, tt=tt, p_e=p_e, m_t=m_t: e.matmul(
                                    p_e[:, :], lhsT=m_t[:, kc, tt * 128:(tt + 1) * 128],
                                    rhs=wo[:, kc, hh * 512:(hh + 1) * 512], start=(kc == 0), stop=(kc == 7)),
                                    reads=[b_m, b_wo], writes=[b_pe])
                            S.dve(lambda e, hh=hh, p_e=p_e, t_m=t_m: e.tensor_tensor(
                                out=t_m[:, :], in0=p_e[:, :], in1=gb[:, 0, hh * 512:(hh + 1) * 512], op=ALU.mult),
                                reads=[b_pe, b_gb], writes=[b_tm])
                            S.dve(lambda e, hh=hh, t_m=t_m, x_o=x_o: e.tensor_tensor(
                                out=x_o[:, hh * 512:(hh + 1) * 512], in0=t_m[:, :],
                                in1=x_o[:, hh * 512:(hh + 1) * 512], op=ALU.add), reads=[b_tm, b_xo], writes=[b_xo])
                        S.dma("sync", x1_scr[tok:tok + 128, :], x_o[:], reads=[b_xo])
                S.flush()

        if "D" in phases:
            with ExitStack() as ph:
                w1, b_w1 = sb(ph, "w1", [128, 8, 4096], BF16)
                w2, b_w2 = sb(ph, "w2", [128, 32, 1024], BF16)
                xo = [sb(ph, "xoD%d" % i, [128, 1024], F32) for i in range(4)]
                hx2 = [sb(ph, "hx2%d" % i, [128, 8, 256], BF16) for i in range(2)]
                aT, b_aT = sb(ph, "aT", [128, 32, 256], BF16)
                rt = [sb(ph, "rt%d" % i, [128, 256], F32) for i in range(2)]
                tm = [sb(ph, "tmD%d" % i, [128, 512], F32) for i in range(2)]
                W = norm_work(ph, "D", 4)
                pT = [ps(ph, "pTD%d" % i, [128, 1024], BF16) for i in range(2)]
                pH = [ps(ph, "pH%d" % i, [128, 512]) for i in range(2)]
                pO = [ps(ph, "pO%d" % i, [128, 512]) for i in range(4)]
                b_w1c = [Buf() for _ in range(8)]
                b_w2c = [Buf() for _ in range(8)]
                w1v = w_mlp1.rearrange("(k p) n -> p k n", p=128)
                w2v = w_mlp2.rearrange("(k p) n -> p k n", p=128)
                for cb in range(8):
                    S.dma("gpsimd", w1[:, :, cb * 512:(cb + 1) * 512], w1v[:, :, cb * 512:(cb + 1) * 512],
                          writes=[b_w1c[cb]])
                for cb in range(8):
                    S.dma("gpsimd", w2[:, cb * 4:cb * 4 + 4, :], w2v[:, cb * 4:cb * 4 + 4, :], writes=[b_w2c[cb]])

                def xb(Gm, tt):
                    return xo[(Gm * 2 + tt) % 4]

                def wk(Gm, tt):
                    return W[(Gm * 2 + tt) % 4]

                def n_front(Gm):
                    for tt in range(2):
                        x_o, b_xo = xb(Gm, tt)
                        tok = Gm * 256 + tt * 128
                        S.dma("sync", x_o[:], x1_scr[tok:tok + 128, :], writes=[b_xo])
                    for tt in range(2):
                        nt_stats(xb(Gm, tt)[0], xb(Gm, tt)[1], wk(Gm, tt))
                    for tt in range(2):
                        nt_recip(wk(Gm, tt))
                    for tt in range(2):
                        nt_scale(xb(Gm, tt)[0], xb(Gm, tt)[1], wk(Gm, tt))

                def n_back(Gm):
                    h_t, b_h = hx2[Gm % 2]
                    for tt in range(2):
                        nt_tr(wk(Gm, tt), pT[tt][0], pT[tt][1])
                    for tt in range(2):
                        nt_mod(pT[tt][0], pT[tt][1], lambda j, tt=tt: h_t[:, j, tt * 128:(tt + 1) * 128], b_h, 16, 48)

                n_front(0)
                n_back(0)
                for Gm in range(16):
                    h_t, b_h = hx2[Gm % 2]
                    if Gm + 1 < 16:
                        n_front(Gm + 1)
                    for c in range(32):
                        p_h, b_ph = pH[c % 2]
                        r_t, b_rt = rt[c % 2]
                        for kc in range(8):
                            S.pe(lambda e, kc=kc, c=c, p_h=p_h, h_t=h_t: e.matmul(
                                p_h[:, 0:256], lhsT=w1[:, kc, c * 128:(c + 1) * 128], rhs=h_t[:, kc, :],
                                start=(kc == 0), stop=(kc == 7)), reads=[b_w1c[c // 4], b_h], writes=[b_ph])
                        S.act(lambda e, p_h=p_h, r_t=r_t: e.activation(out=r_t[:, :], in_=p_h[:, 0:256], func=AF.Relu),
                              reads=[b_ph], writes=[b_rt])
                        S.dve(lambda e, c=c, r_t=r_t: e.tensor_tensor(out=aT[:, c, :], in0=r_t[:, :], in1=r_t[:, :],
                                                                    op=ALU.mult), reads=[b_rt], writes=[b_aT])
                    if Gm + 1 < 16:
                        n_back(Gm + 1)
                    for tt in range(2):
                        x_o, b_xo = xb(Gm, tt)
                        tok = Gm * 256 + tt * 128
                        for hh in range(2):
                            p_o, b_po = pO[tt * 2 + hh]
                            t_m, b_tm = tm[hh]
                            for c in range(32):
                                S.pe(lambda e, c=c, hh=hh, tt=tt, p_o=p_o: e.matmul(
                                    p_o[:, :], lhsT=aT[:, c, tt * 128:(tt + 1) * 128],
                                    rhs=w2[:, c, hh * 512:(hh + 1) * 512], start=(c == 0), stop=(c == 31)),
                                    reads=[b_aT, b_w2c[c // 4]], writes=[b_po])
                            S.dve(lambda e, hh=hh, p_o=p_o, t_m=t_m: e.tensor_tensor(
                                out=t_m[:, :], in0=p_o[:, :], in1=gb[:, 1, hh * 512:(hh + 1) * 512], op=ALU.mult),
                                reads=[b_po, b_gb], writes=[b_tm])
                            S.dve(lambda e, hh=hh, t_m=t_m, x_o=x_o: e.tensor_tensor(
                                out=x_o[:, hh * 512:(hh + 1) * 512], in0=t_m[:, :],
                                in1=x_o[:, hh * 512:(hh + 1) * 512], op=ALU.add), reads=[b_tm, b_xo], writes=[b_xo])
                        S.dma("sync", out[tok:tok + 128, :], x_o[:], reads=[b_xo])
                S.flush()
        S.flush()
    return nc


def _bf(a):
    return np.asarray(a, np.float32).astype(ml_dtypes.bfloat16)


def _consts():
    n = np.arange(128)
    ang = 2 * np.pi * ((n[:, None] * n[None, :]) % 128) / 128.0
    C, Sn = np.cos(ang), np.sin(ang)
    cs128 = np.concatenate([C, Sn], 1)
    ccsc = np.concatenate([C, Sn, -Sn, C], 1)
    ident = np.eye(128)
    return _bf(cs128), _bf(ccsc), _bf(ident)


def _t3(half):
    n2 = np.arange(128) % 64
    k1 = np.arange(128)
    k2 = np.arange(32) + 32 * half
    k = k1[None, :, None] + 128 * k2[None, None, :]
    ang = 2 * np.pi * ((n2[:, None, None] * k) % 8192) / 8192.0
    t = np.stack([np.cos(ang), -np.sin(ang)], 1) / 1024.0
    return _bf(t)


def _rope(half):
    inv = 10000.0 ** (-np.arange(0, 32, 2, dtype=np.float64) / 32.0)
    tab = np.zeros((38, 128, 128), np.float32)
    tab[0:2, :, 0:64] = 1.0
    p = np.arange(128)
    col = (p % 64).astype(np.float64)
    for lt in range(36):
        lr = 2 * lt + p // 64
        row = np.clip(64 * half - 4 + lr, 0, 127).astype(np.float64)
        ar = row[:, None] * inv[None, :]
        ac = col[:, None] * inv[None, :]
        cos = np.concatenate([np.cos(ar), np.cos(ar), np.cos(ac), np.cos(ac)], 1)
        sin = np.concatenate([-np.sin(ar), np.sin(ar), -np.sin(ac), np.sin(ac)], 1)
        tab[2 + lt, :, 0:64] = cos
        tab[2 + lt, :, 64:128] = sin
    return tab


def _bias_tables(rpb):
    kc = np.arange(64)
    c = np.arange(64)
    cs = np.clip(c - 8, 0, 48)
    colv = (kc[:, None] >= cs[None, :]) & (kc[:, None] < cs[None, :] + 16)
    dc = np.clip(kc[:, None] - c[None, :] + 15, 0, 30)
    MT = np.full((2, 64, 8, 9, 64), NEG, np.float32)
    FT = np.full((2, 64, 8, 14, 64), NEG, np.float32)
    for par in range(2):
        for s in range(9):
            dr = (3 - s) + par
            if -4 <= dr <= 3:
                vals = rpb[:, dr + 7][:, dc]
                MT[par, :, :, s, :] = np.where(colv[:, None, :], vals.transpose(1, 0, 2), NEG)
        for s in range(14):
            dr = (6 - s) + par
            if -7 <= dr <= 7:
                vals = rpb[:, dr + 7][:, dc]
                FT[par, :, :, s, :] = np.where(colv[:, None, :], vals.transpose(1, 0, 2), NEG)
    return _bf(MT.reshape(128, 8, 576)), _bf(FT.reshape(128, 8, 896))


def _qmask(half):
    q = np.zeros((4, 512), np.float32)
    if half == 0:
        q[0, 0:256] = NEG
        q[3, :] = NEG
    else:
        q[1, :] = NEG
        q[2, 256:512] = NEG
    return _bf(q.reshape(1, 2048))


def _pj(v, nj):
    return np.ascontiguousarray(np.asarray(v, np.float32).reshape(nj, 128).T)


def make_in_maps(x, c, ctx, c_ctx, w_ada, b_ada, norm1_g, norm2_g, w_in, q_norm_g, k_norm_g, rpb,
                 w_branch_gate, w_fourier_out, w_attn_out, w_out, w_mlp1, w_mlp2):
    f = lambda a: np.ascontiguousarray(np.asarray(a, np.float32))
    x, c, ctx, c_ctx = f(x), f(c), f(ctx), f(c_ctx)
    cs128, ccsc, ident = _consts()
    MT, FT = _bias_tables(f(rpb)[0])
    b = f(b_ada)[0]
    bT = _pj(b, 48)
    bgt = np.ascontiguousarray(np.broadcast_to(np.stack([b[2048:3072], b[5120:6144]], 0)[None], (128, 2, 1024)))
    ngt = np.concatenate([_pj(f(norm1_g)[0], 8), _pj(f(norm2_g)[0], 8)], 1)
    gq = np.tile(f(q_norm_g)[0], 8)
    gk = np.tile(f(k_norm_g)[0], 8)
    gqk = np.ascontiguousarray(np.broadcast_to(np.stack([gq, gk], 0)[None], (128, 2, 512)))
    shared = dict(w_ada=f(w_ada)[0], bT=bT, bg=bgt, ng=ngt, gqk=gqk, w_in=f(w_in)[0], w_gate=f(w_branch_gate)[0],
                  w_fo=f(w_fourier_out)[0], w_ao=f(w_attn_out)[0], w_out=f(w_out)[0], w_mlp1=f(w_mlp1)[0],
                  w_mlp2=f(w_mlp2)[0], MT=MT, FT=FT, ident=ident, cs128=cs128, ccsc=ccsc)
    per_half = [dict(rope=_rope(h), qmask=_qmask(h), t3=_t3(h)) for h in range(2)]
    maps = []
    for core in range(8):
        bi, half = core // 2, core % 2
        rows = np.clip(64 * half - 4 + np.arange(72), 0, 127)
        xl = np.ascontiguousarray(x[bi].reshape(128, 64, 1024)[rows].reshape(4608, 1024))
        ccl = np.ascontiguousarray(np.stack([_pj(c[bi], 8), _pj(c_ctx, 8)], 2))
        m = dict(shared)
        m.update(per_half[half])
        m.update(xf=x[bi], xl=xl, ctx=ctx[bi], cc=ccl)
        maps.append(m)
    return maps


_NC = {}


def kernel(**inputs):
    if "nc" not in _NC:
        _NC["nc"] = build()
    maps = make_in_maps(**inputs)
    res = run_bass_kernel_spmd(_NC["nc"], maps, core_ids=list(range(8)))
    outp = np.empty((4, 8192, 1024), np.float32)
    for core in range(8):
        bi, half = core // 2, core % 2
        outp[bi, half * 4096:(half + 1) * 4096] = res.results[core]["out"]
    return outp
```

```python
import numpy as np
import ml_dtypes
from contextlib import ExitStack
import concourse.bass as bass
import concourse.mybir as mybir
from concourse.bass_utils import run_bass_kernel_spmd

F32 = mybir.dt.float32
BF16 = mybir.dt.bfloat16
AF = mybir.ActivationFunctionType
ALU = mybir.AluOpType
AX = mybir.AxisListType
NEG = -30000.0
EPS = 1e-6


class Buf:
    __slots__ = ("lw", "rd")

    def __init__(self):
        self.lw = None
        self.rd = {}


class Op:
    __slots__ = ("eng", "fn", "deps", "dma", "sem", "val", "signal")


CENG = ["gpsimd", "scalar", "vector", "tensor"]
ENGS = ["sync"] + CENG


class Sched:
    def __init__(self, nc, stack, ndma=40):
        self.nc = nc
        self.csem = {e: stack.enter_context(nc.semaphore("c_" + e)) for e in CENG}
        self.ccount = {e: 0 for e in CENG}
        self.dsem = {q: [stack.enter_context(nc.semaphore("d_%s%d" % (q, i))) for i in range(ndma)]
                     for q in ("sync", "gpsimd")}
        self.dcount = {q: [0] * ndma for q in self.dsem}
        self.dnext = {q: 0 for q in self.dsem}
        self.dlast = {q: [None] * ndma for q in self.dsem}
        self.ndma = ndma
        self.ops = []
        self.waited = {e: {} for e in ENGS}
        self.semobj = {}

    def add(self, eng, fn, reads=(), writes=(), dma=False):
        op = Op()
        op.eng, op.fn, op.dma, op.deps, op.signal = eng, fn, dma, {}, dma
        op.sem = op.val = None
        for b in reads:
            if b.lw is not None:
                op.deps[b.lw] = True
        for b in writes:
            if b.lw is not None and b.lw not in op.deps:
                op.deps[b.lw] = False
            for r in b.rd.values():
                if r is not op and r not in op.deps:
                    op.deps[r] = False
        key = id(op) if dma else eng
        for b in reads:
            b.rd[key] = op
        for b in writes:
            b.lw = op
            b.rd = {}
        self.ops.append(op)
        return op

    def dma(self, q, out, in_, reads=(), writes=()):
        return self.add(q, lambda e: e.dma_start(out=out, in_=in_), reads, writes, dma=True)

    def pe(self, fn, reads=(), writes=()):
        return self.add("tensor", fn, reads, writes)

    def act(self, fn, reads=(), writes=()):
        return self.add("scalar", fn, reads, writes)

    def dve(self, fn, reads=(), writes=()):
        return self.add("vector", fn, reads, writes)

    def pool(self, fn, reads=(), writes=()):
        return self.add("gpsimd", fn, reads, writes)

    def flush(self):
        ops = self.ops
        self.ops = []
        if not ops:
            return
        need = {}
        last = {}
        cur = set(ops)
        for op in ops:
            nl = []
            for d, raw in op.deps.items():
                if d not in cur:
                    continue
                if d.dma or d.eng != op.eng or (raw and not op.dma) or op.dma:
                    if not d.dma:
                        d.signal = True
                    nl.append(d)
            need[op] = nl
            if not op.dma:
                last[op.eng] = op
        for op in last.values():
            op.signal = True
        for op in ops:
            if op.dma:
                q = op.eng
                s = self.dnext[q]
                self.dnext[q] = (s + 1) % self.ndma
                prev = self.dlast[q][s]
                if prev is not None:
                    need[op].append(prev)
                self.dcount[q][s] += 16
                op.sem, op.val = self.dsem[q][s], self.dcount[q][s]
                self.dlast[q][s] = op
            elif op.signal:
                self.ccount[op.eng] += 1
                op.sem, op.val = self.csem[op.eng], self.ccount[op.eng]
        streams = {e: [] for e in ENGS}
        for op in ops:
            streams[op.eng].append(op)
        finals = []
        for e in CENG:
            if self.ccount[e] > 0:
                finals.append((self.csem[e], self.ccount[e]))
        for q in self.dsem:
            for i in range(self.ndma):
                if self.dcount[q][i] > 0:
                    finals.append((self.dsem[q][i], self.dcount[q][i]))

        def run(ename):
            def body(e):
                wd = self.waited[ename]
                for op in streams[ename]:
                    for d in need[op]:
                        k = id(d.sem)
                        if wd.get(k, 0) < d.val:
                            e.wait_ge(d.sem, d.val)
                            wd[k] = d.val
                    ins = op.fn(e)
                    if op.signal:
                        ins.then_inc(op.sem, 16 if op.dma else 1)
                for sem, val in finals:
                    k = id(sem)
                    if wd.get(k, 0) < val:
                        e.wait_ge(sem, val)
                        wd[k] = val
            return body

        with self.nc.Block() as block:
            block.sync(run("sync"))
            block.gpsimd(run("gpsimd"))
            block.scalar(run("scalar"))
            block.vector(run("vector"))
            block.tensor(run("tensor"))


def mkap(t, offset, dims):
    base = t[:]
    p = base.ap[0]
    return bass.AP(tensor=base.tensor, offset=offset, ap=[[p[0], p[1]]] + [list(d) for d in dims])


def build(debug=False, phases="MABCD"):
    nc = bass.Bass("TRN2", target_bir_lowering=False)

    def din(name, shape, dt=F32):
        return nc.dram_tensor(name, list(shape), dt, kind="ExternalInput").ap()

    xf = din("xf", [8192, 1024])
    xl = din("xl", [4608, 1024])
    ctx = din("ctx", [256, 1024])
    cc = din("cc", [128, 8, 2])
    w_ada = din("w_ada", [1024, 6144])
    bT = din("bT", [128, 48])
    bg = din("bg", [128, 2, 1024])
    ng = din("ng", [128, 16])
    gqk = din("gqk", [128, 2, 512])
    w_in = din("w_in", [1024, 2048])
    w_gate = din("w_gate", [1024, 2048])
    w_fo = din("w_fo", [512, 1024])
    w_ao = din("w_ao", [512, 1024])
    w_out = din("w_out", [1024, 1024])
    w_mlp1 = din("w_mlp1", [1024, 4096])
    w_mlp2 = din("w_mlp2", [4096, 1024])
    rope = din("rope", [38, 128, 128])
    MTd = din("MT", [128, 8, 576], BF16)
    FTd = din("FT", [128, 8, 896], BF16)
    qmd = din("qmask", [1, 2048], BF16)
    identd = din("ident", [128, 128], BF16)
    cs128d = din("cs128", [128, 256], BF16)
    ccscd = din("ccsc", [128, 512], BF16)
    t3d = din("t3", [128, 2, 128, 32], BF16)
    out = nc.dram_tensor("out", [4096, 1024], F32, kind="ExternalOutput").ap()
    skind = "ExternalOutput" if debug else "Internal"
    yt_scr = nc.dram_tensor("yt_scr", [4, 128, 4096], BF16, kind=skind).ap()
    ya_scr = nc.dram_tensor("ya_scr", [4, 128, 4096], BF16, kind=skind).ap()
    hx_scr = nc.dram_tensor("hx_scr", [8, 128, 4096], BF16, kind=skind).ap()
    x1_scr = nc.dram_tensor("x1_scr", [4096, 1024], F32, kind=skind).ap()
    if debug:
        dbg = nc.dram_tensor("dbg", [128, 2048], F32, kind="ExternalOutput").ap()

    with ExitStack() as gst:
        GA = gst.enter_context
        S = Sched(nc, gst)

        def sb(st, name, shape, dt):
            return st.enter_context(nc.sbuf_tensor(name, list(shape), dt)), Buf()

        def ps(st, name, shape, dt=F32):
            return st.enter_context(nc.psum_tensor(name, list(shape), dt)), Buf()

        mod, b_mod = sb(gst, "mod", [128, 48, 2], F32)
        sce, b_sce = sb(gst, "sce", [128, 2, 8, 2], F32)
        gb, b_gb = sb(gst, "gb", [128, 2, 1024], F32)
        idt, b_idt = sb(gst, "idt", [128, 128], BF16)
        ones, b_ones = sb(gst, "ones", [128, 128], BF16)
        epsb, b_eps = sb(gst, "epsb", [128, 1], F32)

        S.pool(lambda e: e.memset(epsb[:], EPS), writes=[b_eps])
        S.pool(lambda e: e.memset(ones[:], 1.0), writes=[b_ones])
        S.dma("sync", idt[:], identd, writes=[b_idt])

        def pipeline(n, stages, hooks=None):
            d = len(stages)
            for step in range(n + d - 1):
                if hooks and step in hooks:
                    hooks[step]()
                for si in range(d - 1, -1, -1):
                    t = step - si
                    if 0 <= t < n:
                        stages[si](t)

        def nt_stats(xt, b_xt, W):
            S.act(lambda e: e.activation(out=W["junk"][:], in_=xt[:], func=AF.Square, accum_out=W["st"][:, 0:1]),
                  reads=[b_xt], writes=[W["b_junk"], W["b_st"]])
            S.act(lambda e: e.activation(out=W["st"][:, 1:2], in_=W["st"][:, 0:1], func=AF.Sqrt, scale=1.0 / 1024,
                                         bias=epsb[:, 0:1]), reads=[W["b_st"], b_eps], writes=[W["b_st"]])

        def nt_recip(W):
            S.dve(lambda e: e.reciprocal(out=W["st"][:, 2:3], in_=W["st"][:, 1:2]), reads=[W["b_st"]],
                  writes=[W["b_st"]])

        def nt_scale(xt, b_xt, W):
            S.act(lambda e: e.activation(out=W["xn"][:], in_=xt[:], func=AF.Copy, scale=W["st"][:, 2:3]),
                  reads=[b_xt, W["b_st"]], writes=[W["b_xn"]])

        def nt_tr(W, pT, b_pT):
            for j in range(8):
                S.pe(lambda e, j=j: e.transpose(pT[:, j * 128:(j + 1) * 128], W["xn"][:, j * 128:(j + 1) * 128],
                                                idt[:]), reads=[W["b_xn"], b_idt], writes=[b_pT])

        def nt_mod(pT, b_pT, dst3, b_dst, sc_off, sh_off):
            for j in range(8):
                S.act(lambda e, j=j: e.activation(out=dst3(j), in_=pT[:, j * 128:(j + 1) * 128], func=AF.Identity,
                                                  scale=mkap(sce, sc_off + 2 * j, [[1, 1]]),
                                                  bias=mkap(mod, sh_off + 2 * j, [[1, 1]])),
                      reads=[b_pT, b_sce, b_mod], writes=[b_dst])

        def norm_work(st, tag, n):
            junk, b_junk = sb(st, "junk" + tag, [128, 1024], BF16)
            res = []
            for i in range(n):
                st_, b_st = sb(st, "st%s%d" % (tag, i), [128, 4], F32)
                xn, b_xn = sb(st, "xn%s%d" % (tag, i), [128, 1024], BF16)
                res.append(dict(junk=junk, b_junk=b_junk, st=st_, b_st=b_st, xn=xn, b_xn=b_xn))
            return res

        if "A" in phases:
            with ExitStack() as ph:
                cc_sb, b_cc = sb(ph, "cc_sb", [128, 8, 2], F32)
                scb, b_scb = sb(ph, "scb", [128, 8, 2], BF16)
                screp, b_screp = sb(ph, "screp", [128, 8, 128], BF16)
                bT_sb, b_bT = sb(ph, "bT_sb", [128, 48], F32)
                bg_sb, b_bg = sb(ph, "bg_sb", [128, 2, 1024], F32)
                ng_sb, b_ng = sb(ph, "ng_sb", [128, 16], F32)
                wsl = [sb(ph, "wsl%d" % i, [128, 8, 512], BF16) for i in range(2)]
                w_inf, b_winf = sb(ph, "w_inf", [128, 8, 512], BF16)
                u_all = ph.enter_context(nc.sbuf_tensor("u_all", [128, 64, 512], BF16))
                b_u = [Buf() for _ in range(64)]
                Bg, b_Bg = sb(ph, "Bg", [128, 2, 64, 128], BF16)
                YTg = [sb(ph, "YTg%d" % i, [128, 4096], BF16) for i in range(1)]
                Zt = [sb(ph, "Zt%d" % i, [128, 512], BF16) for i in range(3)]
                t3_sb, b_t3 = sb(ph, "t3_sb", [128, 2, 128, 32], BF16)
                cs_sb, b_cs = sb(ph, "cs_sb", [128, 256], BF16)
                ccsc_sb, b_ccsc = sb(ph, "ccsc_sb", [128, 512], BF16)
                xt = [sb(ph, "xtA%d" % i, [128, 1024], F32) for i in range(3)]
                hxT = [sb(ph, "hxTA%d" % i, [128, 8, 128], BF16) for i in range(2)]
                W = norm_work(ph, "A", 3)
                pT = [ps(ph, "pTA%d" % i, [128, 1024], BF16) for i in range(2)]
                pu = [ps(ph, "puA%d" % i, [128, 512]) for i in range(2)]
                pa = [ps(ph, "paA%d" % i, [128, 512]) for i in range(2)]
                pb = [ps(ph, "pbA%d" % i, [128, 512]) for i in range(2)]
                wv = w_in.rearrange("(k p) n -> p k n", p=128)
                for kc in range(8):
                    S.dma("gpsimd", w_inf[:, kc, :], wv[:, kc, 0:512], writes=[b_winf])
                S.dma("sync", t3_sb[:], t3d, writes=[b_t3])
                S.dma("sync", cs_sb[:], cs128d, writes=[b_cs])
                S.dma("sync", ccsc_sb[:], ccscd, writes=[b_ccsc])
                xsrc = xf.rearrange("(n1 n2) d -> n2 n1 d", n2=64)
                pmod, b_pmod = pb[0]
                pg = [pa[0], pa[1]]
                S.dma("sync", cc_sb[:], cc, writes=[b_cc])
                S.dma("sync", bT_sb[:], bT, writes=[b_bT])
                S.dma("sync", bg_sb[:], bg, writes=[b_bg])
                S.dma("sync", ng_sb[:], ng, writes=[b_ng])
                S.act(lambda e: e.activation(out=scb[:], in_=cc_sb[:], func=AF.Silu), reads=[b_cc], writes=[b_scb])
                S.act(lambda e: e.activation(out=screp[:], in_=cc_sb[:, :, 0:1].broadcast_to([128, 8, 128]),
                                             func=AF.Silu), reads=[b_cc], writes=[b_screp])
                wav = w_ada.rearrange("(k p) n -> p k n", p=128)

                def m_dma(hs):
                    w, b_w = wsl[hs % 2]
                    S.dma("gpsimd", w[:], wav[:, :, hs * 512:(hs + 1) * 512], writes=[b_w])

                def m_proc(hs):
                    w, b_w = wsl[hs % 2]
                    v, hh = hs // 2, hs % 2
                    for j4 in range(4):
                        col = (v * 8 + hh * 4 + j4) * 2
                        for kc in range(8):
                            S.pe(lambda e, j4=j4, kc=kc, col=col: e.matmul(
                                pmod[:, col:col + 2], lhsT=w[:, kc, j4 * 128:(j4 + 1) * 128], rhs=scb[:, kc, :],
                                start=(kc == 0), stop=(kc == 7)), reads=[b_w, b_scb], writes=[b_pmod])
                    if v in (2, 5):
                        gi = 0 if v == 2 else 1
                        pgt, b_pg = pg[hh]
                        for kc in range(8):
                            S.pe(lambda e, kc=kc: e.matmul(pgt[:, :], lhsT=screp[:, kc, :], rhs=w[:, kc, :],
                                                           start=(kc == 0), stop=(kc == 7)),
                                 reads=[b_w, b_screp], writes=[b_pg])
                        S.dve(lambda e: e.tensor_tensor(out=gb[:, gi, hh * 512:(hh + 1) * 512], in0=pgt[:, :],
                                                        in1=bg_sb[:, gi, hh * 512:(hh + 1) * 512], op=ALU.add),
                              reads=[b_pg, b_bg], writes=[b_gb])
                    if hs + 2 < 12:
                        m_dma(hs + 2)
                    if hh == 1:
                        S.dve(lambda e: e.tensor_tensor(
                            out=mod[:, v * 8:(v + 1) * 8, :],
                            in0=pmod[:, v * 16:(v + 1) * 16].rearrange("p (a b) -> p a b", b=2),
                            in1=bT_sb[:, v * 8:(v + 1) * 8].unsqueeze(2).broadcast_to([128, 8, 2]), op=ALU.add),
                            reads=[b_pmod, b_bT], writes=[b_mod])
                        if v in (1, 4):
                            wi = 0 if v == 1 else 1
                            S.dve(lambda e: e.scalar_tensor_tensor(
                                out=sce[:, wi], in0=mod[:, v * 8:(v + 1) * 8, :], scalar=1.0,
                                in1=ng_sb[:, wi * 8:(wi + 1) * 8].unsqueeze(2).broadcast_to([128, 8, 2]),
                                op0=ALU.add, op1=ALU.mult), reads=[b_mod, b_ng], writes=[b_sce])

                m_dma(0)
                m_dma(1)
                for hs in range(4):
                    m_proc(hs)
                sh_rep, b_shrep = sb(ph, "sh_rep", [128, 8, 128], BF16)
                shw, b_shw = sb(ph, "shw", [128, 512], F32)
                S.act(lambda e: e.copy(out=sh_rep[:], in_=mkap(mod, 0, [[2, 8], [0, 128]])), reads=[b_mod],
                      writes=[b_shrep])
                for kc in range(8):
                    S.pe(lambda e, kc=kc: e.matmul(pu[0][0][:, :], lhsT=sh_rep[:, kc, :], rhs=w_inf[:, kc, :],
                                                   start=(kc == 0), stop=(kc == 7)),
                         reads=[b_shrep, b_winf], writes=[pu[0][1]])
                S.dve(lambda e: e.tensor_copy(out=shw[:, :], in_=pu[0][0][:, :]), reads=[pu[0][1]], writes=[b_shw])

                def a0(t):
                    S.dma("sync", xt[t % 3][0][:], xsrc[t], writes=[xt[t % 3][1]])

                def a1(t):
                    nt_stats(xt[t % 3][0], xt[t % 3][1], W[t % 3])

                def a2(t):
                    nt_recip(W[t % 3])

                def a3(t):
                    nt_scale(xt[t % 3][0], xt[t % 3][1], W[t % 3])

                def a4(t):
                    nt_tr(W[t % 3], pT[t % 2][0], pT[t % 2][1])

                def a5(t):
                    h_t, b_h = hxT[t % 2]
                    p_T, b_pT = pT[t % 2]
                    S.dve(lambda e: e.tensor_tensor(out=h_t[:], in0=p_T[:, :].rearrange("p (a b) -> p a b", b=128),
                                                    in1=mkap(sce, 0, [[2, 8], [0, 128]]), op=ALU.mult),
                          reads=[b_pT, b_sce], writes=[b_h])

                def a6(t):
                    h_t, b_h = hxT[t % 2]
                    p_u, b_pu = pu[t % 2]
                    for kc in range(8):
                        S.pe(lambda e, kc=kc: e.matmul(p_u[:, :], lhsT=h_t[:, kc, :], rhs=w_inf[:, kc, :],
                                                       start=(kc == 0), stop=(kc == 7)),
                             reads=[b_h, b_winf], writes=[b_pu])

                def a7(t):
                    p_u, b_pu = pu[t % 2]
                    S.dve(lambda e: e.tensor_tensor(out=u_all[:, t, :], in0=p_u[:, :], in1=shw[:, :], op=ALU.add),
                          reads=[b_pu, b_shw], writes=[b_u[t]])

                pipeline(64, [a0, a1, a2, a3, a4, a5, a6, a7], hooks={8 * (hs - 3): (lambda hs=hs: m_proc(hs))
                                                                    for hs in range(4, 12)})

                for g in range(4):
                    def s1mm(pr, g=g):
                        p_a, b_pa = pa[pr % 2]
                        for q in range(2):
                            n2 = 2 * pr + q
                            S.pe(lambda e, n2=n2, q=q: e.matmul(
                                p_a[:, q * 256:(q + 1) * 256], lhsT=u_all[:, n2, g * 128:(g + 1) * 128],
                                rhs=cs_sb[:, :], start=True, stop=True), reads=[b_u[n2], b_cs], writes=[b_pa])

                    def s1ev(pr):
                        p_a, b_pa = pa[pr % 2]
                        for q in range(2):
                            n2 = 2 * pr + q
                            src = mkap(p_a, q * 256, [[128, 2], [2, 64], [1, 2]])
                            dst = mkap(Bg, 2 * n2, [[8192, 2], [128, 64], [1, 2]])
                            S.act(lambda e, src=src, dst=dst: e.copy(out=dst, in_=src),
                                  reads=[b_pa], writes=[b_Bg])

                    pipeline(32, [s1mm, s1ev])
                    y_t, b_y = YTg[0]

                    def s2mm(j):
                        p_z, b_pz = pb[j % 2]
                        lr = mkap(Bg, j * 128, [[1, 128]])
                        ls = mkap(Bg, 8192 + j * 128, [[1, 128]])
                        S.pe(lambda e: e.matmul(p_z[:, 0:256], lhsT=lr, rhs=ccsc_sb[:, 0:256], start=True, stop=False),
                             reads=[b_Bg, b_ccsc], writes=[b_pz])
                        S.pe(lambda e: e.matmul(p_z[:, 0:256], lhsT=ls, rhs=ccsc_sb[:, 256:512], start=False,
                                                stop=True), reads=[b_Bg, b_ccsc], writes=[b_pz])

                    def s2ev(j):
                        p_z, b_pz = pb[j % 2]
                        z_t, b_z = Zt[j % 3]
                        S.dve(lambda e: e.tensor_copy(out=z_t[:, 0:256], in_=p_z[:, 0:256]), reads=[b_pz],
                              writes=[b_z])

                    def s3mm(j, y_t=y_t, b_y=b_y):
                        z_t, b_z = Zt[j % 3]
                        for lo in range(2):
                            k1 = 2 * j + lo
                            col = (k1 % 16) * 32
                            p_y, b_py = pu[(k1 // 16) % 2]
                            S.pe(lambda e, k1=k1, col=col, p_y=p_y: e.matmul(
                                p_y[:, col:col + 32], lhsT=z_t[:, 0:128], rhs=t3_sb[:, 0, k1, :],
                                start=True, stop=False), reads=[b_z, b_t3], writes=[b_py])
                            S.pe(lambda e, k1=k1, col=col, p_y=p_y: e.matmul(
                                p_y[:, col:col + 32], lhsT=z_t[:, 128:256], rhs=t3_sb[:, 1, k1, :],
                                start=False, stop=True), reads=[b_z, b_t3], writes=[b_py])
                            if k1 % 16 == 15:
                                dst = mkap(y_t, 16 * (k1 // 16), [[1, 16], [128, 32]])
                                S.act(lambda e, dst=dst, p_y=p_y: e.copy(
                                    out=dst, in_=p_y[:, :].rearrange("p (a b) -> p a b", b=32)),
                                    reads=[b_py], writes=[b_y])

                    pipeline(64, [s2mm, s2ev, s3mm])
                    S.dma("sync", yt_scr[g], y_t[:], reads=[b_y])
                S.flush()

        if "B" in phases:
            with ExitStack() as ph:
                KT = ph.enter_context(nc.sbuf_tensor("KT", [128, 4, 4864], BF16))
                b_KT = [Buf() for _ in range(38)]
                V = ph.enter_context(nc.sbuf_tensor("V", [128, 38, 512], BF16))
                b_V = [Buf() for _ in range(38)]
                qT = ph.enter_context(nc.sbuf_tensor("qT", [128, 4, 4096], BF16))
                b_qT = [Buf() for _ in range(32)]
                with ExitStack() as p1:
                    w_qkv, b_wqkv = sb(p1, "w_qkv", [128, 8, 1536], BF16)
                    gqk_sb, b_gqk = sb(p1, "gqk_sb", [128, 2, 512], F32)
                    xt = [sb(p1, "xtB%d" % i, [128, 1024], F32) for i in range(3)]
                    rp = [sb(p1, "rpB%d" % i, [128, 128], F32) for i in range(4)]
                    hxT = [sb(p1, "hxTB%d" % i, [128, 8, 128], BF16) for i in range(3)]
                    W = norm_work(p1, "B", 3)
                    pT = [ps(p1, "pTB%d" % i, [128, 1024], BF16) for i in range(1)]
                    pq = [[ps(p1, "pqB%d_%d" % (i, c), [128, 512]) for c in range(3)] for i in range(2)]
                    ptr = [ps(p1, "ptrB%d" % i, [128, 1024], BF16) for i in range(1)]
                    NB = 2
                    tq = []
                    for i in range(NB):
                        d = {}
                        for nm, dt in ():
                            d[nm] = sb(p1, "%s%d" % (nm, i), [128, 512], dt)
                        d["qr"] = sb(p1, "qr%d" % i, [128, 1024], BF16)
                        d["ss"] = sb(p1, "ss%d" % i, [128, 48], F32)
                        tq.append(d)
                    fr = [dict(qf=sb(p1, "qf%d" % i, [128, 512], F32), kf=sb(p1, "kf%d" % i, [128, 512], F32))
                          for i in range(3)]
                    t2r = dict(t2q=sb(p1, "t2q", [128, 512], F32), t2k=sb(p1, "t2k", [128, 512], F32))
                    t1r = [dict(t1q=sb(p1, "t1q%d" % i, [128, 512], F32), t1k=sb(p1, "t1k%d" % i, [128, 512], F32))
                           for i in range(3)]
                    wv = w_in.rearrange("(k p) n -> p k n", p=128)
                    for kc in range(8):
                        S.dma("gpsimd", w_qkv[:, kc, :], wv[:, kc, 512:2048], writes=[b_wqkv])
                    S.dma("sync", gqk_sb[:], gqk, writes=[b_gqk])
                    hxv = hx_scr.rearrange("k p n -> p k n")

                    def own(t):
                        return 2 <= t - 2 < 34

                    def b0(t):
                        src = ctx[t * 128:(t + 1) * 128, :] if t < 2 else xl[(t - 2) * 128:(t - 1) * 128, :]
                        S.dma("sync", xt[t % 3][0][:], src, writes=[xt[t % 3][1]])

                    def b1(t):
                        nt_stats(xt[t % 3][0], xt[t % 3][1], W[t % 3])

                    def b2(t):
                        nt_recip(W[t % 3])

                    def b3(t):
                        nt_scale(xt[t % 3][0], xt[t % 3][1], W[t % 3])

                    def b4(t):
                        nt_tr(W[t % 3], pT[0][0], pT[0][1])

                    def b5(t):
                        h_t, b_h = hxT[t % 3]
                        cond = 1 if t < 2 else 0
                        nt_mod(pT[0][0], pT[0][1], lambda j: h_t[:, j, :], b_h, cond, cond)

                    def b6(t):
                        h_t, b_h = hxT[t % 3]
                        if own(t):
                            o = (t - 4) * 128
                            S.dma("sync", hxv[:, :, o:o + 128], h_t[:], reads=[b_h])
                        for kc in range(8):
                            for c in range(3):
                                if c == 0 and not own(t):
                                    continue
                                p_q, b_pq = pq[t % 2][c]
                                S.pe(lambda e, kc=kc, c=c, p_q=p_q: e.matmul(
                                    p_q[:, :], lhsT=h_t[:, kc, :], rhs=w_qkv[:, kc, c * 512:(c + 1) * 512],
                                    start=(kc == 0), stop=(kc == 7)), reads=[b_h, b_wqkv], writes=[b_pq])

                    def b7(t):
                        d = tq[t % NB]
                        S.act(lambda e: e.copy(out=V[:, t, :], in_=pq[t % 2][2][0][:, :]),
                              reads=[pq[t % 2][2][1]], writes=[b_V[t]])
                        for w_, nm, sq in ((1, "kf", "t1k"), (0, "qf", "t1q")):
                            if w_ == 0 and not own(t):
                                continue
                            p_s, b_ps = pq[t % 2][w_]
                            f_t, b_f = fr[t % 3][nm]
                            s_t, b_s = t1r[t % 3][sq]
                            S.act(lambda e, f_t=f_t, p_s=p_s: e.copy(out=f_t[:, :], in_=p_s[:, :]),
                                  reads=[b_ps], writes=[b_f])
                            S.act(lambda e, s_t=s_t, p_s=p_s: e.activation(out=s_t[:, :], in_=p_s[:, :],
                                                                          func=AF.Square),
                                  reads=[b_ps], writes=[b_s])

                    def b8(t):
                        S.dma("sync", rp[t % 4][0][:], rope[t], writes=[rp[t % 4][1]])
                        d = tq[t % NB]
                        ss, b_ss = d["ss"]
                        for w_, sq in ((1, "t1k"), (0, "t1q")):
                            if w_ == 0 and not own(t):
                                continue
                            s_t, b_s = t1r[t % 3][sq]
                            S.dve(lambda e, w_=w_, s_t=s_t: e.tensor_reduce(
                                out=ss[:, w_ * 8:(w_ + 1) * 8], in_=s_t[:, :].rearrange("p (h d) -> p h d", d=64),
                                axis=AX.X, op=ALU.add), reads=[b_s], writes=[b_ss])

                    def b9(t):
                        d = tq[t % NB]
                        ss, b_ss = d["ss"]
                        lo_ = 0 if own(t) else 8
                        S.act(lambda e: e.activation(out=ss[:, 16 + lo_:32], in_=ss[:, lo_:16], func=AF.Sqrt,
                                                     scale=1.0 / 64, bias=epsb[:, 0:1]),
                              reads=[b_ss, b_eps], writes=[b_ss])

                    def b10(t):
                        d = tq[t % NB]
                        ss, b_ss = d["ss"]
                        rp_t, b_rp = rp[t % 4]
                        qr, b_qr = d["qr"]
                        lo_ = 0 if own(t) else 8
                        S.dve(lambda e: e.reciprocal(out=ss[:, 32 + lo_:48], in_=ss[:, 16 + lo_:32]), reads=[b_ss],
                              writes=[b_ss])
                        for w_, fn, nn, t1n, t2n in ((1, "kf", "kf", "t1k", "t2k"), (0, "qf", "qf", "t1q", "t2q")):
                            if w_ == 0 and not own(t):
                                continue
                            f_t, b_f = fr[t % 3][fn]
                            n_t, b_n = fr[t % 3][nn]
                            t1, b_t1 = t1r[t % 3][t1n]
                            t2, b_t2 = t2r[t2n]
                            S.dve(lambda e, w_=w_, f_t=f_t, n_t=n_t: e.scalar_tensor_tensor(
                                out=n_t[:, :].rearrange("p (h d) -> p h d", d=64),
                                in0=f_t[:, :].rearrange("p (h d) -> p h d", d=64),
                                scalar=(0.125 if w_ == 0 else 1.0),
                                in1=ss[:, 32 + w_ * 8:32 + (w_ + 1) * 8].unsqueeze(2).broadcast_to([128, 8, 64]),
                                op0=ALU.mult, op1=ALU.mult), reads=[b_f, b_ss], writes=[b_n])
                            S.dve(lambda e, w_=w_, n_t=n_t: e.tensor_tensor(out=n_t[:, :], in0=n_t[:, :],
                                                                         in1=gqk_sb[:, w_, :], op=ALU.mult),
                                  reads=[b_n, b_gqk], writes=[b_n])
                            S.dve(lambda e, n_t=n_t, t1=t1: e.tensor_tensor(
                                out=t1[:, :].rearrange("p (h d) -> p h d", d=64),
                                in0=n_t[:, :].rearrange("p (h d) -> p h d", d=64),
                                in1=rp_t[:, 0:64].unsqueeze(1).broadcast_to([128, 8, 64]), op=ALU.mult),
                                reads=[b_n, b_rp], writes=[b_t1])
                            for a in range(2):
                                o_ap = mkap(t2, a * 16, [[64, 8], [32, 2], [1, 16]])
                                i_ap = mkap(n_t, (1 - a) * 16, [[64, 8], [32, 2], [1, 16]])
                                s_ap = mkap(rp_t, 64 + a * 16, [[0, 8], [32, 2], [1, 16]])
                                S.dve(lambda e, o_ap=o_ap, i_ap=i_ap, s_ap=s_ap: e.tensor_tensor(
                                    out=o_ap, in0=i_ap, in1=s_ap, op=ALU.mult), reads=[b_n, b_rp], writes=[b_t2])
                            S.dve(lambda e, w_=w_, t1=t1, t2=t2: e.tensor_tensor(
                                out=qr[:, w_ * 512:(w_ + 1) * 512], in0=t1[:, :], in1=t2[:, :], op=ALU.add),
                                reads=[b_t1, b_t2], writes=[b_qr])

                    def b11(t):
                        d = tq[t % NB]
                        qr, b_qr = d["qr"]
                        p_t, b_pt = ptr[0]
                        for w_ in (1, 0):
                            if w_ == 0 and not own(t):
                                continue
                            for c in range(4):
                                S.pe(lambda e, c=c, w_=w_: e.transpose(
                                    p_t[:, w_ * 512 + c * 128:w_ * 512 + (c + 1) * 128],
                                    qr[:, w_ * 512 + c * 128:w_ * 512 + (c + 1) * 128], idt[:]),
                                    reads=[b_qr, b_idt], writes=[b_pt])

                    def b12(t):
                        p_t, b_pt = ptr[0]
                        S.act(lambda e: e.copy(out=KT[:, :, t * 128:(t + 1) * 128],
                                               in_=p_t[:, 512:1024].rearrange("p (a b) -> p a b", b=128)),
                              reads=[b_pt], writes=[b_KT[t]])
                        if own(t):
                            o = (t - 4) * 128
                            S.act(lambda e: e.copy(out=qT[:, :, o:o + 128],
                                                   in_=p_t[:, 0:512].rearrange("p (a b) -> p a b", b=128)),
                                  reads=[b_pt], writes=[b_qT[t - 4]])

                    pipeline(38, [b0, b1, b2, b3, b4, b5, b6, b7, b8, b9, b10, b11, b12])
                    S.flush()
                with ExitStack() as p2:
                    MT_sb, b_MT = sb(p2, "MT_sb", [128, 8, 576], BF16)
                    FT_sb, b_FT = sb(p2, "FT_sb", [128, 8, 896], BF16)
                    qm_sb, b_qm = sb(p2, "qm_sb", [1, 2048], BF16)
                    NP = 4
                    PT = [sb(p2, "PT%d" % i, [128, 512], BF16) for i in range(NP)]
                    rden = [sb(p2, "rden%d" % i, [128, 512], F32) for i in range(4)]
                    yat = [sb(p2, "yat%d" % i, [128, 4, 512], BF16) for i in range(2)]
                    pS = [ps(p2, "pS%d" % i, [128, 512]) for i in range(NP)]
                    pnum = [ps(p2, "pnum%d" % i, [128, 512]) for i in range(4)]
                    VA = p2.enter_context(nc.sbuf_tensor("VA", [128, 14, 8, 128], BF16))
                    b_VA = [Buf() for _ in range(14)]
                    S.pool(lambda e: e.memset(VA[:], 1.0), writes=b_VA)

                    def va_slot(vt):
                        return 12 + vt if vt < 2 else (vt - 2) % 12

                    def va_fill(vt):
                        sl = va_slot(vt)
                        S.act(lambda e: e.copy(out=VA[:, sl, :, 0:64],
                                               in_=V[:, vt, :].rearrange("p (h d) -> p h d", d=64)),
                              reads=[b_V[vt]], writes=[b_VA[sl]])
                    S.dma("sync", MT_sb[:], MTd, writes=[b_MT])
                    S.dma("sync", FT_sb[:], FTd, writes=[b_FT])
                    S.dma("sync", qm_sb[:], qmd, writes=[b_qm])
                    S.act(lambda e: e.activation(out=MT_sb[:], in_=MT_sb[:], func=AF.Exp), reads=[b_MT], writes=[b_MT])
                    S.act(lambda e: e.activation(out=FT_sb[:], in_=FT_sb[:], func=AF.Exp), reads=[b_FT], writes=[b_FT])
                    yav = ya_scr.rearrange("c p n -> p c n")
                    items = []

                    def head_chunks(G, h):
                        r0 = 4 + 8 * G
                        chunks = []
                        for c in range(2):
                            chunks.append((c * 128, c, 0, 512, None, None))
                        for m in range(8):
                            lrk = r0 - 4 + 2 * m
                            ia, ib = max(0, 2 * m - 7), min(7, 2 * m + 1)
                            a, b = ia * 64, (ib + 1) * 64
                            bias = MT_sb[:, h, (7 - 2 * m + ia) * 64:(7 - 2 * m + ib + 1) * 64]
                            qm = None
                            if G == 0:
                                qm = qm_sb[0:1, a:b]
                            elif G == 7:
                                qm = qm_sb[0:1, 1024 + a:1024 + b]
                            chunks.append((256 + lrk * 64, 2 + lrk // 2, a, b, bias, qm))
                        if G == 0:
                            for ee in range(4):
                                lrk = r0 + 2 * ee
                                chunks.append((256 + lrk * 64, 2 + lrk // 2, 0, 256,
                                               FT_sb[:, h, (6 - 2 * ee) * 64:(10 - 2 * ee) * 64],
                                               qm_sb[0:1, 512:768]))
                        if G == 7:
                            for ee in range(4):
                                lrk = r0 + 2 * ee
                                chunks.append((256 + lrk * 64, 2 + lrk // 2, 256, 512,
                                               FT_sb[:, h, (10 - 2 * ee) * 64:(14 - 2 * ee) * 64],
                                               qm_sb[0:1, 1536 + 256:1536 + 512]))
                        return chunks

                    for G in range(8):
                        for p in range(4):
                            chA = head_chunks(G, 2 * p)
                            chB = head_chunks(G, 2 * p + 1)
                            for ci in range(len(chA)):
                                items.append((G, 2 * p, ci, len(chA)) + chA[ci])
                                items.append((G, 2 * p + 1, ci, len(chB)) + chB[ci])

                    def c0(n):
                        G, h, ci, nch, kcol, vt, a, b, bias, qm = items[n]
                        if h == 0 and ci == 0:
                            if G == 0:
                                va_fill(0)
                                va_fill(1)
                            for lt in (range(0, 8) if G == 0 else range(4 * G + 4, 4 * G + 8)):
                                va_fill(2 + lt)
                        cp, po = h // 2, (h % 2) * 64
                        p_s, b_ps = pS[n % NP]
                        qb = b_qT[(G * 512 + a) // 128:(G * 512 + b + 127) // 128]
                        S.pe(lambda e: e.matmul(
                            p_s[:, a:b], lhsT=KT[po:po + 64, cp, kcol:kcol + 128],
                            rhs=qT[po:po + 64, cp, G * 512 + a:G * 512 + b], start=True, stop=(qm is None)),
                            reads=[b_KT[kcol // 128]] + qb, writes=[b_ps])
                        if qm is not None:
                            S.pe(lambda e: e.matmul(p_s[:, a:b], lhsT=ones[0:1, :], rhs=qm, start=False, stop=True),
                                 reads=[b_ones, b_qm], writes=[b_ps])

                    def c1(n):
                        G, h, ci, nch, kcol, vt, a, b, bias, qm = items[n]
                        p_s, b_ps = pS[n % NP]
                        p_t, b_pt = PT[n % NP]
                        S.act(lambda e: e.activation(out=p_t[:, a:b], in_=p_s[:, a:b], func=AF.Exp),
                              reads=[b_ps], writes=[b_pt])

                    def c2(n):
                        G, h, ci, nch, kcol, vt, a, b, bias, qm = items[n]
                        p_t, b_pt = PT[n % NP]
                        if bias is not None:
                            S.dve(lambda e: e.tensor_tensor(out=p_t[:, a:b], in0=p_t[:, a:b], in1=bias, op=ALU.mult),
                                  reads=[b_pt, b_MT, b_FT], writes=[b_pt])

                    def c3(n):
                        G, h, ci, nch, kcol, vt, a, b, bias, qm = items[n]
                        p_t, b_pt = PT[n % NP]
                        num, b_num = pnum[h % 4]
                        sl = va_slot(vt)
                        S.pe(lambda e: e.matmul(num[:, a:b], lhsT=VA[:, sl, h, :], rhs=p_t[:, a:b],
                                                start=(ci == 0), stop=(ci == nch - 1)),
                             reads=[b_pt, b_VA[sl]], writes=[b_num])

                    def c4(n):
                        G, h, ci, nch, kcol, vt, a, b, bias, qm = items[n]
                        if ci != nch - 1:
                            return
                        cp, po = h // 2, (h % 2) * 64
                        num, b_num = pnum[h % 4]
                        r_t, b_r = rden[h % 4]
                        y_t, b_y = yat[G % 2]
                        S.act(lambda e: e.activation(out=r_t[0:64, :], in_=num[64:128, :], func=AF.Ln),
                              reads=[b_num], writes=[b_r])
                        S.act(lambda e: e.activation(out=r_t[0:64, :], in_=r_t[0:64, :], func=AF.Exp, scale=-1.0),
                              reads=[b_r], writes=[b_r])
                        S.dve(lambda e: e.tensor_tensor(out=y_t[po:po + 64, cp, :], in0=num[0:64, :],
                                                        in1=r_t[0:64, :], op=ALU.mult),
                              reads=[b_num, b_r], writes=[b_y])
                        if h == 7:
                            S.dma("sync", yav[:, :, G * 512:(G + 1) * 512], y_t[:], reads=[b_y])

                    def batched(fn):
                        return lambda bi: [fn(2 * bi + k) for k in range(2)]

                    pipeline(len(items) // 2, [batched(c0), batched(c1), batched(c2), batched(c3), batched(c4)])
                    S.flush()

        if "C" in phases:
            with ExitStack() as ph:
                wg, b_wg = sb(ph, "wg", [128, 8, 2048], BF16)
                wfo, b_wfo = sb(ph, "wfo", [128, 4, 1024], BF16)
                wao, b_wao = sb(ph, "wao", [128, 4, 1024], BF16)
                wo, b_wo = sb(ph, "wo", [128, 8, 1024], BF16)
                hxg = [sb(ph, "hxg%d" % i, [128, 8, 512], BF16) for i in range(2)]
                ytg = [sb(ph, "ytg%d" % i, [128, 4, 512], BF16) for i in range(2)]
                yag = [sb(ph, "yag%d" % i, [128, 4, 512], BF16) for i in range(2)]
                mg = [sb(ph, "mg%d" % i, [128, 8, 512], BF16) for i in range(2)]
                xo = [sb(ph, "xo%d" % i, [128, 1024], F32) for i in range(8)]
                sgf = [sb(ph, "sgf%d" % i, [128, 512], F32) for i in range(2)]
                sga = [sb(ph, "sga%d" % i, [128, 512], F32) for i in range(2)]
                tf = [sb(ph, "tf%d" % i, [128, 512], F32) for i in range(2)]
                ta = [sb(ph, "ta%d" % i, [128, 512], F32) for i in range(2)]
                tm = [sb(ph, "tm%d" % i, [128, 512], F32) for i in range(2)]
                pA = [ps(ph, "pA%d" % i, [128, 512]) for i in range(2)]
                pB = [ps(ph, "pB%d" % i, [128, 512]) for i in range(2)]
                pC, b_pC = ps(ph, "pC", [128, 512])
                pD, b_pD = ps(ph, "pD", [128, 512])
                pE = [ps(ph, "pE%d" % i, [128, 512]) for i in range(2)]
                b_wgc = [Buf() for _ in range(8)]
                b_wfoc = [Buf() for _ in range(4)]
                b_waoc = [Buf() for _ in range(4)]
                wgv = w_gate.rearrange("(k p) n -> p k n", p=128)
                wfov = w_fo.rearrange("(k p) n -> p k n", p=128)
                waov = w_ao.rearrange("(k p) n -> p k n", p=128)
                for i4 in range(4):
                    for blk in (i4, 4 + i4):
                        S.dma("gpsimd", wg[:, :, blk * 256:(blk + 1) * 256], wgv[:, :, blk * 256:(blk + 1) * 256],
                              writes=[b_wgc[blk]])
                    S.dma("gpsimd", wfo[:, :, i4 * 256:(i4 + 1) * 256], wfov[:, :, i4 * 256:(i4 + 1) * 256],
                          writes=[b_wfoc[i4]])
                    S.dma("gpsimd", wao[:, :, i4 * 256:(i4 + 1) * 256], waov[:, :, i4 * 256:(i4 + 1) * 256],
                          writes=[b_waoc[i4]])
                for kc in range(0, 8, 4):
                    S.dma("gpsimd", wo[:, kc:kc + 4, :], w_out.rearrange("(k p) n -> p k n", p=128)[:, kc:kc + 4, :],
                          writes=[b_wo])
                hxv = hx_scr.rearrange("k p n -> p k n")
                ytv = yt_scr.rearrange("c p n -> p c n")
                yav = ya_scr.rearrange("c p n -> p c n")

                def loads(G):
                    i = G % 2
                    sl = slice(G * 512, (G + 1) * 512)
                    S.dma("sync", hxg[i][0][:], hxv[:, :, sl], writes=[hxg[i][1]])
                    S.dma("sync", ytg[i][0][:], ytv[:, :, sl], writes=[ytg[i][1]])
                    S.dma("sync", yag[i][0][:], yav[:, :, sl], writes=[yag[i][1]])
                    for tt in range(4):
                        x_o, b_xo = xo[(G * 4 + tt) % 8]
                        tok = G * 512 + tt * 128
                        S.dma("sync", x_o[:], xl[256 + tok:256 + tok + 128, :], writes=[b_xo])

                loads(0)
                for G in range(8):
                    i = G % 2
                    if G + 1 < 8:
                        loads(G + 1)
                    hx_t, b_hx = hxg[i]
                    yt_t, b_yt = ytg[i]
                    ya_t, b_ya = yag[i]
                    m_t, b_m = mg[i]
                    for j in range(8):
                        k = j % 2
                        p_a, b_pa = pA[k]
                        p_b, b_pb = pB[k]
                        for kc in range(8):
                            S.pe(lambda e, kc=kc, j=j, p_a=p_a, hx_t=hx_t: e.matmul(
                                p_a[:, :], lhsT=wg[:, kc, j * 128:(j + 1) * 128], rhs=hx_t[:, kc, :],
                                start=(kc == 0), stop=(kc == 7)), reads=[b_wgc[j // 2], b_hx], writes=[b_pa])
                        for kc in range(8):
                            S.pe(lambda e, kc=kc, j=j, p_b=p_b, hx_t=hx_t: e.matmul(
                                p_b[:, :], lhsT=wg[:, kc, 1024 + j * 128:1024 + (j + 1) * 128], rhs=hx_t[:, kc, :],
                                start=(kc == 0), stop=(kc == 7)), reads=[b_wgc[4 + j // 2], b_hx], writes=[b_pb])
                        for kc in range(4):
                            S.pe(lambda e, kc=kc, j=j, yt_t=yt_t: e.matmul(
                                pC[:, :], lhsT=wfo[:, kc, j * 128:(j + 1) * 128], rhs=yt_t[:, kc, :],
                                start=(kc == 0), stop=(kc == 3)), reads=[b_wfoc[j // 2], b_yt], writes=[b_pC])
                        for kc in range(4):
                            S.pe(lambda e, kc=kc, j=j, ya_t=ya_t: e.matmul(
                                pD[:, :], lhsT=wao[:, kc, j * 128:(j + 1) * 128], rhs=ya_t[:, kc, :],
                                start=(kc == 0), stop=(kc == 3)), reads=[b_waoc[j // 2], b_ya], writes=[b_pD])
                        s_f, b_sf = sgf[k]
                        s_a, b_sa = sga[k]
                        t_f, b_tf = tf[k]
                        t_a, b_ta = ta[k]
                        S.act(lambda e, s_f=s_f, p_a=p_a: e.activation(out=s_f[:, :], in_=p_a[:, :], func=AF.Sigmoid),
                              reads=[b_pa], writes=[b_sf])
                        S.act(lambda e, s_a=s_a, p_b=p_b: e.activation(out=s_a[:, :], in_=p_b[:, :], func=AF.Sigmoid),
                              reads=[b_pb], writes=[b_sa])
                        S.dve(lambda e, t_f=t_f, s_f=s_f: e.tensor_tensor(out=t_f[:, :], in0=pC[:, :], in1=s_f[:, :],
                                                                         op=ALU.mult),
                              reads=[b_pC, b_sf], writes=[b_tf])
                        S.dve(lambda e, t_a=t_a, s_a=s_a: e.tensor_tensor(out=t_a[:, :], in0=pD[:, :], in1=s_a[:, :],
                                                                         op=ALU.mult),
                              reads=[b_pD, b_sa], writes=[b_ta])
                        S.dve(lambda e, j=j, m_t=m_t, t_f=t_f, t_a=t_a: e.tensor_tensor(
                            out=m_t[:, j, :], in0=t_f[:, :], in1=t_a[:, :], op=ALU.add),
                            reads=[b_tf, b_ta], writes=[b_m])
                    for tt in range(4):
                        x_o, b_xo = xo[(G * 4 + tt) % 8]
                        tok = G * 512 + tt * 128
                        for hh in range(2):
                            p_e, b_pe = pE[hh]
                            t_m, b_tm = tm[hh]
                            for kc in range(8):
                                S.pe(lambda e, kc=kc, hh=hh, tt=tt, p_e=p_e, m_t=m_t: e.matmul(
                                    p_e[:, :], lhsT=m_t[:, kc, tt * 128:(tt + 1) * 128],
                                    rhs=wo[:, kc, hh * 512:(hh + 1) * 512], start=(kc == 0), stop=(kc == 7)),
                                    reads=[b_m, b_wo], writes=[b_pe])
                            S.dve(lambda e, hh=hh, p_e=p_e, t_m=t_m: e.tensor_tensor(
                                out=t_m[:, :], in0=p_e[:, :], in1=gb[:, 0, hh * 512:(hh + 1) * 512], op=ALU.mult),
                                reads=[b_pe, b_gb], writes=[b_tm])
                            S.dve(lambda e, hh=hh, t_m=t_m, x_o=x_o: e.tensor_tensor(
                                out=x_o[:, hh * 512:(hh + 1) * 512], in0=t_m[:, :],
                                in1=x_o[:, hh * 512:(hh + 1) * 512], op=ALU.add), reads=[b_tm, b_xo], writes=[b_xo])
                        S.dma("sync", x1_scr[tok:tok + 128, :], x_o[:], reads=[b_xo])
                S.flush()

        if "D" in phases:
            with ExitStack() as ph:
                w1, b_w1 = sb(ph, "w1", [128, 8, 4096], BF16)
                w2, b_w2 = sb(ph, "w2", [128, 32, 1024], BF16)
                xo = [sb(ph, "xoD%d" % i, [128, 1024], F32) for i in range(4)]
                hx2 = [sb(ph, "hx2%d" % i, [128, 8, 256], BF16) for i in range(2)]
                aT, b_aT = sb(ph, "aT", [128, 32, 256], BF16)
                rt = [sb(ph, "rt%d" % i, [128, 256], F32) for i in range(2)]
                tm = [sb(ph, "tmD%d" % i, [128, 512], F32) for i in range(2)]
                W = norm_work(ph, "D", 4)
                pT = [ps(ph, "pTD%d" % i, [128, 1024], BF16) for i in range(2)]
                pH = [ps(ph, "pH%d" % i, [128, 512]) for i in range(2)]
                pO = [ps(ph, "pO%d" % i, [128, 512]) for i in range(4)]
                b_w1c = [Buf() for _ in range(8)]
                b_w2c = [Buf() for _ in range(8)]
                w1v = w_mlp1.rearrange("(k p) n -> p k n", p=128)
                w2v = w_mlp2.rearrange("(k p) n -> p k n", p=128)
                for cb in range(8):
                    S.dma("gpsimd", w1[:, :, cb * 512:(cb + 1) * 512], w1v[:, :, cb * 512:(cb + 1) * 512],
                          writes=[b_w1c[cb]])
                for cb in range(8):
                    S.dma("gpsimd", w2[:, cb * 4:cb * 4 + 4, :], w2v[:, cb * 4:cb * 4 + 4, :], writes=[b_w2c[cb]])

                def xb(Gm, tt):
                    return xo[(Gm * 2 + tt) % 4]

                def wk(Gm, tt):
                    return W[(Gm * 2 + tt) % 4]

                def n_front(Gm):
                    for tt in range(2):
                        x_o, b_xo = xb(Gm, tt)
                        tok = Gm * 256 + tt * 128
                        S.dma("sync", x_o[:], x1_scr[tok:tok + 128, :], writes=[b_xo])
                    for tt in range(2):
                        nt_stats(xb(Gm, tt)[0], xb(Gm, tt)[1], wk(Gm, tt))
                    for tt in range(2):
                        nt_recip(wk(Gm, tt))
                    for tt in range(2):
                        nt_scale(xb(Gm, tt)[0], xb(Gm, tt)[1], wk(Gm, tt))

                def n_back(Gm):
                    h_t, b_h = hx2[Gm % 2]
                    for tt in range(2):
                        nt_tr(wk(Gm, tt), pT[tt][0], pT[tt][1])
                    for tt in range(2):
                        nt_mod(pT[tt][0], pT[tt][1], lambda j, tt=tt: h_t[:, j, tt * 128:(tt + 1) * 128], b_h, 16, 48)

                n_front(0)
                n_back(0)
                for Gm in range(16):
                    h_t, b_h = hx2[Gm % 2]
                    if Gm + 1 < 16:
                        n_front(Gm + 1)
                    for c in range(32):
                        p_h, b_ph = pH[c % 2]
                        r_t, b_rt = rt[c % 2]
                        for kc in range(8):
                            S.pe(lambda e, kc=kc, c=c, p_h=p_h, h_t=h_t: e.matmul(
                                p_h[:, 0:256], lhsT=w1[:, kc, c * 128:(c + 1) * 128], rhs=h_t[:, kc, :],
                                start=(kc == 0), stop=(kc == 7)), reads=[b_w1c[c // 4], b_h], writes=[b_ph])
                        S.act(lambda e, p_h=p_h, r_t=r_t: e.activation(out=r_t[:, :], in_=p_h[:, 0:256], func=AF.Relu),
                              reads=[b_ph], writes=[b_rt])
                        S.dve(lambda e, c=c, r_t=r_t: e.tensor_tensor(out=aT[:, c, :], in0=r_t[:, :], in1=r_t[:, :],
                                                                    op=ALU.mult), reads=[b_rt], writes=[b_aT])
                    if Gm + 1 < 16:
                        n_back(Gm + 1)
                    for tt in range(2):
                        x_o, b_xo = xb(Gm, tt)
                        tok = Gm * 256 + tt * 128
                        for hh in range(2):
                            p_o, b_po = pO[tt * 2 + hh]
                            t_m, b_tm = tm[hh]
                            for c in range(32):
                                S.pe(lambda e, c=c, hh=hh, tt=tt, p_o=p_o: e.matmul(
                                    p_o[:, :], lhsT=aT[:, c, tt * 128:(tt + 1) * 128],
                                    rhs=w2[:, c, hh * 512:(hh + 1) * 512], start=(c == 0), stop=(c == 31)),
                                    reads=[b_aT, b_w2c[c // 4]], writes=[b_po])
                            S.dve(lambda e, hh=hh, p_o=p_o, t_m=t_m: e.tensor_tensor(
                                out=t_m[:, :], in0=p_o[:, :], in1=gb[:, 1, hh * 512:(hh + 1) * 512], op=ALU.mult),
                                reads=[b_po, b_gb], writes=[b_tm])
                            S.dve(lambda e, hh=hh, t_m=t_m, x_o=x_o: e.tensor_tensor(
                                out=x_o[:, hh * 512:(hh + 1) * 512], in0=t_m[:, :],
                                in1=x_o[:, hh * 512:(hh + 1) * 512], op=ALU.add), reads=[b_tm, b_xo], writes=[b_xo])
                        S.dma("sync", out[tok:tok + 128, :], x_o[:], reads=[b_xo])
                S.flush()
        S.flush()
    return nc


def _bf(a):
    return np.asarray(a, np.float32).astype(ml_dtypes.bfloat16)


def _consts():
    n = np.arange(128)
    ang = 2 * np.pi * ((n[:, None] * n[None, :]) % 128) / 128.0
    C, Sn = np.cos(ang), np.sin(ang)
    cs128 = np.concatenate([C, Sn], 1)
    ccsc = np.concatenate([C, Sn, -Sn, C], 1)
    ident = np.eye(128)
    return _bf(cs128), _bf(ccsc), _bf(ident)


def _t3(half):
    p = np.arange(128)
    n2, lo = p // 2, p % 2
    k1 = np.arange(128)
    k2 = np.arange(32) + 32 * half
    k = k1[None, :, None] + 128 * k2[None, None, :]
    ang = 2 * np.pi * ((n2[:, None, None] * k) % 8192) / 8192.0
    live = (lo[:, None, None] == (k1 % 2)[None, :, None])
    t = np.stack([np.cos(ang) * live, -np.sin(ang) * live], 1) / 1024.0
    return _bf(t)


def _rope(half):
    inv = 10000.0 ** (-np.arange(0, 32, 2, dtype=np.float64) / 32.0)
    tab = np.zeros((38, 128, 128), np.float32)
    tab[0:2, :, 0:64] = 1.0
    p = np.arange(128)
    col = (p % 64).astype(np.float64)
    for lt in range(36):
        lr = 2 * lt + p // 64
        row = np.clip(64 * half - 4 + lr, 0, 127).astype(np.float64)
        ar = row[:, None] * inv[None, :]
        ac = col[:, None] * inv[None, :]
        cos = np.concatenate([np.cos(ar), np.cos(ar), np.cos(ac), np.cos(ac)], 1)
        sin = np.concatenate([-np.sin(ar), np.sin(ar), -np.sin(ac), np.sin(ac)], 1)
        tab[2 + lt, :, 0:64] = cos
        tab[2 + lt, :, 64:128] = sin
    return tab


def _bias_tables(rpb):
    kc = np.arange(64)
    c = np.arange(64)
    cs = np.clip(c - 8, 0, 48)
    colv = (kc[:, None] >= cs[None, :]) & (kc[:, None] < cs[None, :] + 16)
    dc = np.clip(kc[:, None] - c[None, :] + 15, 0, 30)
    MT = np.full((2, 64, 8, 9, 64), NEG, np.float32)
    FT = np.full((2, 64, 8, 14, 64), NEG, np.float32)
    for par in range(2):
        for s in range(9):
            dr = (3 - s) + par
            if -4 <= dr <= 3:
                vals = rpb[:, dr + 7][:, dc]
                MT[par, :, :, s, :] = np.where(colv[:, None, :], vals.transpose(1, 0, 2), NEG)
        for s in range(14):
            dr = (6 - s) + par
            if -7 <= dr <= 7:
                vals = rpb[:, dr + 7][:, dc]
                FT[par, :, :, s, :] = np.where(colv[:, None, :], vals.transpose(1, 0, 2), NEG)
    return _bf(MT.reshape(128, 8, 576)), _bf(FT.reshape(128, 8, 896))


def _qmask(half):
    q = np.zeros((4, 512), np.float32)
    if half == 0:
        q[0, 0:256] = NEG
        q[3, :] = NEG
    else:
        q[1, :] = NEG
        q[2, 256:512] = NEG
    return _bf(q.reshape(1, 2048))


def _pj(v, nj):
    return np.ascontiguousarray(np.asarray(v, np.float32).reshape(nj, 128).T)


def make_in_maps(x, c, ctx, c_ctx, w_ada, b_ada, norm1_g, norm2_g, w_in, q_norm_g, k_norm_g, rpb,
                 w_branch_gate, w_fourier_out, w_attn_out, w_out, w_mlp1, w_mlp2):
    f = lambda a: np.ascontiguousarray(np.asarray(a, np.float32))
    x, c, ctx, c_ctx = f(x), f(c), f(ctx), f(c_ctx)
    cs128, ccsc, ident = _consts()
    MT, FT = _bias_tables(f(rpb)[0])
    b = f(b_ada)[0]
    bT = _pj(b, 48)
    bgt = np.ascontiguousarray(np.broadcast_to(np.stack([b[2048:3072], b[5120:6144]], 0)[None], (128, 2, 1024)))
    ngt = np.concatenate([_pj(f(norm1_g)[0], 8), _pj(f(norm2_g)[0], 8)], 1)
    gq = np.tile(f(q_norm_g)[0], 8)
    gk = np.tile(f(k_norm_g)[0], 8)
    gqk = np.ascontiguousarray(np.broadcast_to(np.stack([gq, gk], 0)[None], (128, 2, 512)))
    shared = dict(w_ada=f(w_ada)[0], bT=bT, bg=bgt, ng=ngt, gqk=gqk, w_in=f(w_in)[0], w_gate=f(w_branch_gate)[0],
                  w_fo=f(w_fourier_out)[0], w_ao=f(w_attn_out)[0], w_out=f(w_out)[0], w_mlp1=f(w_mlp1)[0],
                  w_mlp2=f(w_mlp2)[0], MT=MT, FT=FT, ident=ident, cs128=cs128, ccsc=ccsc)
    per_half = [dict(rope=_rope(h), qmask=_qmask(h), t3=_t3(h)) for h in range(2)]
    maps = []
    for core in range(8):
        bi, half = core // 2, core % 2
        rows = np.clip(64 * half - 4 + np.arange(72), 0, 127)
        xl = np.ascontiguousarray(x[bi].reshape(128, 64, 1024)[rows].reshape(4608, 1024))
        ccl = np.ascontiguousarray(np.stack([_pj(c[bi], 8), _pj(c_ctx, 8)], 2))
        m = dict(shared)
        m.update(per_half[half])
        m.update(xf=x[bi], xl=xl, ctx=ctx[bi], cc=ccl)
        maps.append(m)
    return maps


_NC = {}


def kernel(**inputs):
    if "nc" not in _NC:
        _NC["nc"] = build()
    maps = make_in_maps(**inputs)
    res = run_bass_kernel_spmd(_NC["nc"], maps, core_ids=list(range(8)))
    outp = np.empty((4, 8192, 1024), np.float32)
    for core in range(8):
        bi, half = core // 2, core % 2
        outp[bi, half * 4096:(half + 1) * 4096] = res.results[core]["out"]
    return outp
```

```python
import numpy as np
import ml_dtypes
from contextlib import ExitStack
import concourse.bass as bass
import concourse.mybir as mybir
from concourse.bass_utils import run_bass_kernel_spmd

F32 = mybir.dt.float32
BF16 = mybir.dt.bfloat16
AF = mybir.ActivationFunctionType
ALU = mybir.AluOpType
AX = mybir.AxisListType
NEG = -30000.0
EPS = 1e-6


class Buf:
    __slots__ = ("lw", "rd")

    def __init__(self):
        self.lw = None
        self.rd = {}


class Op:
    __slots__ = ("eng", "fn", "deps", "dma", "sem", "val", "signal")


CENG = ["gpsimd", "scalar", "vector", "tensor"]
ENGS = ["sync"] + CENG


class Sched:
    def __init__(self, nc, stack, ndma=40):
        self.nc = nc
        self.csem = {e: stack.enter_context(nc.semaphore("c_" + e)) for e in CENG}
        self.ccount = {e: 0 for e in CENG}
        self.dsem = {q: [stack.enter_context(nc.semaphore("d_%s%d" % (q, i))) for i in range(ndma)]
                     for q in ("sync", "gpsimd")}
        self.dcount = {q: [0] * ndma for q in self.dsem}
        self.dnext = {q: 0 for q in self.dsem}
        self.dlast = {q: [None] * ndma for q in self.dsem}
        self.ndma = ndma
        self.ops = []
        self.waited = {e: {} for e in ENGS}
        self.semobj = {}

    def add(self, eng, fn, reads=(), writes=(), dma=False):
        op = Op()
        op.eng, op.fn, op.dma, op.deps, op.signal = eng, fn, dma, {}, dma
        op.sem = op.val = None
        for b in reads:
            if b.lw is not None:
                op.deps[b.lw] = True
        for b in writes:
            if b.lw is not None and b.lw not in op.deps:
                op.deps[b.lw] = False
            for r in b.rd.values():
                if r is not op and r not in op.deps:
                    op.deps[r] = False
        key = id(op) if dma else eng
        for b in reads:
            b.rd[key] = op
        for b in writes:
            b.lw = op
            b.rd = {}
        self.ops.append(op)
        return op

    def dma(self, q, out, in_, reads=(), writes=()):
        return self.add(q, lambda e: e.dma_start(out=out, in_=in_), reads, writes, dma=True)

    def pe(self, fn, reads=(), writes=()):
        return self.add("tensor", fn, reads, writes)

    def act(self, fn, reads=(), writes=()):
        return self.add("scalar", fn, reads, writes)

    def dve(self, fn, reads=(), writes=()):
        return self.add("vector", fn, reads, writes)

    def pool(self, fn, reads=(), writes=()):
        return self.add("gpsimd", fn, reads, writes)

    def flush(self):
        ops = self.ops
        self.ops = []
        if not ops:
            return
        need = {}
        last = {}
        cur = set(ops)
        for op in ops:
            nl = []
            for d, raw in op.deps.items():
                if d not in cur:
                    continue
                if d.dma or d.eng != op.eng or (raw and not op.dma) or op.dma:
                    if not d.dma:
                        d.signal = True
                    nl.append(d)
            need[op] = nl
            if not op.dma:
                last[op.eng] = op
        for op in last.values():
            op.signal = True
        for op in ops:
            if op.dma:
                q = op.eng
                s = self.dnext[q]
                self.dnext[q] = (s + 1) % self.ndma
                prev = self.dlast[q][s]
                if prev is not None:
                    need[op].append(prev)
                self.dcount[q][s] += 16
                op.sem, op.val = self.dsem[q][s], self.dcount[q][s]
                self.dlast[q][s] = op
            elif op.signal:
                self.ccount[op.eng] += 1
                op.sem, op.val = self.csem[op.eng], self.ccount[op.eng]
        streams = {e: [] for e in ENGS}
        for op in ops:
            streams[op.eng].append(op)
        finals = []
        for e in CENG:
            if self.ccount[e] > 0:
                finals.append((self.csem[e], self.ccount[e]))
        for q in self.dsem:
            for i in range(self.ndma):
                if self.dcount[q][i] > 0:
                    finals.append((self.dsem[q][i], self.dcount[q][i]))

        def run(ename):
            def body(e):
                wd = self.waited[ename]
                for op in streams[ename]:
                    for d in need[op]:
                        k = id(d.sem)
                        if wd.get(k, 0) < d.val:
                            e.wait_ge(d.sem, d.val)
                            wd[k] = d.val
                    ins = op.fn(e)
                    if op.signal:
                        ins.then_inc(op.sem, 16 if op.dma else 1)
                for sem, val in finals:
                    k = id(sem)
                    if wd.get(k, 0) < val:
                        e.wait_ge(sem, val)
                        wd[k] = val
            return body

        with self.nc.Block() as block:
            block.sync(run("sync"))
            block.gpsimd(run("gpsimd"))
            block.scalar(run("scalar"))
            block.vector(run("vector"))
            block.tensor(run("tensor"))


def mkap(t, offset, dims):
    base = t[:]
    p = base.ap[0]
    return bass.AP(tensor=base.tensor, offset=offset, ap=[[p[0], p[1]]] + [list(d) for d in dims])


def build(debug=False, phases="MABCD"):
    nc = bass.Bass("TRN2", target_bir_lowering=False)

    def din(name, shape, dt=F32):
        return nc.dram_tensor(name, list(shape), dt, kind="ExternalInput").ap()

    xf = din("xf", [8192, 1024])
    xl = din("xl", [4608, 1024])
    ctx = din("ctx", [256, 1024])
    cc = din("cc", [128, 8, 2])
    w_ada = din("w_ada", [1024, 6144])
    bT = din("bT", [128, 48])
    bg = din("bg", [128, 2, 1024])
    ng = din("ng", [128, 16])
    gqk = din("gqk", [128, 2, 512])
    w_in = din("w_in", [1024, 2048])
    w_gate = din("w_gate", [1024, 2048])
    w_fo = din("w_fo", [512, 1024])
    w_ao = din("w_ao", [512, 1024])
    w_out = din("w_out", [1024, 1024])
    w_mlp1 = din("w_mlp1", [1024, 4096])
    w_mlp2 = din("w_mlp2", [4096, 1024])
    rope = din("rope", [38, 128, 128])
    MTd = din("MT", [128, 8, 576], BF16)
    FTd = din("FT", [128, 8, 896], BF16)
    qmd = din("qmask", [1, 2048], BF16)
    identd = din("ident", [128, 128], BF16)
    cs128d = din("cs128", [128, 256], BF16)
    ccscd = din("ccsc", [128, 512], BF16)
    t3d = din("t3", [128, 2, 128, 32], BF16)
    out = nc.dram_tensor("out", [4096, 1024], F32, kind="ExternalOutput").ap()
    skind = "ExternalOutput" if debug else "Internal"
    yt_scr = nc.dram_tensor("yt_scr", [4, 128, 4096], BF16, kind=skind).ap()
    ya_scr = nc.dram_tensor("ya_scr", [4, 128, 4096], BF16, kind=skind).ap()
    hx_scr = nc.dram_tensor("hx_scr", [8, 128, 4096], BF16, kind=skind).ap()
    x1_scr = nc.dram_tensor("x1_scr", [4096, 1024], F32, kind=skind).ap()
    if debug:
        dbg = nc.dram_tensor("dbg", [128, 2048], F32, kind="ExternalOutput").ap()

    with ExitStack() as gst:
        GA = gst.enter_context
        S = Sched(nc, gst)

        def sb(st, name, shape, dt):
            return st.enter_context(nc.sbuf_tensor(name, list(shape), dt)), Buf()

        def ps(st, name, shape, dt=F32):
            return st.enter_context(nc.psum_tensor(name, list(shape), dt)), Buf()

        mod, b_mod = sb(gst, "mod", [128, 48, 2], F32)
        sce, b_sce = sb(gst, "sce", [128, 2, 8, 2], F32)
        gb, b_gb = sb(gst, "gb", [128, 2, 1024], F32)
        idt, b_idt = sb(gst, "idt", [128, 128], BF16)
        ones, b_ones = sb(gst, "ones", [128, 128], BF16)
        epsb, b_eps = sb(gst, "epsb", [128, 1], F32)

        S.pool(lambda e: e.memset(epsb[:], EPS), writes=[b_eps])
        S.pool(lambda e: e.memset(ones[:], 1.0), writes=[b_ones])
        S.dma("sync", idt[:], identd, writes=[b_idt])

        def pipeline(n, stages, hooks=None):
            d = len(stages)
            for step in range(n + d - 1):
                if hooks and step in hooks:
                    hooks[step]()
                for si in range(d - 1, -1, -1):
                    t = step - si
                    if 0 <= t < n:
                        stages[si](t)

        def nt_stats(xt, b_xt, W):
            S.act(lambda e: e.activation(out=W["junk"][:], in_=xt[:], func=AF.Square, accum_out=W["st"][:, 0:1]),
                  reads=[b_xt], writes=[W["b_junk"], W["b_st"]])
            S.act(lambda e: e.activation(out=W["st"][:, 1:2], in_=W["st"][:, 0:1], func=AF.Sqrt, scale=1.0 / 1024,
                                         bias=epsb[:, 0:1]), reads=[W["b_st"], b_eps], writes=[W["b_st"]])

        def nt_recip(W):
            S.dve(lambda e: e.reciprocal(out=W["st"][:, 2:3], in_=W["st"][:, 1:2]), reads=[W["b_st"]],
                  writes=[W["b_st"]])

        def nt_scale(xt, b_xt, W):
            S.act(lambda e: e.activation(out=W["xn"][:], in_=xt[:], func=AF.Copy, scale=W["st"][:, 2:3]),
                  reads=[b_xt, W["b_st"]], writes=[W["b_xn"]])

        def nt_tr(W, pT, b_pT):
            for j in range(8):
                S.pe(lambda e, j=j: e.transpose(pT[:, j * 128:(j + 1) * 128], W["xn"][:, j * 128:(j + 1) * 128],
                                                idt[:]), reads=[W["b_xn"], b_idt], writes=[b_pT])

        def nt_mod(pT, b_pT, dst3, b_dst, sc_off, sh_off):
            for j in range(8):
                S.act(lambda e, j=j: e.activation(out=dst3(j), in_=pT[:, j * 128:(j + 1) * 128], func=AF.Identity,
                                                  scale=mkap(sce, sc_off + 2 * j, [[1, 1]]),
                                                  bias=mkap(mod, sh_off + 2 * j, [[1, 1]])),
                      reads=[b_pT, b_sce, b_mod], writes=[b_dst])

        def norm_work(st, tag, n):
            junk, b_junk = sb(st, "junk" + tag, [128, 1024], BF16)
            res = []
            for i in range(n):
                st_, b_st = sb(st, "st%s%d" % (tag, i), [128, 4], F32)
                xn, b_xn = sb(st, "xn%s%d" % (tag, i), [128, 1024], BF16)
                res.append(dict(junk=junk, b_junk=b_junk, st=st_, b_st=b_st, xn=xn, b_xn=b_xn))
            return res

        if "A" in phases:
            with ExitStack() as ph:
                cc_sb, b_cc = sb(ph, "cc_sb", [128, 8, 2], F32)
                scb, b_scb = sb(ph, "scb", [128, 8, 2], BF16)
                screp, b_screp = sb(ph, "screp", [128, 8, 128], BF16)
                bT_sb, b_bT = sb(ph, "bT_sb", [128, 48], F32)
                bg_sb, b_bg = sb(ph, "bg_sb", [128, 2, 1024], F32)
                ng_sb, b_ng = sb(ph, "ng_sb", [128, 16], F32)
                wsl = [sb(ph, "wsl%d" % i, [128, 8, 512], BF16) for i in range(2)]
                w_inf, b_winf = sb(ph, "w_inf", [128, 8, 512], BF16)
                u_all = ph.enter_context(nc.sbuf_tensor("u_all", [128, 64, 512], BF16))
                b_u = [Buf() for _ in range(64)]
                Bg, b_Bg = sb(ph, "Bg", [128, 2, 64, 128], BF16)
                YTg = [sb(ph, "YTg%d" % i, [128, 4096], BF16) for i in range(1)]
                Zt = [sb(ph, "Zt%d" % i, [128, 512], BF16) for i in range(3)]
                t3_sb, b_t3 = sb(ph, "t3_sb", [128, 2, 128, 32], BF16)
                cs_sb, b_cs = sb(ph, "cs_sb", [128, 256], BF16)
                ccsc_sb, b_ccsc = sb(ph, "ccsc_sb", [128, 512], BF16)
                xt = [sb(ph, "xtA%d" % i, [128, 1024], F32) for i in range(3)]
                hxT = [sb(ph, "hxTA%d" % i, [128, 8, 128], BF16) for i in range(2)]
                W = norm_work(ph, "A", 3)
                pT = [ps(ph, "pTA%d" % i, [128, 1024], BF16) for i in range(2)]
                pu = [ps(ph, "puA%d" % i, [128, 512]) for i in range(2)]
                pa = [ps(ph, "paA%d" % i, [128, 512]) for i in range(2)]
                pb = [ps(ph, "pbA%d" % i, [128, 512]) for i in range(2)]
                wv = w_in.rearrange("(k p) n -> p k n", p=128)
                for kc in range(8):
                    S.dma("gpsimd", w_inf[:, kc, :], wv[:, kc, 0:512], writes=[b_winf])
                S.dma("sync", t3_sb[:], t3d, writes=[b_t3])
                S.dma("sync", cs_sb[:], cs128d, writes=[b_cs])
                S.dma("sync", ccsc_sb[:], ccscd, writes=[b_ccsc])
                xsrc = xf.rearrange("(n1 n2) d -> n2 n1 d", n2=64)
                pmod, b_pmod = pb[0]
                pg = [pa[0], pa[1]]
                S.dma("sync", cc_sb[:], cc, writes=[b_cc])
                S.dma("sync", bT_sb[:], bT, writes=[b_bT])
                S.dma("sync", bg_sb[:], bg, writes=[b_bg])
                S.dma("sync", ng_sb[:], ng, writes=[b_ng])
                S.act(lambda e: e.activation(out=scb[:], in_=cc_sb[:], func=AF.Silu), reads=[b_cc], writes=[b_scb])
                S.act(lambda e: e.activation(out=screp[:], in_=cc_sb[:, :, 0:1].broadcast_to([128, 8, 128]),
                                             func=AF.Silu), reads=[b_cc], writes=[b_screp])
                wav = w_ada.rearrange("(k p) n -> p k n", p=128)

                def m_dma(hs):
                    w, b_w = wsl[hs % 2]
                    S.dma("gpsimd", w[:], wav[:, :, hs * 512:(hs + 1) * 512], writes=[b_w])

                def m_proc(hs):
                    w, b_w = wsl[hs % 2]
                    v, hh = hs // 2, hs % 2
                    for j4 in range(4):
                        col = (v * 8 + hh * 4 + j4) * 2
                        for kc in range(8):
                            S.pe(lambda e, j4=j4, kc=kc, col=col: e.matmul(
                                pmod[:, col:col + 2], lhsT=w[:, kc, j4 * 128:(j4 + 1) * 128], rhs=scb[:, kc, :],
                                start=(kc == 0), stop=(kc == 7)), reads=[b_w, b_scb], writes=[b_pmod])
                    if v in (2, 5):
                        gi = 0 if v == 2 else 1
                        pgt, b_pg = pg[hh]
                        for kc in range(8):
                            S.pe(lambda e, kc=kc: e.matmul(pgt[:, :], lhsT=screp[:, kc, :], rhs=w[:, kc, :],
                                                           start=(kc == 0), stop=(kc == 7)),
                                 reads=[b_w, b_screp], writes=[b_pg])
                        S.dve(lambda e: e.tensor_tensor(out=gb[:, gi, hh * 512:(hh + 1) * 512], in0=pgt[:, :],
                                                        in1=bg_sb[:, gi, hh * 512:(hh + 1) * 512], op=ALU.add),
                              reads=[b_pg, b_bg], writes=[b_gb])
                    if hs + 2 < 12:
                        m_dma(hs + 2)
                    if hh == 1:
                        S.dve(lambda e: e.tensor_tensor(
                            out=mod[:, v * 8:(v + 1) * 8, :],
                            in0=pmod[:, v * 16:(v + 1) * 16].rearrange("p (a b) -> p a b", b=2),
                            in1=bT_sb[:, v * 8:(v + 1) * 8].unsqueeze(2).broadcast_to([128, 8, 2]), op=ALU.add),
                            reads=[b_pmod, b_bT], writes=[b_mod])
                        if v in (1, 4):
                            wi = 0 if v == 1 else 1
                            S.dve(lambda e: e.scalar_tensor_tensor(
                                out=sce[:, wi], in0=mod[:, v * 8:(v + 1) * 8, :], scalar=1.0,
                                in1=ng_sb[:, wi * 8:(wi + 1) * 8].unsqueeze(2).broadcast_to([128, 8, 2]),
                                op0=ALU.add, op1=ALU.mult), reads=[b_mod, b_ng], writes=[b_sce])

                m_dma(0)
                m_dma(1)
                for hs in range(4):
                    m_proc(hs)
                sh_rep, b_shrep = sb(ph, "sh_rep", [128, 8, 128], BF16)
                shw, b_shw = sb(ph, "shw", [128, 512], F32)
                S.act(lambda e: e.copy(out=sh_rep[:], in_=mkap(mod, 0, [[2, 8], [0, 128]])), reads=[b_mod],
                      writes=[b_shrep])
                for kc in range(8):
                    S.pe(lambda e, kc=kc: e.matmul(pu[0][0][:, :], lhsT=sh_rep[:, kc, :], rhs=w_inf[:, kc, :],
                                                   start=(kc == 0), stop=(kc == 7)),
                         reads=[b_shrep, b_winf], writes=[pu[0][1]])
                S.dve(lambda e: e.tensor_copy(out=shw[:, :], in_=pu[0][0][:, :]), reads=[pu[0][1]], writes=[b_shw])

                def a0(t):
                    S.dma("sync", xt[t % 3][0][:], xsrc[t], writes=[xt[t % 3][1]])

                def a1(t):
                    nt_stats(xt[t % 3][0], xt[t % 3][1], W[t % 3])

                def a2(t):
                    nt_recip(W[t % 3])

                def a3(t):
                    nt_scale(xt[t % 3][0], xt[t % 3][1], W[t % 3])

                def a4(t):
                    nt_tr(W[t % 3], pT[t % 2][0], pT[t % 2][1])

                def a5(t):
                    h_t, b_h = hxT[t % 2]
                    p_T, b_pT = pT[t % 2]
                    S.dve(lambda e: e.tensor_tensor(out=h_t[:], in0=p_T[:, :].rearrange("p (a b) -> p a b", b=128),
                                                    in1=mkap(sce, 0, [[2, 8], [0, 128]]), op=ALU.mult),
                          reads=[b_pT, b_sce], writes=[b_h])

                def a6(t):
                    h_t, b_h = hxT[t % 2]
                    p_u, b_pu = pu[t % 2]
                    for kc in range(8):
                        S.pe(lambda e, kc=kc: e.matmul(p_u[:, :], lhsT=h_t[:, kc, :], rhs=w_inf[:, kc, :],
                                                       start=(kc == 0), stop=(kc == 7)),
                             reads=[b_h, b_winf], writes=[b_pu])

                def a7(t):
                    p_u, b_pu = pu[t % 2]
                    S.dve(lambda e: e.tensor_tensor(out=u_all[:, t, :], in0=p_u[:, :], in1=shw[:, :], op=ALU.add),
                          reads=[b_pu, b_shw], writes=[b_u[t]])

                pipeline(64, [a0, a1, a2, a3, a4, a5, a6, a7], hooks={8 * (hs - 3): (lambda hs=hs: m_proc(hs))
                                                                    for hs in range(4, 12)})

                for g in range(4):
                    def s1mm(pr, g=g):
                        p_a, b_pa = pa[pr % 2]
                        for q in range(2):
                            n2 = 2 * pr + q
                            S.pe(lambda e, n2=n2, q=q: e.matmul(
                                p_a[:, q * 256:(q + 1) * 256], lhsT=u_all[:, n2, g * 128:(g + 1) * 128],
                                rhs=cs_sb[:, :], start=True, stop=True), reads=[b_u[n2], b_cs], writes=[b_pa])

                    def s1ev(pr):
                        p_a, b_pa = pa[pr % 2]
                        for q in range(2):
                            n2 = 2 * pr + q
                            src = mkap(p_a, q * 256, [[128, 2], [2, 64], [1, 2]])
                            dst = mkap(Bg, 2 * n2, [[8192, 2], [128, 64], [1, 2]])
                            S.act(lambda e, src=src, dst=dst: e.copy(out=dst, in_=src),
                                  reads=[b_pa], writes=[b_Bg])

                    pipeline(32, [s1mm, s1ev])
                    y_t, b_y = YTg[0]

                    def s2mm(j):
                        p_z, b_pz = pb[j % 2]
                        lr = mkap(Bg, j * 128, [[1, 128]])
                        ls = mkap(Bg, 8192 + j * 128, [[1, 128]])
                        S.pe(lambda e: e.matmul(p_z[:, 0:256], lhsT=lr, rhs=ccsc_sb[:, 0:256], start=True, stop=False),
                             reads=[b_Bg, b_ccsc], writes=[b_pz])
                        S.pe(lambda e: e.matmul(p_z[:, 0:256], lhsT=ls, rhs=ccsc_sb[:, 256:512], start=False,
                                                stop=True), reads=[b_Bg, b_ccsc], writes=[b_pz])

                    def s2ev(j):
                        p_z, b_pz = pb[j % 2]
                        z_t, b_z = Zt[j % 3]
                        S.dve(lambda e: e.tensor_copy(out=z_t[:, 0:256], in_=p_z[:, 0:256]), reads=[b_pz],
                              writes=[b_z])

                    def s3mm(j, y_t=y_t, b_y=b_y):
                        z_t, b_z = Zt[j % 3]
                        for lo in range(2):
                            k1 = 2 * j + lo
                            col = (k1 % 16) * 32
                            p_y, b_py = pu[(k1 // 16) % 2]
                            S.pe(lambda e, k1=k1, col=col, p_y=p_y: e.matmul(
                                p_y[:, col:col + 32], lhsT=z_t[:, 0:128], rhs=t3_sb[:, 0, k1, :],
                                start=True, stop=False), reads=[b_z, b_t3], writes=[b_py])
                            S.pe(lambda e, k1=k1, col=col, p_y=p_y: e.matmul(
                                p_y[:, col:col + 32], lhsT=z_t[:, 128:256], rhs=t3_sb[:, 1, k1, :],
                                start=False, stop=True), reads=[b_z, b_t3], writes=[b_py])
                            if k1 % 16 == 15:
                                dst = mkap(y_t, 16 * (k1 // 16), [[1, 16], [128, 32]])
                                S.act(lambda e, dst=dst, p_y=p_y: e.copy(
                                    out=dst, in_=p_y[:, :].rearrange("p (a b) -> p a b", b=32)),
                                    reads=[b_py], writes=[b_y])

                    pipeline(64, [s2mm, s2ev, s3mm])
                    S.dma("sync", yt_scr[g], y_t[:], reads=[b_y])
                S.flush()

        if "B" in phases:
            with ExitStack() as ph:
                KT = ph.enter_context(nc.sbuf_tensor("KT", [128, 4, 4864], BF16))
                b_KT = [Buf() for _ in range(38)]
                V = ph.enter_context(nc.sbuf_tensor("V", [128, 38, 512], BF16))
                b_V = [Buf() for _ in range(38)]
                qT = ph.enter_context(nc.sbuf_tensor("qT", [128, 4, 4096], BF16))
                b_qT = [Buf() for _ in range(32)]
                with ExitStack() as p1:
                    w_qkv, b_wqkv = sb(p1, "w_qkv", [128, 8, 1536], BF16)
                    gqk_sb, b_gqk = sb(p1, "gqk_sb", [128, 2, 512], F32)
                    xt = [sb(p1, "xtB%d" % i, [128, 1024], F32) for i in range(3)]
                    rp = [sb(p1, "rpB%d" % i, [128, 128], F32) for i in range(4)]
                    hxT = [sb(p1, "hxTB%d" % i, [128, 8, 128], BF16) for i in range(3)]
                    W = norm_work(p1, "B", 3)
                    pT = [ps(p1, "pTB%d" % i, [128, 1024], BF16) for i in range(1)]
                    pq = [[ps(p1, "pqB%d_%d" % (i, c), [128, 512]) for c in range(3)] for i in range(2)]
                    ptr = [ps(p1, "ptrB%d" % i, [128, 1024], BF16) for i in range(1)]
                    NB = 2
                    tq = []
                    for i in range(NB):
                        d = {}
                        for nm, dt in ():
                            d[nm] = sb(p1, "%s%d" % (nm, i), [128, 512], dt)
                        d["qr"] = sb(p1, "qr%d" % i, [128, 1024], BF16)
                        d["ss"] = sb(p1, "ss%d" % i, [128, 48], F32)
                        tq.append(d)
                    fr = [dict(qf=sb(p1, "qf%d" % i, [128, 512], F32), kf=sb(p1, "kf%d" % i, [128, 512], F32))
                          for i in range(3)]
                    t2r = dict(t2q=sb(p1, "t2q", [128, 512], F32), t2k=sb(p1, "t2k", [128, 512], F32))
                    t1r = [dict(t1q=sb(p1, "t1q%d" % i, [128, 512], F32), t1k=sb(p1, "t1k%d" % i, [128, 512], F32))
                           for i in range(3)]
                    wv = w_in.rearrange("(k p) n -> p k n", p=128)
                    for kc in range(8):
                        S.dma("gpsimd", w_qkv[:, kc, :], wv[:, kc, 512:2048], writes=[b_wqkv])
                    S.dma("sync", gqk_sb[:], gqk, writes=[b_gqk])
                    hxv = hx_scr.rearrange("k p n -> p k n")

                    def own(t):
                        return 2 <= t - 2 < 34

                    def b0(t):
                        src = ctx[t * 128:(t + 1) * 128, :] if t < 2 else xl[(t - 2) * 128:(t - 1) * 128, :]
                        S.dma("sync", xt[t % 3][0][:], src, writes=[xt[t % 3][1]])

                    def b1(t):
                        nt_stats(xt[t % 3][0], xt[t % 3][1], W[t % 3])

                    def b2(t):
                        nt_recip(W[t % 3])

                    def b3(t):
                        nt_scale(xt[t % 3][0], xt[t % 3][1], W[t % 3])

                    def b4(t):
                        nt_tr(W[t % 3], pT[0][0], pT[0][1])

                    def b5(t):
                        h_t, b_h = hxT[t % 3]
                        cond = 1 if t < 2 else 0
                        nt_mod(pT[0][0], pT[0][1], lambda j: h_t[:, j, :], b_h, cond, cond)

                    def b6(t):
                        h_t, b_h = hxT[t % 3]
                        if own(t):
                            o = (t - 4) * 128
                            S.dma("sync", hxv[:, :, o:o + 128], h_t[:], reads=[b_h])
                        for kc in range(8):
                            for c in range(3):
                                if c == 0 and not own(t):
                                    continue
                                p_q, b_pq = pq[t % 2][c]
                                S.pe(lambda e, kc=kc, c=c, p_q=p_q: e.matmul(
                                    p_q[:, :], lhsT=h_t[:, kc, :], rhs=w_qkv[:, kc, c * 512:(c + 1) * 512],
                                    start=(kc == 0), stop=(kc == 7)), reads=[b_h, b_wqkv], writes=[b_pq])

                    def b7(t):
                        d = tq[t % NB]
                        S.act(lambda e: e.copy(out=V[:, t, :], in_=pq[t % 2][2][0][:, :]),
                              reads=[pq[t % 2][2][1]], writes=[b_V[t]])
                        for w_, nm, sq in ((1, "kf", "t1k"), (0, "qf", "t1q")):
                            if w_ == 0 and not own(t):
                                continue
                            p_s, b_ps = pq[t % 2][w_]
                            f_t, b_f = fr[t % 3][nm]
                            s_t, b_s = t1r[t % 3][sq]
                            S.act(lambda e, f_t=f_t, p_s=p_s: e.copy(out=f_t[:, :], in_=p_s[:, :]),
                                  reads=[b_ps], writes=[b_f])
                            S.act(lambda e, s_t=s_t, p_s=p_s: e.activation(out=s_t[:, :], in_=p_s[:, :],
                                                                          func=AF.Square),
                                  reads=[b_ps], writes=[b_s])

                    def b8(t):
                        S.dma("sync", rp[t % 4][0][:], rope[t], writes=[rp[t % 4][1]])
                        d = tq[t % NB]
                        ss, b_ss = d["ss"]
                        for w_, sq in ((1, "t1k"), (0, "t1q")):
                            if w_ == 0 and not own(t):
                                continue
                            s_t, b_s = t1r[t % 3][sq]
                            S.dve(lambda e, w_=w_, s_t=s_t: e.tensor_reduce(
                                out=ss[:, w_ * 8:(w_ + 1) * 8], in_=s_t[:, :].rearrange("p (h d) -> p h d", d=64),
                                axis=AX.X, op=ALU.add), reads=[b_s], writes=[b_ss])

                    def b9(t):
                        d = tq[t % NB]
                        ss, b_ss = d["ss"]
                        lo_ = 0 if own(t) else 8
                        S.act(lambda e: e.activation(out=ss[:, 16 + lo_:32], in_=ss[:, lo_:16], func=AF.Sqrt,
                                                     scale=1.0 / 64, bias=epsb[:, 0:1]),
                              reads=[b_ss, b_eps], writes=[b_ss])

                    def b10(t):
                        d = tq[t % NB]
                        ss, b_ss = d["ss"]
                        rp_t, b_rp = rp[t % 4]
                        qr, b_qr = d["qr"]
                        lo_ = 0 if own(t) else 8
                        S.dve(lambda e: e.reciprocal(out=ss[:, 32 + lo_:48], in_=ss[:, 16 + lo_:32]), reads=[b_ss],
                              writes=[b_ss])
                        for w_, fn, nn, t1n, t2n in ((1, "kf", "kf", "t1k", "t2k"), (0, "qf", "qf", "t1q", "t2q")):
                            if w_ == 0 and not own(t):
                                continue
                            f_t, b_f = fr[t % 3][fn]
                            n_t, b_n = fr[t % 3][nn]
                            t1, b_t1 = t1r[t % 3][t1n]
                            t2, b_t2 = t2r[t2n]
                            S.dve(lambda e, w_=w_, f_t=f_t, n_t=n_t: e.scalar_tensor_tensor(
                                out=n_t[:, :].rearrange("p (h d) -> p h d", d=64),
                                in0=f_t[:, :].rearrange("p (h d) -> p h d", d=64),
                                scalar=(0.125 if w_ == 0 else 1.0),
                                in1=ss[:, 32 + w_ * 8:32 + (w_ + 1) * 8].unsqueeze(2).broadcast_to([128, 8, 64]),
                                op0=ALU.mult, op1=ALU.mult), reads=[b_f, b_ss], writes=[b_n])
                            S.dve(lambda e, w_=w_, n_t=n_t: e.tensor_tensor(out=n_t[:, :], in0=n_t[:, :],
                                                                         in1=gqk_sb[:, w_, :], op=ALU.mult),
                                  reads=[b_n, b_gqk], writes=[b_n])
                            S.dve(lambda e, n_t=n_t, t1=t1: e.tensor_tensor(
                                out=t1[:, :].rearrange("p (h d) -> p h d", d=64),
                                in0=n_t[:, :].rearrange("p (h d) -> p h d", d=64),
                                in1=rp_t[:, 0:64].unsqueeze(1).broadcast_to([128, 8, 64]), op=ALU.mult),
                                reads=[b_n, b_rp], writes=[b_t1])
                            for a in range(2):
                                o_ap = mkap(t2, a * 16, [[64, 8], [32, 2], [1, 16]])
                                i_ap = mkap(n_t, (1 - a) * 16, [[64, 8], [32, 2], [1, 16]])
                                s_ap = mkap(rp_t, 64 + a * 16, [[0, 8], [32, 2], [1, 16]])
                                S.dve(lambda e, o_ap=o_ap, i_ap=i_ap, s_ap=s_ap: e.tensor_tensor(
                                    out=o_ap, in0=i_ap, in1=s_ap, op=ALU.mult), reads=[b_n, b_rp], writes=[b_t2])
                            S.dve(lambda e, w_=w_, t1=t1, t2=t2: e.tensor_tensor(
                                out=qr[:, w_ * 512:(w_ + 1) * 512], in0=t1[:, :], in1=t2[:, :], op=ALU.add),
                                reads=[b_t1, b_t2], writes=[b_qr])

                    def b11(t):
                        d = tq[t % NB]
                        qr, b_qr = d["qr"]
                        p_t, b_pt = ptr[0]
                        for w_ in (1, 0):
                            if w_ == 0 and not own(t):
                                continue
                            for c in range(4):
                                S.pe(lambda e, c=c, w_=w_: e.transpose(
                                    p_t[:, w_ * 512 + c * 128:w_ * 512 + (c + 1) * 128],
                                    qr[:, w_ * 512 + c * 128:w_ * 512 + (c + 1) * 128], idt[:]),
                                    reads=[b_qr, b_idt], writes=[b_pt])

                    def b12(t):
                        p_t, b_pt = ptr[0]
                        S.act(lambda e: e.copy(out=KT[:, :, t * 128:(t + 1) * 128],
                                               in_=p_t[:, 512:1024].rearrange("p (a b) -> p a b", b=128)),
                              reads=[b_pt], writes=[b_KT[t]])
                        if own(t):
                            o = (t - 4) * 128
                            S.act(lambda e: e.copy(out=qT[:, :, o:o + 128],
                                                   in_=p_t[:, 0:512].rearrange("p (a b) -> p a b", b=128)),
                                  reads=[b_pt], writes=[b_qT[t - 4]])

                    pipeline(38, [b0, b1, b2, b3, b4, b5, b6, b7, b8, b9, b10, b11, b12])
                    S.flush()
                with ExitStack() as p2:
                    MT_sb, b_MT = sb(p2, "MT_sb", [128, 8, 576], BF16)
                    FT_sb, b_FT = sb(p2, "FT_sb", [128, 8, 896], BF16)
                    qm_sb, b_qm = sb(p2, "qm_sb", [1, 2048], BF16)
                    NP = 4
                    PT = [sb(p2, "PT%d" % i, [128, 512], BF16) for i in range(NP)]
                    rden = [sb(p2, "rden%d" % i, [128, 512], F32) for i in range(4)]
                    yat = [sb(p2, "yat%d" % i, [128, 4, 512], BF16) for i in range(2)]
                    pS = [ps(p2, "pS%d" % i, [128, 512]) for i in range(NP)]
                    pnum = [ps(p2, "pnum%d" % i, [128, 512]) for i in range(4)]
                    VA = p2.enter_context(nc.sbuf_tensor("VA", [128, 14, 8, 128], BF16))
                    b_VA = [Buf() for _ in range(14)]
                    S.pool(lambda e: e.memset(VA[:], 1.0), writes=b_VA)

                    def va_slot(vt):
                        return 12 + vt if vt < 2 else (vt - 2) % 12

                    def va_fill(vt):
                        sl = va_slot(vt)
                        S.dve(lambda e: e.tensor_copy(out=VA[:, sl, :, 0:64],
                                                      in_=V[:, vt, :].rearrange("p (h d) -> p h d", d=64)),
                              reads=[b_V[vt]], writes=[b_VA[sl]])
                    S.dma("sync", MT_sb[:], MTd, writes=[b_MT])
                    S.dma("sync", FT_sb[:], FTd, writes=[b_FT])
                    S.dma("sync", qm_sb[:], qmd, writes=[b_qm])
                    S.act(lambda e: e.activation(out=MT_sb[:], in_=MT_sb[:], func=AF.Exp), reads=[b_MT], writes=[b_MT])
                    S.act(lambda e: e.activation(out=FT_sb[:], in_=FT_sb[:], func=AF.Exp), reads=[b_FT], writes=[b_FT])
                    yav = ya_scr.rearrange("c p n -> p c n")
                    items = []

                    def head_chunks(G, h):
                        r0 = 4 + 8 * G
                        chunks = []
                        for c in range(2):
                            chunks.append((c * 128, c, 0, 512, None, None))
                        for m in range(8):
                            lrk = r0 - 4 + 2 * m
                            ia, ib = max(0, 2 * m - 7), min(7, 2 * m + 1)
                            a, b = ia * 64, (ib + 1) * 64
                            bias = MT_sb[:, h, (7 - 2 * m + ia) * 64:(7 - 2 * m + ib + 1) * 64]
                            qm = None
                            if G == 0:
                                qm = qm_sb[0:1, a:b]
                            elif G == 7:
                                qm = qm_sb[0:1, 1024 + a:1024 + b]
                            chunks.append((256 + lrk * 64, 2 + lrk // 2, a, b, bias, qm))
                        if G == 0:
                            for ee in range(4):
                                lrk = r0 + 2 * ee
                                chunks.append((256 + lrk * 64, 2 + lrk // 2, 0, 256,
                                               FT_sb[:, h, (6 - 2 * ee) * 64:(10 - 2 * ee) * 64],
                                               qm_sb[0:1, 512:768]))
                        if G == 7:
                            for ee in range(4):
                                lrk = r0 + 2 * ee
                                chunks.append((256 + lrk * 64, 2 + lrk // 2, 256, 512,
                                               FT_sb[:, h, (10 - 2 * ee) * 64:(14 - 2 * ee) * 64],
                                               qm_sb[0:1, 1536 + 256:1536 + 512]))
                        return chunks

                    for G in range(8):
                        for p in range(4):
                            chA = head_chunks(G, 2 * p)
                            chB = head_chunks(G, 2 * p + 1)
                            for ci in range(len(chA)):
                                items.append((G, 2 * p, ci, len(chA)) + chA[ci])
                                items.append((G, 2 * p + 1, ci, len(chB)) + chB[ci])

                    def c0(n):
                        G, h, ci, nch, kcol, vt, a, b, bias, qm = items[n]
                        if h == 0 and ci == 0:
                            if G == 0:
                                va_fill(0)
                                va_fill(1)
                            for lt in (range(0, 8) if G == 0 else range(4 * G + 4, 4 * G + 8)):
                                va_fill(2 + lt)
                        cp, po = h // 2, (h % 2) * 64
                        p_s, b_ps = pS[n % NP]
                        qb = b_qT[(G * 512 + a) // 128:(G * 512 + b + 127) // 128]
                        S.pe(lambda e: e.matmul(
                            p_s[:, a:b], lhsT=KT[po:po + 64, cp, kcol:kcol + 128],
                            rhs=qT[po:po + 64, cp, G * 512 + a:G * 512 + b], start=True, stop=(qm is None)),
                            reads=[b_KT[kcol // 128]] + qb, writes=[b_ps])
                        if qm is not None:
                            S.pe(lambda e: e.matmul(p_s[:, a:b], lhsT=ones[0:1, :], rhs=qm, start=False, stop=True),
                                 reads=[b_ones, b_qm], writes=[b_ps])

                    def c1(n):
                        G, h, ci, nch, kcol, vt, a, b, bias, qm = items[n]
                        p_s, b_ps = pS[n % NP]
                        p_t, b_pt = PT[n % NP]
                        S.act(lambda e: e.activation(out=p_t[:, a:b], in_=p_s[:, a:b], func=AF.Exp),
                              reads=[b_ps], writes=[b_pt])

                    def c2(n):
                        G, h, ci, nch, kcol, vt, a, b, bias, qm = items[n]
                        p_t, b_pt = PT[n % NP]
                        if bias is not None:
                            S.dve(lambda e: e.tensor_tensor(out=p_t[:, a:b], in0=p_t[:, a:b], in1=bias, op=ALU.mult),
                                  reads=[b_pt, b_MT, b_FT], writes=[b_pt])

                    def c3(n):
                        G, h, ci, nch, kcol, vt, a, b, bias, qm = items[n]
                        p_t, b_pt = PT[n % NP]
                        num, b_num = pnum[h % 4]
                        sl = va_slot(vt)
                        S.pe(lambda e: e.matmul(num[:, a:b], lhsT=VA[:, sl, h, :], rhs=p_t[:, a:b],
                                                start=(ci == 0), stop=(ci == nch - 1)),
                             reads=[b_pt, b_VA[sl]], writes=[b_num])

                    def c4(n):
                        G, h, ci, nch, kcol, vt, a, b, bias, qm = items[n]
                        if ci != nch - 1:
                            return
                        cp, po = h // 2, (h % 2) * 64
                        num, b_num = pnum[h % 4]
                        r_t, b_r = rden[h % 4]
                        y_t, b_y = yat[G % 2]
                        S.act(lambda e: e.activation(out=r_t[0:64, :], in_=num[64:128, :], func=AF.Ln),
                              reads=[b_num], writes=[b_r])
                        S.act(lambda e: e.activation(out=r_t[0:64, :], in_=r_t[0:64, :], func=AF.Exp, scale=-1.0),
                              reads=[b_r], writes=[b_r])
                        S.dve(lambda e: e.tensor_tensor(out=y_t[po:po + 64, cp, :], in0=num[0:64, :],
                                                        in1=r_t[0:64, :], op=ALU.mult),
                              reads=[b_num, b_r], writes=[b_y])
                        if h == 7:
                            S.dma("sync", yav[:, :, G * 512:(G + 1) * 512], y_t[:], reads=[b_y])

                    def batched(fn):
                        return lambda bi: [fn(2 * bi + k) for k in range(2)]

                    pipeline(len(items) // 2, [batched(c0), batched(c1), batched(c2), batched(c3), batched(c4)])
                    S.flush()

        if "C" in phases:
            with ExitStack() as ph:
                wg, b_wg = sb(ph, "wg", [128, 8, 2048], BF16)
                wfo, b_wfo = sb(ph, "wfo", [128, 4, 1024], BF16)
                wao, b_wao = sb(ph, "wao", [128, 4, 1024], BF16)
                wo, b_wo = sb(ph, "wo", [128, 8, 1024], BF16)
                hxg = [sb(ph, "hxg%d" % i, [128, 8, 512], BF16) for i in range(2)]
                ytg = [sb(ph, "ytg%d" % i, [128, 4, 512], BF16) for i in range(2)]
                yag = [sb(ph, "yag%d" % i, [128, 4, 512], BF16) for i in range(2)]
                mg = [sb(ph, "mg%d" % i, [128, 8, 512], BF16) for i in range(2)]
                xo = [sb(ph, "xo%d" % i, [128, 1024], F32) for i in range(8)]
                sgf = [sb(ph, "sgf%d" % i, [128, 512], F32) for i in range(2)]
                sga = [sb(ph, "sga%d" % i, [128, 512], F32) for i in range(2)]
                tf = [sb(ph, "tf%d" % i, [128, 512], F32) for i in range(2)]
                ta = [sb(ph, "ta%d" % i, [128, 512], F32) for i in range(2)]
                tm = [sb(ph, "tm%d" % i, [128, 512], F32) for i in range(2)]
                pA = [ps(ph, "pA%d" % i, [128, 512]) for i in range(2)]
                pB = [ps(ph, "pB%d" % i, [128, 512]) for i in range(2)]
                pC, b_pC = ps(ph, "pC", [128, 512])
                pD, b_pD = ps(ph, "pD", [128, 512])
                pE = [ps(ph, "pE%d" % i, [128, 512]) for i in range(2)]
                b_wgc = [Buf() for _ in range(8)]
                b_wfoc = [Buf() for _ in range(4)]
                b_waoc = [Buf() for _ in range(4)]
                wgv = w_gate.rearrange("(k p) n -> p k n", p=128)
                wfov = w_fo.rearrange("(k p) n -> p k n", p=128)
                waov = w_ao.rearrange("(k p) n -> p k n", p=128)
                for i4 in range(4):
                    for blk in (i4, 4 + i4):
                        S.dma("gpsimd", wg[:, :, blk * 256:(blk + 1) * 256], wgv[:, :, blk * 256:(blk + 1) * 256],
                              writes=[b_wgc[blk]])
                    S.dma("gpsimd", wfo[:, :, i4 * 256:(i4 + 1) * 256], wfov[:, :, i4 * 256:(i4 + 1) * 256],
                          writes=[b_wfoc[i4]])
                    S.dma("gpsimd", wao[:, :, i4 * 256:(i4 + 1) * 256], waov[:, :, i4 * 256:(i4 + 1) * 256],
                          writes=[b_waoc[i4]])
                for kc in range(0, 8, 4):
                    S.dma("gpsimd", wo[:, kc:kc + 4, :], w_out.rearrange("(k p) n -> p k n", p=128)[:, kc:kc + 4, :],
                          writes=[b_wo])
                hxv = hx_scr.rearrange("k p n -> p k n")
                ytv = yt_scr.rearrange("c p n -> p c n")
                yav = ya_scr.rearrange("c p n -> p c n")

                def loads(G):
                    i = G % 2
                    sl = slice(G * 512, (G + 1) * 512)
                    S.dma("sync", hxg[i][0][:], hxv[:, :, sl], writes=[hxg[i][1]])
                    S.dma("sync", ytg[i][0][:], ytv[:, :, sl], writes=[ytg[i][1]])
                    S.dma("sync", yag[i][0][:], yav[:, :, sl], writes=[yag[i][1]])
                    for tt in range(4):
                        x_o, b_xo = xo[(G * 4 + tt) % 8]
                        tok = G * 512 + tt * 128
                        S.dma("sync", x_o[:], xl[256 + tok:256 + tok + 128, :], writes=[b_xo])

                loads(0)
                for G in range(8):
                    i = G % 2
                    if G + 1 < 8:
                        loads(G + 1)
                    hx_t, b_hx = hxg[i]
                    yt_t, b_yt = ytg[i]
                    ya_t, b_ya = yag[i]
                    m_t, b_m = mg[i]
                    for j in range(8):
                        k = j % 2
                        p_a, b_pa = pA[k]
                        p_b, b_pb = pB[k]
                        for kc in range(8):
                            S.pe(lambda e, kc=kc, j=j, p_a=p_a, hx_t=hx_t: e.matmul(
                                p_a[:, :], lhsT=wg[:, kc, j * 128:(j + 1) * 128], rhs=hx_t[:, kc, :],
                                start=(kc == 0), stop=(kc == 7)), reads=[b_wgc[j // 2], b_hx], writes=[b_pa])
                        for kc in range(8):
                            S.pe(lambda e, kc=kc, j=j, p_b=p_b, hx_t=hx_t: e.matmul(
                                p_b[:, :], lhsT=wg[:, kc, 1024 + j * 128:1024 + (j + 1) * 128], rhs=hx_t[:, kc, :],
                                start=(kc == 0), stop=(kc == 7)), reads=[b_wgc[4 + j // 2], b_hx], writes=[b_pb])
                        for kc in range(4):
                            S.pe(lambda e, kc=kc, j=j, yt_t=yt_t: e.matmul(
                                pC[:, :], lhsT=wfo[:, kc, j * 128:(j + 1) * 128], rhs=yt_t[:, kc, :],
                                start=(kc == 0), stop=(kc == 3)), reads=[b_wfoc[j // 2], b_yt], writes=[b_pC])
                        for kc in range(4):
                            S.pe(lambda e, kc=kc, j=j, ya_t=ya_t: e.matmul(
                                pD[:, :], lhsT=wao[:, kc, j * 128:(j + 1) * 128], rhs=ya_t[:, kc, :],
                                start=(kc == 0), stop=(kc == 3)), reads=[b_waoc[j // 2], b_ya], writes=[b_pD])
                        s_f, b_sf = sgf[k]
                        s_a, b_sa = sga[k]
                        t_f, b_tf = tf[k]
                        t_a, b_ta = ta[k]
                        S.act(lambda e, s_f=s_f, p_a=p_a: e.activation(out=s_f[:, :], in_=p_a[:, :], func=AF.Sigmoid),
                              reads=[b_pa], writes=[b_sf])
                        S.act(lambda e, s_a=s_a, p_b=p_b: e.activation(out=s_a[:, :], in_=p_b[:, :], func=AF.Sigmoid),
                              reads=[b_pb], writes=[b_sa])
                        S.dve(lambda e, t_f=t_f, s_f=s_f: e.tensor_tensor(out=t_f[:, :], in0=pC[:, :], in1=s_f[:, :],
                                                                         op=ALU.mult),
                              reads=[b_pC, b_sf], writes=[b_tf])
                        S.dve(lambda e, t_a=t_a, s_a=s_a: e.tensor_tensor(out=t_a[:, :], in0=pD[:, :], in1=s_a[:, :],
                                                                         op=ALU.mult),
                              reads=[b_pD, b_sa], writes=[b_ta])
                        S.dve(lambda e, j=j, m_t=m_t, t_f=t_f, t_a=t_a: e.tensor_tensor(
                            out=m_t[:, j, :], in0=t_f[:, :], in1=t_a[:, :], op=ALU.add),
                            reads=[b_tf, b_ta], writes=[b_m])
                    for tt in range(4):
                        x_o, b_xo = xo[(G * 4 + tt) % 8]
                        tok = G * 512 + tt * 128
                        for hh in range(2):
                            p_e, b_pe = pE[hh]
                            t_m, b_tm = tm[hh]
                            for kc in range(8):
                                S.pe(lambda e, kc=kc, hh=hh, tt=tt, p_e=p_e, m_t=m_t: e.matmul(
                                    p_e[:, :], lhsT=m_t[:, kc, tt * 128:(tt + 1) * 128],
                                    rhs=wo[:, kc, hh * 512:(hh + 1) * 512], start=(kc == 0), stop=(kc == 7)),
                                    reads=[b_m, b_wo], writes=[b_pe])
                            S.dve(lambda e, hh=hh, p_e=p_e, t_m=t_m: e.tensor_tensor(
                                out=t_m[:, :], in0=p_e[:, :], in1=gb[:, 0, hh * 512:(hh + 1) * 512], op=ALU.mult),
                                reads=[b_pe, b_gb], writes=[b_tm])
                            S.dve(lambda e, hh=hh, t_m=t_m, x_o=x_o: e.tensor_tensor(
                                out=x_o[:, hh * 512:(hh + 1) * 512], in0=t_m[:, :],
                                in1=x_o[:, hh * 512:(hh + 1) * 512], op=ALU.add), reads=[b_tm, b_xo], writes=[b_xo])
                        S.dma("sync", x1_scr[tok:tok + 128, :], x_o[:], reads=[b_xo])
                S.flush()

        if "D" in phases:
            with ExitStack() as ph:
                w1, b_w1 = sb(ph, "w1", [128, 8, 4096], BF16)
                w2, b_w2 = sb(ph, "w2", [128, 32, 1024], BF16)
                xo = [sb(ph, "xoD%d" % i, [128, 1024], F32) for i in range(4)]
                hx2 = [sb(ph, "hx2%d" % i, [128, 8, 256], BF16) for i in range(2)]
                aT, b_aT = sb(ph, "aT", [128, 32, 256], BF16)
                rt = [sb(ph, "rt%d" % i, [128, 256], F32) for i in range(2)]
                tm = [sb(ph, "tmD%d" % i, [128, 512], F32) for i in range(2)]
                W = norm_work(ph, "D", 4)
                pT = [ps(ph, "pTD%d" % i, [128, 1024], BF16) for i in range(2)]
                pH = [ps(ph, "pH%d" % i, [128, 512]) for i in range(2)]
                pO = [ps(ph, "pO%d" % i, [128, 512]) for i in range(4)]
                b_w1c = [Buf() for _ in range(8)]
                b_w2c = [Buf() for _ in range(8)]
                w1v = w_mlp1.rearrange("(k p) n -> p k n", p=128)
                w2v = w_mlp2.rearrange("(k p) n -> p k n", p=128)
                for cb in range(8):
                    S.dma("gpsimd", w1[:, :, cb * 512:(cb + 1) * 512], w1v[:, :, cb * 512:(cb + 1) * 512],
                          writes=[b_w1c[cb]])
                for cb in range(8):
                    S.dma("gpsimd", w2[:, cb * 4:cb * 4 + 4, :], w2v[:, cb * 4:cb * 4 + 4, :], writes=[b_w2c[cb]])

                def xb(Gm, tt):
                    return xo[(Gm * 2 + tt) % 4]

                def wk(Gm, tt):
                    return W[(Gm * 2 + tt) % 4]

                def n_front(Gm):
                    for tt in range(2):
                        x_o, b_xo = xb(Gm, tt)
                        tok = Gm * 256 + tt * 128
                        S.dma("sync", x_o[:], x1_scr[tok:tok + 128, :], writes=[b_xo])
                    for tt in range(2):
                        nt_stats(xb(Gm, tt)[0], xb(Gm, tt)[1], wk(Gm, tt))
                    for tt in range(2):
                        nt_recip(wk(Gm, tt))
                    for tt in range(2):
                        nt_scale(xb(Gm, tt)[0], xb(Gm, tt)[1], wk(Gm, tt))

                def n_back(Gm):
                    h_t, b_h = hx2[Gm % 2]
                    for tt in range(2):
                        nt_tr(wk(Gm, tt), pT[tt][0], pT[tt][1])
                    for tt in range(2):
                        nt_mod(pT[tt][0], pT[tt][1], lambda j, tt=tt: h_t[:, j, tt * 128:(tt + 1) * 128], b_h, 16, 48)

                n_front(0)
                n_back(0)
                for Gm in range(16):
                    h_t, b_h = hx2[Gm % 2]
                    if Gm + 1 < 16:
                        n_front(Gm + 1)
                    for c in range(32):
                        p_h, b_ph = pH[c % 2]
                        r_t, b_rt = rt[c % 2]
                        for kc in range(8):
                            S.pe(lambda e, kc=kc, c=c, p_h=p_h, h_t=h_t: e.matmul(
                                p_h[:, 0:256], lhsT=w1[:, kc, c * 128:(c + 1) * 128], rhs=h_t[:, kc, :],
                                start=(kc == 0), stop=(kc == 7)), reads=[b_w1c[c // 4], b_h], writes=[b_ph])
                        S.act(lambda e, p_h=p_h, r_t=r_t: e.activation(out=r_t[:, :], in_=p_h[:, 0:256], func=AF.Relu),
                              reads=[b_ph], writes=[b_rt])
                        S.dve(lambda e, c=c, r_t=r_t: e.tensor_tensor(out=aT[:, c, :], in0=r_t[:, :], in1=r_t[:, :],
                                                                    op=ALU.mult), reads=[b_rt], writes=[b_aT])
                    if Gm + 1 < 16:
                        n_back(Gm + 1)
                    for tt in range(2):
                        x_o, b_xo = xb(Gm, tt)
                        tok = Gm * 256 + tt * 128
                        for hh in range(2):
                            p_o, b_po = pO[tt * 2 + hh]
                            t_m, b_tm = tm[hh]
                            for c in range(32):
                                S.pe(lambda e, c=c, hh=hh, tt=tt, p_o=p_o: e.matmul(
                                    p_o[:, :], lhsT=aT[:, c, tt * 128:(tt + 1) * 128],
                                    rhs=w2[:, c, hh * 512:(hh + 1) * 512], start=(c == 0), stop=(c == 31)),
                                    reads=[b_aT, b_w2c[c // 4]], writes=[b_po])
                            S.dve(lambda e, hh=hh, p_o=p_o, t_m=t_m: e.tensor_tensor(
                                out=t_m[:, :], in0=p_o[:, :], in1=gb[:, 1, hh * 512:(hh + 1) * 512], op=ALU.mult),
                                reads=[b_po, b_gb], writes=[b_tm])
                            S.dve(lambda e, hh=hh, t_m=t_m, x_o=x_o: e.tensor_tensor(
                                out=x_o[:, hh * 512:(hh + 1) * 512], in0=t_m[:, :],
                                in1=x_o[:, hh * 512:(hh + 1) * 512], op=ALU.add), reads=[b_tm, b_xo], writes=[b_xo])
                        S.dma("sync", out[tok:tok + 128, :], x_o[:], reads=[b_xo])
                S.flush()
        S.flush()
    return nc


def _bf(a):
    return np.asarray(a, np.float32).astype(ml_dtypes.bfloat16)


def _consts():
    n = np.arange(128)
    ang = 2 * np.pi * ((n[:, None] * n[None, :]) % 128) / 128.0
    C, Sn = np.cos(ang), np.sin(ang)
    cs128 = np.concatenate([C, Sn], 1)
    ccsc = np.concatenate([C, Sn, -Sn, C], 1)
    ident = np.eye(128)
    return _bf(cs128), _bf(ccsc), _bf(ident)


def _t3(half):
    p = np.arange(128)
    n2, lo = p // 2, p % 2
    k1 = np.arange(128)
    k2 = np.arange(32) + 32 * half
    k = k1[None, :, None] + 128 * k2[None, None, :]
    ang = 2 * np.pi * ((n2[:, None, None] * k) % 8192) / 8192.0
    live = (lo[:, None, None] == (k1 % 2)[None, :, None])
    t = np.stack([np.cos(ang) * live, -np.sin(ang) * live], 1) / 1024.0
    return _bf(t)


def _rope(half):
    inv = 10000.0 ** (-np.arange(0, 32, 2, dtype=np.float64) / 32.0)
    tab = np.zeros((38, 128, 128), np.float32)
    tab[0:2, :, 0:64] = 1.0
    p = np.arange(128)
    col = (p % 64).astype(np.float64)
    for lt in range(36):
        lr = 2 * lt + p // 64
        row = np.clip(64 * half - 4 + lr, 0, 127).astype(np.float64)
        ar = row[:, None] * inv[None, :]
        ac = col[:, None] * inv[None, :]
        cos = np.concatenate([np.cos(ar), np.cos(ar), np.cos(ac), np.cos(ac)], 1)
        sin = np.concatenate([-np.sin(ar), np.sin(ar), -np.sin(ac), np.sin(ac)], 1)
        tab[2 + lt, :, 0:64] = cos
        tab[2 + lt, :, 64:128] = sin
    return tab


def _bias_tables(rpb):
    kc = np.arange(64)
    c = np.arange(64)
    cs = np.clip(c - 8, 0, 48)
    colv = (kc[:, None] >= cs[None, :]) & (kc[:, None] < cs[None, :] + 16)
    dc = np.clip(kc[:, None] - c[None, :] + 15, 0, 30)
    MT = np.full((2, 64, 8, 9, 64), NEG, np.float32)
    FT = np.full((2, 64, 8, 14, 64), NEG, np.float32)
    for par in range(2):
        for s in range(9):
            dr = (3 - s) + par
            if -4 <= dr <= 3:
                vals = rpb[:, dr + 7][:, dc]
                MT[par, :, :, s, :] = np.where(colv[:, None, :], vals.transpose(1, 0, 2), NEG)
        for s in range(14):
            dr = (6 - s) + par
            if -7 <= dr <= 7:
                vals = rpb[:, dr + 7][:, dc]
                FT[par, :, :, s, :] = np.where(colv[:, None, :], vals.transpose(1, 0, 2), NEG)
    return _bf(MT.reshape(128, 8, 576)), _bf(FT.reshape(128, 8, 896))


def _qmask(half):
    q = np.zeros((4, 512), np.float32)
    if half == 0:
        q[0, 0:256] = NEG
        q[3, :] = NEG
    else:
        q[1, :] = NEG
        q[2, 256:512] = NEG
    return _bf(q.reshape(1, 2048))


def _pj(v, nj):
    return np.ascontiguousarray(np.asarray(v, np.float32).reshape(nj, 128).T)


def make_in_maps(x, c, ctx, c_ctx, w_ada, b_ada, norm1_g, norm2_g, w_in, q_norm_g, k_norm_g, rpb,
                 w_branch_gate, w_fourier_out, w_attn_out, w_out, w_mlp1, w_mlp2):
    f = lambda a: np.ascontiguousarray(np.asarray(a, np.float32))
    x, c, ctx, c_ctx = f(x), f(c), f(ctx), f(c_ctx)
    cs128, ccsc, ident = _consts()
    MT, FT = _bias_tables(f(rpb)[0])
    b = f(b_ada)[0]
    bT = _pj(b, 48)
    bgt = np.ascontiguousarray(np.broadcast_to(np.stack([b[2048:3072], b[5120:6144]], 0)[None], (128, 2, 1024)))
    ngt = np.concatenate([_pj(f(norm1_g)[0], 8), _pj(f(norm2_g)[0], 8)], 1)
    gq = np.tile(f(q_norm_g)[0], 8)
    gk = np.tile(f(k_norm_g)[0], 8)
    gqk = np.ascontiguousarray(np.broadcast_to(np.stack([gq, gk], 0)[None], (128, 2, 512)))
    shared = dict(w_ada=f(w_ada)[0], bT=bT, bg=bgt, ng=ngt, gqk=gqk, w_in=f(w_in)[0], w_gate=f(w_branch_gate)[0],
                  w_fo=f(w_fourier_out)[0], w_ao=f(w_attn_out)[0], w_out=f(w_out)[0], w_mlp1=f(w_mlp1)[0],
                  w_mlp2=f(w_mlp2)[0], MT=MT, FT=FT, ident=ident, cs128=cs128, ccsc=ccsc)
    per_half = [dict(rope=_rope(h), qmask=_qmask(h), t3=_t3(h)) for h in range(2)]
    maps = []
    for core in range(8):
        bi, half = core // 2, core % 2
        rows = np.clip(64 * half - 4 + np.arange(72), 0, 127)
        xl = np.ascontiguousarray(x[bi].reshape(128, 64, 1024)[rows].reshape(4608, 1024))
        ccl = np.ascontiguousarray(np.stack([_pj(c[bi], 8), _pj(c_ctx, 8)], 2))
        m = dict(shared)
        m.update(per_half[half])
        m.update(xf=x[bi], xl=xl, ctx=ctx[bi], cc=ccl)
        maps.append(m)
    return maps


_NC = {}


def kernel(**inputs):
    if "nc" not in _NC:
        _NC["nc"] = build()
    maps = make_in_maps(**inputs)
    res = run_bass_kernel_spmd(_NC["nc"], maps, core_ids=list(range(8)))
    outp = np.empty((4, 8192, 1024), np.float32)
    for core in range(8):
        bi, half = core // 2, core % 2
        outp[bi, half * 4096:(half + 1) * 4096] = res.results[core]["out"]
    return outp
```
